# Optimizing a Trainium2 kernel written in Bass

```python
import math
import jax, jax.numpy as jnp
from jax import lax
import numpy as np


D_MODEL = 1024
BATCH = 16
SEQ = 2048
DEPTH = 2

N_META = 16
MLA_HEADS = 8
MLA_NOPE = 64
MLA_ROPE = 32
MLA_V = 64
MLA_Q_RANK = 256
MLA_KV_RANK = 256
ROPE_THETA = 10000.0
Q_BLOCK = 128
MASK_VALUE = -1e9
S5_WIDTH = 512
S5_GROUP = 16
S5_GROUPS = S5_WIDTH // S5_GROUP
S5_STATE = 64
S5_DT_MIN = 1e-3
S5_DT_MAX = 1e-1
HG_HEADS = 4
HG_KEY = 128
HG_VAL = 128
HG_CHUNK = 64
HG_QK = HG_HEADS * HG_KEY
HG_VW = HG_HEADS * HG_VAL
HG_F_MIN = 1e-6
N_BRANCH = 3
D_FF = -(-(8 * D_MODEL) // (3 * 256)) * 256
ALPHA = (2 * DEPTH) ** 0.25
BETA = (8 * DEPTH) ** -0.25
MLA_IN = MLA_Q_RANK + MLA_KV_RANK + MLA_ROPE
S5_IN = S5_WIDTH
HG_IN = 2 * HG_QK + 2 * HG_VW
GATE_IN = N_BRANCH * D_MODEL
D_IN = MLA_IN + S5_IN + HG_IN + GATE_IN
SPLIT_IN = [MLA_IN, MLA_IN + S5_IN, MLA_IN + S5_IN + HG_IN]

kernel_name = 'hybrid_mla_s5_hgrn2_deepnorm_meta'


def layer_norm(x, g, b, eps=1e-5):
    xf = x.astype(jnp.float32)
    mu = jnp.mean(xf, axis=-1, keepdims=True)
    var = jnp.mean(jnp.square(xf - mu), axis=-1, keepdims=True)
    return ((xf - mu) * lax.rsqrt(var + eps) * g.astype(jnp.float32) + b.astype(jnp.float32)).astype(x.dtype)


def rms_norm(x, g, eps=1e-6):
    xf = x.astype(jnp.float32)
    return (xf * lax.rsqrt(jnp.mean(jnp.square(xf), axis=-1, keepdims=True) + eps) * g.astype(jnp.float32)).astype(x.dtype)


def rope_tables(pos):
    inv = ROPE_THETA ** (-(jnp.arange(0, MLA_ROPE, 2, dtype=jnp.float32) / MLA_ROPE))
    ang = pos.astype(jnp.float32)[..., None] * inv
    return jnp.cos(ang), jnp.sin(ang)


def apply_rope(x, cos, sin):
    x1, x2 = jnp.split(x.astype(jnp.float32), 2, axis=-1)
    return jnp.concatenate([x1 * cos - x2 * sin, x1 * sin + x2 * cos], axis=-1).astype(x.dtype)


def mla_mixer(z, cos, sin, q_norm, w_uq, kv_norm, w_ukv):
    B_, L_, _ = z.shape
    c_q, c_kv, k_r = jnp.split(z, [MLA_Q_RANK, MLA_Q_RANK + MLA_KV_RANK], axis=-1)
    q = (rms_norm(c_q, q_norm) @ w_uq).reshape(B_, L_, MLA_HEADS, MLA_NOPE + MLA_ROPE)
    q_nope, q_rope = jnp.split(q, [MLA_NOPE], axis=-1)
    q_rope = apply_rope(q_rope, cos[:, :, None, :], sin[:, :, None, :])
    kv = (rms_norm(c_kv, kv_norm) @ w_ukv).reshape(B_, L_, MLA_HEADS, MLA_NOPE + MLA_V)
    k_nope, v = jnp.split(kv, [MLA_NOPE], axis=-1)
    k_rope = apply_rope(k_r, cos, sin)
    scale = (MLA_NOPE + MLA_ROPE) ** -0.5
    key_idx = jnp.arange(L_)

    def attend(qn, qr, q_start):
        s = (jnp.einsum('bqhd,bkhd->bhqk', qn, k_nope) + jnp.einsum('bqhr,bkr->bhqk', qr, k_rope)).astype(jnp.float32) * scale
        q_idx = q_start + jnp.arange(qn.shape[1])
        s = jnp.where(key_idx[None, :] <= q_idx[:, None], s, MASK_VALUE)
        p = jax.nn.softmax(s, axis=-1).astype(v.dtype)
        return jnp.einsum('bhqk,bkhd->bqhd', p, v)

    o_meta = attend(q_nope[:, :N_META], q_rope[:, :N_META], 0).reshape(B_, N_META, MLA_HEADS * MLA_V)
    n_blk = (L_ - N_META) // Q_BLOCK

    def blocks(t):
        return t[:, N_META:].reshape((B_, n_blk, Q_BLOCK) + t.shape[2:]).swapaxes(0, 1)

    starts = N_META + Q_BLOCK * jnp.arange(n_blk, dtype=jnp.int32)
    o_real = lax.map(lambda a: attend(a[0], a[1], a[2]), (blocks(q_nope), blocks(q_rope), starts))
    o_real = o_real.swapaxes(0, 1).reshape(B_, L_ - N_META, MLA_HEADS * MLA_V)
    return jnp.concatenate([o_meta, o_real], axis=1)


def s5_mixer(u, lam_re, lam_im, log_dt, b_re, b_im, c_re, c_im, d_skip, w_glu):
    f32 = jnp.float32
    B_, L_, _ = u.shape
    lr = jnp.minimum(lam_re.astype(f32), -1e-4)
    li = lam_im.astype(f32)
    dt = jnp.exp(log_dt.astype(f32))[:, None]
    mag = jnp.exp(lr * dt)
    ab_r = mag * jnp.cos(li * dt)
    ab_i = mag * jnp.sin(li * dt)
    den = lr * lr + li * li
    nr = ab_r - 1.0
    coef_r = ((nr * lr + ab_i * li) / den)[..., None]
    coef_i = ((ab_i * lr - nr * li) / den)[..., None]
    bb_r = coef_r * b_re.astype(f32) - coef_i * b_im.astype(f32)
    bb_i = coef_r * b_im.astype(f32) + coef_i * b_re.astype(f32)
    uf = u.astype(f32)
    ug = uf.reshape(B_, L_, S5_GROUPS, S5_GROUP)
    bu_r = jnp.einsum('blgc,gnc->blgn', ug, bb_r)
    bu_i = jnp.einsum('blgc,gnc->blgn', ug, bb_i)
    a_r = jnp.broadcast_to(ab_r[None, None], (1, L_, S5_GROUPS, S5_STATE))
    a_i = jnp.broadcast_to(ab_i[None, None], (1, L_, S5_GROUPS, S5_STATE))

    def combine(e1, e2):
        a1r, a1i, b1r, b1i = e1
        a2r, a2i, b2r, b2i = e2
        return (a2r * a1r - a2i * a1i, a2r * a1i + a2i * a1r,
                a2r * b1r - a2i * b1i + b2r, a2r * b1i + a2i * b1r + b2i)

    _, _, x_r, x_i = lax.associative_scan(combine, (a_r, a_i, bu_r, bu_i), axis=1)
    y = jnp.einsum('blgn,gcn->blgc', x_r, c_re.astype(f32)) - jnp.einsum('blgn,gcn->blgc', x_i, c_im.astype(f32))
    y = y.reshape(B_, L_, S5_WIDTH) + d_skip.astype(f32) * uf
    y = jax.nn.gelu(y)
    y = y * jax.nn.sigmoid(y @ w_glu.astype(f32))
    return y.astype(u.dtype)


def hgrn2_chunk(state, q, k, v, log_f):
    cum = jnp.cumsum(log_f, axis=2)
    n = q.shape[2]
    causal = jnp.tril(jnp.ones((n, n), dtype=bool))[None, None, :, :, None]
    rel = cum[:, :, :, None, :] - cum[:, :, None, :, :]
    decay = jnp.where(causal, jnp.exp(jnp.minimum(rel, 0.0)), 0.0)
    scores = jnp.einsum('bhtk,bhsk,bhtsk->bhts', q, k, decay)
    out = jnp.einsum('bhts,bhsv->bhtv', scores, v) + jnp.einsum('bhtk,bhkv->bhtv', q * jnp.exp(cum), state)
    last = cum[:, :, -1:, :]
    new_state = jnp.exp(last[:, :, 0, :, None]) * state + jnp.einsum('bhsk,bhsv->bhkv', k * jnp.exp(last - cum), v)
    return new_state, out


def hgrn2_mixer(z, lb, out_norm):
    f32 = jnp.float32
    B_, L_, _ = z.shape
    q, zf, v, g = jnp.split(z.astype(f32), [HG_QK, 2 * HG_QK, 2 * HG_QK + HG_VW], axis=-1)
    lb = lb.astype(f32)
    f = lb + (1.0 - lb) * jax.nn.sigmoid(zf)
    log_f = jnp.log(jnp.maximum(f, HG_F_MIN))
    k = (1.0 - lb) * jax.nn.sigmoid(-zf)

    def heads(t, d):
        return t.reshape(B_, L_, HG_HEADS, d).transpose(0, 2, 1, 3)

    q, k, log_f, v = heads(q, HG_KEY), heads(k, HG_KEY), heads(log_f, HG_KEY), heads(v, HG_VAL)
    s0 = jnp.zeros((B_, HG_HEADS, HG_KEY, HG_VAL), f32)
    s_meta, o_meta = hgrn2_chunk(s0, q[:, :, :N_META], k[:, :, :N_META], v[:, :, :N_META], log_f[:, :, :N_META])
    n_chunks = (L_ - N_META) // HG_CHUNK

    def to_chunks(t):
        return t[:, :, N_META:].reshape(B_, HG_HEADS, n_chunks, HG_CHUNK, t.shape[-1]).transpose(2, 0, 1, 3, 4)

    def step(s, xs):
        return hgrn2_chunk(s, xs[0], xs[1], xs[2], xs[3])

    _, o_real = lax.scan(step, s_meta, (to_chunks(q), to_chunks(k), to_chunks(v), to_chunks(log_f)))
    o_real = o_real.transpose(1, 2, 0, 3, 4).reshape(B_, HG_HEADS, L_ - N_META, HG_VAL)
    o = jnp.concatenate([o_meta, o_real], axis=2).transpose(0, 2, 1, 3)
    o = rms_norm(o, out_norm.reshape(HG_HEADS, HG_VAL))
    return (o.reshape(B_, L_, HG_VW) * jax.nn.silu(g)).astype(z.dtype)


def setup_inputs(seed: int = 0) -> dict:
    key = jax.random.key(seed)
    ks = iter(jax.random.split(key, 40))
    f32 = jnp.float32

    def nrm(shape, scale):
        return scale * jax.random.normal(next(ks), shape, f32)

    def gain(shape):
        return 1.0 + nrm(shape, 0.02)

    L = DEPTH
    x = nrm((BATCH, SEQ, D_MODEL), 1.0)
    positions = jnp.broadcast_to(jnp.arange(SEQ, dtype=jnp.int32)[None], (BATCH, SEQ))
    meta_tokens = nrm((N_META, D_MODEL), 1.0)
    ln_in_g = gain((D_MODEL,))
    ln_in_b = nrm((D_MODEL,), 0.02)
    w_in = nrm((L, D_MODEL, D_IN), D_MODEL ** -0.5)
    mla_q_norm = gain((L, MLA_Q_RANK))
    mla_w_uq = nrm((L, MLA_Q_RANK, MLA_HEADS * (MLA_NOPE + MLA_ROPE)), MLA_Q_RANK ** -0.5)
    mla_kv_norm = gain((L, MLA_KV_RANK))
    mla_w_ukv = nrm((L, MLA_KV_RANK, MLA_HEADS * (MLA_NOPE + MLA_V)), MLA_KV_RANK ** -0.5)
    s5_lam_re = -0.5 + nrm((L, S5_GROUPS, S5_STATE), 0.01)
    s5_lam_im = jnp.pi * jnp.arange(S5_STATE, dtype=f32)[None, None, :] + nrm((L, S5_GROUPS, S5_STATE), 0.01)
    s5_log_dt = jax.random.uniform(next(ks), (L, S5_GROUPS), f32, math.log(S5_DT_MIN), math.log(S5_DT_MAX))
    s5_b_re = nrm((L, S5_GROUPS, S5_STATE, S5_GROUP), (2 * S5_GROUP) ** -0.5)
    s5_b_im = nrm((L, S5_GROUPS, S5_STATE, S5_GROUP), (2 * S5_GROUP) ** -0.5)
    s5_c_re = nrm((L, S5_GROUPS, S5_GROUP, S5_STATE), S5_STATE ** -0.5)
    s5_c_im = nrm((L, S5_GROUPS, S5_GROUP, S5_STATE), S5_STATE ** -0.5)
    s5_d = nrm((L, S5_WIDTH), 1.0)
    s5_w_glu = nrm((L, S5_WIDTH, S5_WIDTH), S5_WIDTH ** -0.5)
    hg_lb_logits = nrm((L, HG_QK), 0.1)
    hg_out_norm = gain((L, HG_VW))
    w_br_mla = nrm((L, MLA_HEADS * MLA_V, D_MODEL), BETA * (MLA_HEADS * MLA_V) ** -0.5)
    w_br_s5 = nrm((L, S5_WIDTH, D_MODEL), BETA * S5_WIDTH ** -0.5)
    w_br_hg = nrm((L, HG_VW, D_MODEL), BETA * HG_VW ** -0.5)
    w_out = nrm((L, D_MODEL, D_MODEL), BETA * D_MODEL ** -0.5)
    ln1_g = gain((L, D_MODEL))
    ln1_b = nrm((L, D_MODEL), 0.02)
    w_ffn_gate = nrm((L, D_MODEL, D_FF), D_MODEL ** -0.5)
    w_ffn_up = nrm((L, D_MODEL, D_FF), D_MODEL ** -0.5)
    w_ffn_down = nrm((L, D_FF, D_MODEL), BETA * D_FF ** -0.5)
    ln2_g = gain((L, D_MODEL))
    ln2_b = nrm((L, D_MODEL), 0.02)
    return {'x': x, 'positions': positions, 'meta_tokens': meta_tokens,
            'ln_in_g': ln_in_g, 'ln_in_b': ln_in_b, 'w_in': w_in,
            'mla_q_norm': mla_q_norm, 'mla_w_uq': mla_w_uq, 'mla_kv_norm': mla_kv_norm, 'mla_w_ukv': mla_w_ukv,
            's5_lam_re': s5_lam_re, 's5_lam_im': s5_lam_im, 's5_log_dt': s5_log_dt,
            's5_b_re': s5_b_re, 's5_b_im': s5_b_im, 's5_c_re': s5_c_re, 's5_c_im': s5_c_im,
            's5_d': s5_d, 's5_w_glu': s5_w_glu,
            'hg_lb_logits': hg_lb_logits, 'hg_out_norm': hg_out_norm,
            'w_br_mla': w_br_mla, 'w_br_s5': w_br_s5, 'w_br_hg': w_br_hg, 'w_out': w_out,
            'ln1_g': ln1_g, 'ln1_b': ln1_b,
            'w_ffn_gate': w_ffn_gate, 'w_ffn_up': w_ffn_up, 'w_ffn_down': w_ffn_down,
            'ln2_g': ln2_g, 'ln2_b': ln2_b}


def reference(x, positions, meta_tokens, ln_in_g, ln_in_b, w_in,
              mla_q_norm, mla_w_uq, mla_kv_norm, mla_w_ukv,
              s5_lam_re, s5_lam_im, s5_log_dt, s5_b_re, s5_b_im, s5_c_re, s5_c_im, s5_d, s5_w_glu,
              hg_lb_logits, hg_out_norm,
              w_br_mla, w_br_s5, w_br_hg, w_out, ln1_g, ln1_b,
              w_ffn_gate, w_ffn_up, w_ffn_down, ln2_g, ln2_b):
    B_ = x.shape[0]
    meta = jnp.broadcast_to(meta_tokens.astype(x.dtype)[None], (B_, N_META, D_MODEL))
    h = layer_norm(jnp.concatenate([meta, x], axis=1), ln_in_g, ln_in_b)
    meta_pos = jnp.broadcast_to(jnp.arange(N_META, dtype=jnp.int32)[None], (B_, N_META))
    pos = jnp.concatenate([meta_pos, positions.astype(jnp.int32) + N_META], axis=1)
    cos, sin = rope_tables(pos)
    p_lb = jax.nn.softmax(hg_lb_logits.astype(jnp.float32), axis=0)
    lower_bounds = jnp.cumsum(p_lb, axis=0) - p_lb[0]
    for l in range(DEPTH):
        z = h @ w_in[l]
        z_mla, z_s5, z_hg, z_gate = jnp.split(z, SPLIT_IN, axis=-1)
        y_mla = mla_mixer(z_mla, cos, sin, mla_q_norm[l], mla_w_uq[l], mla_kv_norm[l], mla_w_ukv[l]) @ w_br_mla[l]
        y_s5 = s5_mixer(z_s5, s5_lam_re[l], s5_lam_im[l], s5_log_dt[l], s5_b_re[l], s5_b_im[l],
                        s5_c_re[l], s5_c_im[l], s5_d[l], s5_w_glu[l]) @ w_br_s5[l]
        y_hg = hgrn2_mixer(z_hg, lower_bounds[l], hg_out_norm[l]) @ w_br_hg[l]
        g_mla, g_s5, g_hg = jnp.split(jax.nn.sigmoid(z_gate), N_BRANCH, axis=-1)
        mixed = (g_mla * y_mla + g_s5 * y_s5 + g_hg * y_hg) @ w_out[l]
        h = layer_norm(ALPHA * h + mixed, ln1_g[l], ln1_b[l])
        ffn = (jax.nn.silu(h @ w_ffn_gate[l]) * (h @ w_ffn_up[l])) @ w_ffn_down[l]
        h = layer_norm(ALPHA * h + ffn, ln2_g[l], ln2_b[l])
    return h[:, N_META:]
```

```python
import contextlib
import math
import numpy as np
import concourse.bass as bass
import concourse.mybir as mybir
from concourse.bass_utils import run_bass_kernel_spmd

F32 = mybir.dt.float32
BF16 = mybir.dt.bfloat16
I32 = mybir.dt.int32
AF = mybir.ActivationFunctionType
ALU = mybir.AluOpType

L = 2064
NMETA = 16
NT = [(0, 400), (400, 912), (912, 1424), (1424, 1936), (1936, 2064)]
TOKT = [(0, 16)] + [(16 + 128 * i, 128) for i in range(16)]
HCH = [(0, 16)] + [(16 + 64 * i, 64) for i in range(32)]
TC = 344
S5CH = [(TC * i, TC * (i + 1)) for i in range(6)]
ALPHA = 4 ** 0.25
MAGIC = 12582912.0
TWO_PI = 2.0 * math.pi
C1 = 6.28125
C2 = float(np.float32(TWO_PI - 6.28125))
C3 = float(TWO_PI - 6.28125 - float(np.float32(TWO_PI - 6.28125)))
GELU_K = 2.0 * math.sqrt(2.0 / math.pi)


INAMES = {}


class Tile:
    __slots__ = ("name", "w", "r")

    def __init__(self, name=""):
        self.name = name
        self.w = None
        self.r = {}


class Eng:
    def __init__(self, name, handle, sem):
        self.name = name
        self.h = handle
        self.sem = sem
        self.count = 0
        self.q = []
        self.waited = {}


class FW:
    NSLOT = 16

    def __init__(self, nc, stack):
        self.nc = nc
        self.sems = {}
        self.engs = {}
        for name, h in (("pe", nc.tensor), ("act", nc.scalar), ("dve", nc.vector),
                        ("pool", nc.gpsimd), ("sp", nc.sync)):
            sem = stack.enter_context(nc.semaphore("s_" + name))
            self.sems["s_" + name] = sem
            self.engs[name] = Eng(name, h, sem)
        self.slots = {}
        for qn in ("sp", "pool"):
            lst = []
            for i in range(self.NSLOT):
                key = "d_%s_%d" % (qn, i)
                sem = stack.enter_context(nc.semaphore(key))
                self.sems[key] = sem
                lst.append([key, 0])
            self.slots[qn] = [lst, 0]

    def _deps(self, reads, writes, self_key=None):
        deps = {}
        for t in reads:
            if t.w is not None and deps.get(t.w[0], 0) < t.w[1]:
                deps[t.w[0]] = t.w[1]
        for t in writes:
            if t.w is not None and deps.get(t.w[0], 0) < t.w[1]:
                deps[t.w[0]] = t.w[1]
            for k, v in t.r.items():
                if k == self_key:
                    continue
                if deps.get(k, 0) < v:
                    deps[k] = v
        return deps

    def _emit_waits(self, eng, deps, skip_self=False):
        for k, v in deps.items():
            if skip_self and k == "s_" + eng.name:
                continue
            if eng.waited.get(k, 0) >= v:
                continue
            eng.waited[k] = v
            eng.q.append(lambda e=eng.h, s=self.sems[k], v=v: e.wait_ge(s, v))

    def _mark(self, tok, reads, writes):
        k, v = tok
        for t in reads:
            if t.r.get(k, 0) < v:
                t.r[k] = v
        for t in writes:
            t.w = tok
            t.r = {}

    def op(self, engname, method, args, kw, reads=(), writes=()):
        eng = self.engs[engname]
        deps = self._deps(reads, writes, self_key="s_" + engname)
        self._emit_waits(eng, deps, skip_self=(engname == "pe"))
        eng.count += 1
        import sys as _sys
        ln = _sys._getframe(2).f_lineno

        def _mk(e=eng.h, s=eng.sem, m=method, a=args, kw=kw, ln=ln):
            ins = getattr(e, m)(*a, **kw)
            try:
                INAMES[ins.ins.name] = (m, ln)
            except Exception:
                pass
            return ins.then_inc(s, 1)
        eng.q.append(_mk)
        self._mark(("s_" + engname, eng.count), reads, writes)

    def dma(self, qname, out, in_, reads=(), writes=()):
        eng = self.engs[qname]
        lst, idx = self.slots[qname]
        slot = lst[idx % self.NSLOT]
        self.slots[qname][1] = idx + 1
        deps = self._deps(reads, writes)
        if slot[1] > 0:
            deps[slot[0]] = max(deps.get(slot[0], 0), slot[1])
        self._emit_waits(eng, deps)
        slot[1] += 16
        eng.q.append(lambda e=eng.h, s=self.sems[slot[0]], o=out, i=in_:
                     e.dma_start(out=o, in_=i).then_inc(s, 16))
        self._mark((slot[0], slot[1]), reads, writes)

    def barrier(self):
        deps = {}
        for n, e in self.engs.items():
            if e.count:
                deps["s_" + n] = e.count
        for qn, (lst, idx) in self.slots.items():
            for key, v in lst:
                if v:
                    deps[key] = v
        for n in self.engs:
            self._emit_waits(self.engs[n], deps)

    def finish(self):
        self.barrier()
        nc = self.nc
        with nc.Block() as block:
            @block.tensor
            def _(e):
                for f in self.engs["pe"].q:
                    f()

            @block.scalar
            def _(e):
                for f in self.engs["act"].q:
                    f()

            @block.vector
            def _(e):
                for f in self.engs["dve"].q:
                    f()

            @block.gpsimd
            def _(e):
                for f in self.engs["pool"].q:
                    f()

            @block.sync
            def _(e):
                for f in self.engs["sp"].q:
                    f()


class Buf:
    def __init__(self, t, name):
        self.t = t
        self.name = name
        self.tiles = {}

    def T(self, key=0):
        if key not in self.tiles:
            self.tiles[key] = Tile("%s/%s" % (self.name, key))
        return self.tiles[key]

    def __getitem__(self, idx):
        return self.t[idx]


class Alloc:
    def __init__(self, nc, limit):
        self.nc = nc
        self.off = 0
        self.limit = limit
        self.n = 0
        self.peak = 0
        self.big = nc.alloc_sbuf_tensor("bigbuf", [128, limit // 2], BF16)

    def alloc(self, name, shape, dtype):
        nel = int(np.prod(shape[1:]))
        esz = 4 if dtype in (F32, I32) else 2
        nbytes = (nel * esz + 63) // 64 * 64
        self.n += 1
        assert self.off + nbytes <= self.limit, "SBUF overflow at %s: %d + %d" % (name, self.off, nbytes)
        ap = self.big[:, self.off // 2:(self.off + nbytes) // 2]
        if dtype != BF16:
            ap = ap.bitcast(dtype)
        ap = ap[:, 0:nel]
        if len(shape) == 3:
            ap = ap.rearrange("p (a b) -> p a b", b=shape[2])
        elif len(shape) == 4:
            ap = ap.rearrange("p (a b c) -> p a b c", b=shape[2], c=shape[3])
        if shape[0] < 128:
            ap = ap[0:shape[0]]
        self.off += nbytes
        self.peak = max(self.peak, self.off)
        return Buf(ap, name)

    def mark(self):
        return self.off

    def release(self, m):
        self.off = m


class Builder:
    def __init__(self, nseq, debug=None):
        self.nseq = nseq
        self.debug = debug or []
        self.dbg_outs = {}

    def declare(self, nc):
        d = {}

        def inp(name, shape, dt=F32):
            d[name] = nc.dram_tensor(name, list(shape), dt, kind="ExternalInput").ap()
        ns = self.nseq
        inp("x", [ns, 2048, 1024]); inp("positions", [ns, 2048], I32); inp("meta_tokens", [16, 1024])
        inp("ln_in_g", [128, 8]); inp("ln_in_b", [128, 8])
        inp("w_in", [2, 1024, 6176]); inp("w_kr", [2, 1024, 96]); inp("w_kr_sw", [2, 1024, 96])
        inp("q_norm", [2, 128, 2]); inp("kv_norm", [2, 128, 2])
        inp("w_uq", [2, 256, 768]); inp("w_uq_sw", [2, 256, 768]); inp("w_ukv", [2, 256, 1024])
        for n in ("lamre_s", "lamim_s", "logdt_s"):
            inp(n, [2, 128, 16])
        for n in ("lamre_c", "lamim_c", "logdt_c"):
            inp(n, [2, 128, 256])
        inp("bre_c", [2, 128, 256]); inp("bim_c", [2, 128, 256])
        inp("cre_pad", [2, 128, 2048]); inp("cim_pad", [2, 128, 2048])
        inp("s5_d", [2, 128, 4]); inp("w_glu", [2, 512, 512])
        inp("lb_logits", [2, 128, 4]); inp("hg_norm", [2, 128, 4])
        inp("w_br_mla", [2, 512, 1024]); inp("w_br_s5", [2, 512, 1024]); inp("w_br_hg", [2, 512, 1024])
        inp("w_out", [2, 1024, 1024])
        for n in ("ln1_g", "ln1_b", "ln2_g", "ln2_b"):
            inp(n, [2, 128, 8])
        inp("w_ffn_gate", [2, 1024, 2816]); inp("w_ffn_up", [2, 1024, 2816]); inp("w_ffn_down", [2, 2816, 1024])
        inp("c_inv", [128, 1]); inp("c_sgn", [128, 1]); inp("c_tau", [128, TC + 1]); inp("c_metapos", [128, 16])
        inp("c_bmask", [128, 32])
        d["out"] = nc.dram_tensor("out", [ns, 2048, 1024], F32, kind="ExternalOutput").ap()
        self.d = d

    def dbg(self, name, ap, tiles, shape):
        if name not in self.debug:
            return
        o = self.nc.dram_tensor("dbg_" + name, list(shape), ap.dtype, kind="ExternalOutput").ap()
        self.dbg_outs[name] = shape
        self.fw.dma("sp", o, ap, reads=tiles, writes=[Tile()])

    def E(self, eng, method, *args, reads=(), writes=(), **kw):
        self.fw.op(eng, method, args, kw, reads, writes)

    def bank(self):
        i = self.bank_i
        self.bank_i = (i + 1) % len(self.bank_list)
        b = self.bank_list[i]
        return self.P[b], self.PT[b]

    def wbuf(self):
        i = self.w_i
        self.w_i = (i + 1) % len(self.WB)
        return self.WB[i]

    def load_w(self, src2d, kin, ncols, rows=128):
        wb = self.wbuf()
        view = wb.t[0:rows, 0:kin * ncols].rearrange("p (k c) -> p k c", c=ncols)
        self.fw.dma("pool", view, src2d.rearrange("(k p) c -> p k c", p=rows), writes=[wb.T()])
        return view, wb.T()

    def proj(self, specs, act, consume, ranges=NT):
        loaded = {}
        D = 2

        def ensure(i):
            if i < len(specs) and i not in loaded:
                s = specs[i]
                loaded[i] = self.load_w(s[0], s[1], s[2], s[3] if len(s) > 3 else 128)
        for i in range(min(D, len(specs))):
            ensure(i)
        for mi, s in enumerate(specs):
            ensure(mi + D)
            wv, wt = loaded.pop(mi)
            kin, ncols = s[1], s[2]
            for ni, (n0, n1) in enumerate(ranges):
                pb, pt = self.bank()
                for k in range(kin):
                    a, at = act(k)
                    self.E("pe", "matmul", pb[0:ncols, 0:n1 - n0], wv[:, k, :], a[:, n0:n1], start=(k == 0), stop=(k == kin - 1),
                           reads=[wt, at], writes=[pt])
                consume(mi, ni, pb, pt, n0, n1)

    def layer_norm(self, xk, xt, g, b, ranges, fill=None, ni_base=0):
        A = self.A
        m0 = A.mark()
        SQ = [A.alloc("lnsq", [128, 512], F32) for _ in range(2)]
        MEANS = [A.alloc("lnmean", [128, 512], F32) for _ in range(2)]
        RSTDS = [A.alloc("lnrstd", [128, 512], F32) for _ in range(2)]

        def stats(idx):
            n0, n1 = ranges[idx]
            ni = idx + ni_base
            n = n1 - n0
            MEAN, RSTD = MEANS[idx % 2], RSTDS[idx % 2]
            pa, pat = self.bank()
            pb, pbt = self.bank()
            if fill is not None:
                for k in range(8):
                    fill(k, ni, n0, n1)
            for k in range(8):
                sq = SQ[k % 2]
                x = xk(k, n0, n1)
                self.E("act", "activation", sq[:, 0:n], x, AF.Square,
                       reads=[xt(k, ni)], writes=[sq.T()])
                self.E("pe", "matmul", pa[:, 0:n], self.ONESF[:, 0:128], x, start=(k == 0), stop=(k == 7),
                       reads=[xt(k, ni), self.ONESF.T()], writes=[pat])
                self.E("pe", "matmul", pb[:, 0:n], self.ONESF[:, 0:128], sq[:, 0:n], start=(k == 0), stop=(k == 7),
                       reads=[sq.T(), self.ONESF.T()], writes=[pbt])
            self.E("act", "activation", MEAN[:, 0:n], pa[:, 0:n], AF.Copy, scale=1.0 / 1024,
                   reads=[pat], writes=[MEAN.T()])
            self.E("dve", "tensor_tensor", RSTD[:, 0:n], MEAN[:, 0:n], MEAN[:, 0:n], ALU.mult,
                   reads=[MEAN.T()], writes=[RSTD.T()])
            self.E("dve", "scalar_tensor_tensor", RSTD[:, 0:n], pb[:, 0:n], 1.0 / 1024, RSTD[:, 0:n], ALU.mult, ALU.subtract,
                   reads=[pbt, RSTD.T()], writes=[RSTD.T()])
            self.E("act", "activation", RSTD[:, 0:n], RSTD[:, 0:n], AF.Sqrt, bias=self.EPS5[:, 0:1], scale=1.0,
                   reads=[RSTD.T(), self.EPS5.T()], writes=[RSTD.T()])
            self.E("dve", "reciprocal", RSTD[:, 0:n], RSTD[:, 0:n], reads=[RSTD.T()], writes=[RSTD.T()])

        def apply(idx):
            n0, n1 = ranges[idx]
            ni = idx + ni_base
            n = n1 - n0
            MEAN, RSTD = MEANS[idx % 2], RSTDS[idx % 2]
            for k in range(8):
                x = xk(k, n0, n1)
                t = xt(k, ni)
                self.E("dve", "tensor_tensor", x, x, MEAN[:, 0:n], ALU.subtract, reads=[t, MEAN.T()], writes=[t])
                self.E("dve", "tensor_tensor", x, x, RSTD[:, 0:n], ALU.mult, reads=[t, RSTD.T()], writes=[t])
                self.E("act", "activation", x, x, AF.Identity, bias=b[:, k:k + 1], scale=g[:, k:k + 1],
                       reads=[t, g.T(), b.T()], writes=[t])
                hi = self.HI[:, k, n0:n1]
                lo = self.LO[:, k, n0:n1]
                self.E("act", "copy", hi, x, reads=[t], writes=[self.HI.T((k, ni))])
                self.E("pool", "tensor_tensor", lo, x, hi, ALU.subtract,
                       reads=[t, self.HI.T((k, ni))], writes=[self.LO.T((k, ni))])
        stats(0)
        for idx in range(len(ranges)):
            if idx + 1 < len(ranges):
                stats(idx + 1)
            apply(idx)
        A.release(m0)

    def act_hi(self, k):
        return self.HI[:, k, :], self.HI.T("all")

    def build(self):
        nc = bass.Bass("TRN2", target_bir_lowering=False)
        self.nc = nc
        self.declare(nc)
        d = self.d
        with contextlib.ExitStack() as st:
            fw = FW(nc, st)
            self.fw = fw
            A = Alloc(nc, 212480)
            self.A = A
            self.P = [st.enter_context(nc.psum_tensor("ps%d" % i, [128, 512], F32)) for i in range(8)]
            self.PT = [Tile("ps%d" % i) for i in range(8)]
            self.bank_list = [0, 1, 2, 3, 4, 5, 6, 7]
            self.bank_i = 0
            self.ONESF = A.alloc("onesf", [128, L], F32)
            self.IDF = A.alloc("identf", [128, 128], F32)
            self.IDB = A.alloc("identb", [128, 128], BF16)
            self.TRI = A.alloc("tri", [128, 128], BF16)
            self.EPS5 = A.alloc("eps5", [128, 1], F32)
            self.EPS6 = A.alloc("eps6", [128, 1], F32)
            self.HALFPI = A.alloc("halfpi", [128, 1], F32)
            self.LNP = A.alloc("lnp", [128, 10, 8], F32)
            self.LBT = A.alloc("lbt", [128, 2, 2, 4], F32)
            self.HGN = A.alloc("hgn", [128, 2, 4], F32)
            self.CINV = A.alloc("cinv", [128, 1], F32)
            self.CSGN = A.alloc("csgn", [128, 1], F32)
            self.QKN = A.alloc("qkn", [128, 2, 2, 2], F32)
            self.WB = [A.alloc("wb%d" % i, [128, 1408], BF16) for i in range(8)]
            self.w_i = 0
            self.HI = A.alloc("hi", [128, 8, L], BF16)
            self.LO = A.alloc("lo", [128, 8, L], BF16)
            E = self.E
            E("dve", "memset", self.ONESF[:], 1.0, writes=[self.ONESF.T()])
            E("pool", "memset", self.IDF[:], 0.0, writes=[self.IDF.T()])
            E("pool", "affine_select", self.IDF[:], self.IDF[:], [[-1, 128]], ALU.not_equal, 1.0, base=0, channel_multiplier=1,
              reads=[self.IDF.T()], writes=[self.IDF.T()])
            E("dve", "tensor_copy", self.IDB[:], self.IDF[:], reads=[self.IDF.T()], writes=[self.IDB.T()])
            E("pool", "memset", self.TRI[:], 1.0, writes=[self.TRI.T()])
            E("pool", "affine_select", self.TRI[:], self.TRI[:], [[1, 128]], ALU.is_ge, 0.0, base=0, channel_multiplier=-1,
              reads=[self.TRI.T()], writes=[self.TRI.T()])
            E("dve", "memset", self.EPS5[:], 1e-5, writes=[self.EPS5.T()])
            E("dve", "memset", self.EPS6[:], 1e-6, writes=[self.EPS6.T()])
            E("dve", "memset", self.HALFPI[:], math.pi / 2, writes=[self.HALFPI.T()])
            for i, n in enumerate(["ln_in_g", "ln_in_b"]):
                fw.dma("sp", self.LNP[:, i, :], d[n], writes=[self.LNP.T()])
            for l in range(2):
                for i, n in enumerate(["ln1_g", "ln1_b", "ln2_g", "ln2_b"]):
                    fw.dma("sp", self.LNP[:, 2 + 4 * l + i, :], d[n][l], writes=[self.LNP.T()])
                fw.dma("sp", self.HGN[:, l, :], d["hg_norm"][l], writes=[self.HGN.T()])
                fw.dma("sp", self.QKN[:, l, 0, :], d["q_norm"][l], writes=[self.QKN.T()])
                fw.dma("sp", self.QKN[:, l, 1, :], d["kv_norm"][l], writes=[self.QKN.T()])
            fw.dma("sp", self.CINV[:], d["c_inv"], writes=[self.CINV.T()])
            fw.dma("sp", self.CSGN[:], d["c_sgn"], writes=[self.CSGN.T()])
            m0 = A.mark()
            LG = A.alloc("lg", [128, 2, 4], F32)
            for l in range(2):
                fw.dma("sp", LG[:, l, :], d["lb_logits"][l], writes=[LG.T()])
            E("dve", "memset", self.LBT[:, 0, 0, :], 0.0, writes=[self.LBT.T()])
            E("dve", "memset", self.LBT[:, 0, 1, :], 1.0, writes=[self.LBT.T()])
            E("dve", "tensor_tensor", LG[:, 1, :], LG[:, 1, :], LG[:, 0, :], ALU.subtract, reads=[LG.T()], writes=[LG.T()])
            E("act", "activation", self.LBT[:, 1, 0, :], LG[:, 1, :], AF.Sigmoid, reads=[LG.T()], writes=[self.LBT.T()])
            E("dve", "tensor_scalar", self.LBT[:, 1, 1, :], self.LBT[:, 1, 0, :], -1.0, 1.0, ALU.mult, ALU.add,
              reads=[self.LBT.T()], writes=[self.LBT.T()])
            fw.barrier()
            A.release(m0)

            for s in range(self.nseq):
                self.seq(s)
            fw.finish()
        print("SBUF peak bytes/partition:", A.peak, " instr counts:", {n: e.count for n, e in fw.engs.items()})
        return nc

    def seq(self, s):
        fw, A, d, E = self.fw, self.A, self.d, self.E
        self.HI.tiles = {}
        self.LO.tiles = {}
        ms = A.mark()
        m0 = A.mark()
        X = A.alloc("xin_fm", [128, 8, 512], F32)
        XIN = [A.alloc("xin%d" % i, [128, 1024], F32) for i in range(2)]
        xi = 0
        for ni, (n0, n1) in enumerate(NT):
            for (c0, sz) in TOKT:
                if not (n0 <= c0 < n1):
                    continue
                xb = XIN[xi % 2]
                xi += 1
                src = d["meta_tokens"] if c0 == 0 else d["x"][s, c0 - 16:c0 - 16 + sz, :]
                fw.dma("sp", xb[0:sz, :], src, writes=[xb.T()])
                for k in range(8):
                    pb, pt = self.bank()
                    E("pe", "transpose", pb[:, 0:sz], xb[0:sz, k * 128:(k + 1) * 128], self.IDF[0:sz, 0:sz],
                      reads=[xb.T(), self.IDF.T()], writes=[pt])
                    eng = "act" if k % 2 else "dve"
                    dst = X[:, k, c0 - n0:c0 - n0 + sz]
                    if eng == "act":
                        E("act", "copy", dst, pb[:, 0:sz], reads=[pt], writes=[X.T(k)])
                    else:
                        E("dve", "tensor_copy", dst, pb[:, 0:sz], reads=[pt], writes=[X.T(k)])
            self.layer_norm(lambda k, a, b, X=X, n0=n0: X[:, k, a - n0:b - n0], lambda k, ni_, X=X: X.T(k),
                            Buf(self.LNP[:, 0, :], "g0"), Buf(self.LNP[:, 1, :], "b0"), [(n0, n1)])
        fw.barrier()
        A.release(m0)
        self.dbg("h0", self.HI[:, :, :], [], [128, 8, L])
        for l in range(2):
            self.layer(s, l)
        m0 = A.mark()
        YT = [A.alloc("yt%d" % i, [128, 128], F32) for i in range(2)]
        OB = [A.alloc("ob%d" % i, [128, 1024], F32) for i in range(2)]
        for ti, (c0, sz) in enumerate(TOKT[1:]):
            ob = OB[ti % 2]
            for k in range(8):
                yt = YT[k % 2]
                E("dve", "tensor_tensor", yt[:], self.HI[:, k, c0:c0 + 128], self.LO[:, k, c0:c0 + 128], ALU.add,
                  reads=[], writes=[yt.T()])
                pb, pt = self.bank()
                E("pe", "transpose", pb[:, 0:128], yt[:], self.IDF[:], reads=[yt.T(), self.IDF.T()], writes=[pt])
                E("act", "copy", ob[:, k * 128:(k + 1) * 128], pb[:, 0:128], reads=[pt], writes=[ob.T()])
            fw.dma("sp", d["out"][s, c0 - 16:c0 - 16 + 128, :], ob[:], reads=[ob.T()], writes=[Tile()])
        fw.barrier()
        A.release(m0)
        A.release(ms)

    def sincos(self, ANG, KK, SIN, COS, n):
        E = self.E
        E("dve", "tensor_scalar", KK[:, 0:n], ANG[:, 0:n], 1.0 / TWO_PI, MAGIC, ALU.mult, ALU.add, reads=[ANG.T()], writes=[KK.T()])
        E("dve", "tensor_scalar", KK[:, 0:n], KK[:, 0:n], -MAGIC, None, ALU.add, reads=[KK.T()], writes=[KK.T()])
        E("dve", "scalar_tensor_tensor", ANG[:, 0:n], KK[:, 0:n], -C1, ANG[:, 0:n], ALU.mult, ALU.add, reads=[KK.T(), ANG.T()], writes=[ANG.T()])
        E("dve", "scalar_tensor_tensor", ANG[:, 0:n], KK[:, 0:n], -C2, ANG[:, 0:n], ALU.mult, ALU.add, reads=[KK.T(), ANG.T()], writes=[ANG.T()])
        E("dve", "scalar_tensor_tensor", ANG[:, 0:n], KK[:, 0:n], -C3, ANG[:, 0:n], ALU.mult, ALU.add, reads=[KK.T(), ANG.T()], writes=[ANG.T()])
        E("dve", "tensor_scalar", ANG[:, 0:n], ANG[:, 0:n], -math.pi, math.pi, ALU.max, ALU.min, reads=[ANG.T()], writes=[ANG.T()])
        E("act", "activation", SIN[:, 0:n], ANG[:, 0:n], AF.Sin, reads=[ANG.T()], writes=[SIN.T()])
        E("act", "activation", KK[:, 0:n], ANG[:, 0:n], AF.Abs, reads=[ANG.T()], writes=[KK.T()])
        E("act", "activation", COS[:, 0:n], KK[:, 0:n], AF.Sin, bias=self.HALFPI[:, 0:1], scale=-1.0,
          reads=[KK.T(), self.HALFPI.T()], writes=[COS.T()])

    def layer(self, s, l):
        fw, A, d, E = self.fw, self.A, self.d, self.E
        W = d["w_in"][l]
        ml = A.mark()
        BRM = A.alloc("brm", [128, 4, L], BF16)
        self.mla(s, l, W, BRM)
        fw.barrier()
        self.dbg("brm%d" % l, BRM[:, :, :], [], [128, 4, L])
        BRS = A.alloc("brs", [128, 4, L], BF16)
        self.s5(s, l, W, BRS)
        fw.barrier()
        self.dbg("brs%d" % l, BRS[:, :, :], [], [128, 4, L])
        BRH = A.alloc("brh", [128, 4, L], BF16)
        self.hgrn(s, l, W, BRH)
        fw.barrier()
        self.dbg("brh%d" % l, BRH[:, :, :], [], [128, 4, L])
        MIX = A.alloc("mix", [128, 8, L], BF16)
        m0 = A.mark()
        GT = [A.alloc("gt%d" % i, [128, 512], F32) for i in range(3)]
        PR = [A.alloc("pr%d" % i, [128, 512], F32) for i in range(3)]
        brs = [(BRM, d["w_br_mla"][l]), (BRS, d["w_br_s5"][l]), (BRH, d["w_br_hg"][l])]
        for m in range(8):
            gw = [self.load_w(W[:, 3104 + b * 1024 + m * 128: 3104 + b * 1024 + (m + 1) * 128], 8, 128) for b in range(3)]
            bw = [self.load_w(brs[b][1][:, m * 128:(m + 1) * 128], 4, 128) for b in range(3)]
            for ni, (n0, n1) in enumerate(NT):
                n = n1 - n0
                for b in range(3):
                    pg, pgt = self.bank()
                    for k in range(8):
                        E("pe", "matmul", pg[:, 0:n1 - n0], gw[b][0][:, k, :], self.HI[:, k, n0:n1], start=(k == 0), stop=(k == 7),
                          reads=[gw[b][1], self.HI.T("all")], writes=[pgt])
                    g = GT[b]
                    p = PR[b]
                    src = brs[b][0]
                    E("act", "activation", g[:, 0:n], pg[:, 0:n], AF.Sigmoid, reads=[pgt], writes=[GT[b].T()])
                    py, pyt = self.bank()
                    for k in range(4):
                        E("pe", "matmul", py[:, 0:n1 - n0], bw[b][0][:, k, :], src[:, k, n0:n1], start=(k == 0), stop=(k == 3),
                          reads=[bw[b][1], brs[b][0].T("all")], writes=[pyt])
                    E("dve", "tensor_tensor", p[:, 0:n], py[:, 0:n], g[:, 0:n], ALU.mult,
                      reads=[pyt, GT[b].T()], writes=[PR[b].T()])
                E("dve", "tensor_tensor", PR[0][:, 0:n], PR[0][:, 0:n], PR[1][:, 0:n], ALU.add, reads=[PR[0].T(), PR[1].T()], writes=[PR[0].T()])
                E("dve", "tensor_tensor", MIX[:, m, n0:n1], PR[0][:, 0:n], PR[2][:, 0:n], ALU.add,
                  reads=[PR[0].T(), PR[2].T()], writes=[MIX.T("all")])
        fw.barrier()
        A.release(m0)
        self.dbg("mix%d" % l, MIX[:, :, :], [], [128, 8, L])
        T32 = [A.alloc("t32_%d" % i, [128, 512], F32) for i in range(2)]
        self.HI.tiles = {}
        self.LO.tiles = {}

        def cons_out(mi, ni, pb, pt, n0, n1):
            n = n1 - n0
            t = T32[ni % 2]
            hk = self.HI.T(("w", mi, ni))
            lk = self.LO.T(("w", mi, ni))
            E("dve", "scalar_tensor_tensor", t[:, 0:n], self.HI[:, mi, n0:n1], ALPHA, pb[:, 0:n], ALU.mult, ALU.add,
              reads=[pt, hk], writes=[t.T()])
            E("dve", "scalar_tensor_tensor", t[:, 0:n], self.LO[:, mi, n0:n1], ALPHA, t[:, 0:n], ALU.mult, ALU.add,
              reads=[t.T(), lk], writes=[t.T()])
            E("act", "copy", self.HI[:, mi, n0:n1], t[:, 0:n], reads=[t.T()], writes=[hk])
            E("dve", "tensor_tensor", self.LO[:, mi, n0:n1], t[:, 0:n], self.HI[:, mi, n0:n1], ALU.subtract, reads=[t.T(), hk], writes=[lk])
        self.proj([(d["w_out"][l][:, m * 128:(m + 1) * 128], 8, 128) for m in range(8)],
                  lambda k: (MIX[:, k, :], MIX.T("all")), cons_out)
        fw.barrier()
        A.release(ml)
        self.HI.tiles = {}
        self.LO.tiles = {}
        XX = [A.alloc("ln1x%d" % i, [128, 8, 512], F32) for i in range(2)]

        def fill1(k, ni, n0, n1):
            X = XX[ni % 2]
            E("dve", "tensor_tensor", X[:, k, 0:n1 - n0], self.HI[:, k, n0:n1], self.LO[:, k, n0:n1], ALU.add,
              reads=[self.HI.T((k, ni)), self.LO.T((k, ni))], writes=[X.T(k)])
        nidx = {r: i for i, r in enumerate(NT)}
        self.layer_norm(lambda k, a, b: XX[nidx[(a, b)] % 2][:, k, 0:b - a], lambda k, ni_: XX[ni_ % 2].T(k),
                        Buf(self.LNP[:, 2 + 4 * l, :], "g1"), Buf(self.LNP[:, 3 + 4 * l, :], "b1"), NT, fill=fill1)
        fw.barrier()
        A.release(ml)
        self.HI.tiles = {}
        self.LO.tiles = {}
        self.dbg("h1_%d" % l, self.HI[:, :, :], [], [128, 8, L])
        RACC = A.alloc("racc", [128, 8, L], F32)
        mf = A.mark()
        ACTT = A.alloc("actt", [128, 8, L], BF16)
        SG = [A.alloc("sg%d" % i, [128, 512], F32) for i in range(2)]
        for k in range(8):
            E("dve", "tensor_tensor", RACC[:, k, :], self.HI[:, k, :], self.LO[:, k, :], ALU.add, reads=[], writes=[RACC.T("all")])
            E("dve", "tensor_scalar", RACC[:, k, :], RACC[:, k, :], ALPHA, None, ALU.mult, reads=[RACC.T("all")], writes=[RACC.T("all")])
        fw.barrier()
        groups = [list(range(g, min(g + 8, 22))) for g in range(0, 22, 8)]
        for grp in groups:
            for fi, f in enumerate(grp):
                wg, wgt = self.load_w(d["w_ffn_gate"][l][:, f * 128:(f + 1) * 128], 8, 128)
                wu, wut = self.load_w(d["w_ffn_up"][l][:, f * 128:(f + 1) * 128], 8, 128)
                for ni, (n0, n1) in enumerate(NT):
                    n = n1 - n0
                    pg, pgt = self.bank()
                    pu, put = self.bank()
                    for k in range(8):
                        E("pe", "matmul", pg[:, 0:n1 - n0], wg[:, k, :], self.HI[:, k, n0:n1], start=(k == 0), stop=(k == 7),
                          reads=[wgt, self.HI.T("all")], writes=[pgt])
                    for k in range(8):
                        E("pe", "matmul", pu[:, 0:n1 - n0], wu[:, k, :], self.HI[:, k, n0:n1], start=(k == 0), stop=(k == 7),
                          reads=[wut, self.HI.T("all")], writes=[put])
                    sg = SG[ni % 2]
                    E("act", "activation", sg[:, 0:n], pg[:, 0:n], AF.Silu, reads=[pgt], writes=[sg.T()])
                    E("dve", "tensor_tensor", ACTT[:, fi, n0:n1], pu[:, 0:n], sg[:, 0:n], ALU.mult,
                      reads=[put, sg.T()], writes=[ACTT.T("all")])
            ng = len(grp)
            f0 = grp[0]

            def cons_dn(mi, ni, pb, pt, n0, n1):
                n = n1 - n0
                E("dve", "tensor_tensor", RACC[:, mi, n0:n1], RACC[:, mi, n0:n1], pb[:, 0:n], ALU.add,
                  reads=[pt], writes=[RACC.T((mi, ni))])
            self.proj([(d["w_ffn_down"][l][f0 * 128:(f0 + ng) * 128, m * 128:(m + 1) * 128], ng, 128) for m in range(8)],
                      lambda k: (ACTT[:, k, :], ACTT.T("all")), cons_dn)
        fw.barrier()
        A.release(mf)
        self.HI.tiles = {}
        self.LO.tiles = {}
        self.layer_norm(lambda k, a, b: RACC[:, k, a:b], lambda k, ni: RACC.T((k, ni)),
                        Buf(self.LNP[:, 4 + 4 * l, :], "g2"), Buf(self.LNP[:, 5 + 4 * l, :], "b2"), NT)
        fw.barrier()
        self.HI.tiles = {}
        self.LO.tiles = {}
        self.dbg("h2_%d" % l, self.HI[:, :, :], [], [128, 8, L])
        A.release(ml)

    def rope_tables(self, s):
        fw, A, d, E = self.fw, self.A, self.d, self.E
        self.TCOS = A.alloc("tcos", [128, L], F32)
        self.TSIN = A.alloc("tsin", [128, L], F32)
        m0 = A.mark()
        PI = A.alloc("posi", [128, 2048], I32)
        ANG = A.alloc("ang", [128, L], F32)
        KK = A.alloc("kk", [128, L], F32)
        fw.dma("sp", PI[:], d["positions"][s:s + 1, :].partition_broadcast(128), writes=[PI.T()])
        fw.dma("sp", ANG[:, 0:16], d["c_metapos"], writes=[ANG.T()])
        E("dve", "tensor_copy", ANG[:, 16:L], PI[:], reads=[PI.T(), ANG.T()], writes=[ANG.T()])
        E("dve", "tensor_scalar", ANG[:, 16:L], ANG[:, 16:L], 16.0, None, ALU.add, reads=[ANG.T()], writes=[ANG.T()])
        E("dve", "tensor_scalar", ANG[:], ANG[:], self.CINV[:, 0:1], None, ALU.mult, reads=[ANG.T(), self.CINV.T()], writes=[ANG.T()])
        self.sincos(ANG, KK, self.TSIN, self.TCOS, L)
        E("dve", "tensor_scalar", self.TSIN[:], self.TSIN[:], self.CSGN[:, 0:1], None, ALU.mult,
          reads=[self.TSIN.T(), self.CSGN.T()], writes=[self.TSIN.T()])
        fw.barrier()
        A.release(m0)

    def mla(self, s, l, W, BRM):
        fw, A, d, E = self.fw, self.A, self.d, self.E
        m0 = A.mark()
        self.rope_tables(s)
        CN = A.alloc("cn", [128, 4, L], BF16)
        RAW = A.alloc("craw", [128, 2, L], F32)
        KRO = A.alloc("kro", [128, L], BF16)
        SQ = A.alloc("msq", [128, 512], F32)
        R = A.alloc("mr", [128, 512], F32)
        T1 = A.alloc("mt1", [128, 512], F32)
        T2 = A.alloc("mt2", [128, 512], F32)
        for which in range(2):
            def cons(mi, ni, pb, pt, n0, n1):
                E("act", "copy", RAW[:, mi, n0:n1], pb[:, 0:n1 - n0], reads=[pt], writes=[RAW.T((mi, ni))])
            self.proj([(W[:, which * 256 + m * 128: which * 256 + (m + 1) * 128], 8, 128) for m in range(2)], self.act_hi, cons)
            for ni, (n0, n1) in enumerate(NT):
                n = n1 - n0
                pb, pt = self.bank()
                for m in range(2):
                    E("act", "activation", SQ[:, 0:n], RAW[:, m, n0:n1], AF.Square, reads=[RAW.T((m, ni))], writes=[SQ.T()])
                    E("pe", "matmul", pb[:, 0:n], self.ONESF[:, 0:128], SQ[:, 0:n], start=(m == 0), stop=(m == 1),
                      reads=[SQ.T(), self.ONESF.T()], writes=[pt])
                E("act", "activation", R[:, 0:n], pb[:, 0:n], AF.Sqrt, bias=self.EPS6[:, 0:1], scale=1.0 / 256,
                  reads=[pt, self.EPS6.T()], writes=[R.T()])
                E("dve", "reciprocal", R[:, 0:n], R[:, 0:n], reads=[R.T()], writes=[R.T()])
                for m in range(2):
                    E("dve", "scalar_tensor_tensor", CN[:, 2 * which + m, n0:n1], RAW[:, m, n0:n1], self.QKN[:, l, which, m:m + 1], R[:, 0:n], ALU.mult, ALU.mult,
                      reads=[RAW.T((m, ni)), R.T(), self.QKN.T()], writes=[CN.T("all")])
        wa, wat = self.load_w(d["w_kr"][l], 8, 96)
        wb_, wbt = self.load_w(d["w_kr_sw"][l], 8, 96)
        for ni, (n0, n1) in enumerate(NT):
            n = n1 - n0
            pa, pat = self.bank()
            pb, pbt = self.bank()
            for k in range(8):
                E("pe", "matmul", pa[0:96, 0:n], wa[:, k, :], self.HI[:, k, n0:n1], start=(k == 0), stop=(k == 7), reads=[wat, self.HI.T("all")], writes=[pat])
            for k in range(8):
                E("pe", "matmul", pb[0:96, 0:n], wb_[:, k, :], self.HI[:, k, n0:n1], start=(k == 0), stop=(k == 7), reads=[wbt, self.HI.T("all")], writes=[pbt])
            E("dve", "tensor_tensor", T1[64:96, 0:n], pa[64:96, 0:n], self.TCOS[64:96, n0:n1], ALU.mult, reads=[pat, self.TCOS.T()], writes=[T1.T()])
            E("dve", "tensor_tensor", T2[64:96, 0:n], pb[64:96, 0:n], self.TSIN[64:96, n0:n1], ALU.mult, reads=[pbt, self.TSIN.T()], writes=[T2.T()])
            E("dve", "tensor_tensor", KRO[64:96, n0:n1], T1[64:96, 0:n], T2[64:96, 0:n], ALU.add, reads=[T1.T(), T2.T()], writes=[KRO.T()])
        QT = [A.alloc("qt%d" % i, [128, L], BF16) for i in range(2)]
        KT = [A.alloc("kt%d" % i, [128, L], BF16) for i in range(2)]
        VA = [A.alloc("va%d" % i, [128, 17, 128], BF16) for i in range(2)]
        PTB = [A.alloc("ptb%d" % i, [128, 512], BF16) for i in range(3)]
        RC = A.alloc("rc", [128, 512], F32)
        E("pool", "memset", VA[0][:, :, 64:128], 1.0, writes=[VA[0].T()])
        E("pool", "memset", VA[1][:, :, 0:64], 1.0, writes=[VA[1].T()])
        scale = 96.0 ** -0.5
        pti = 0
        poi = 0
        self.bank_list = [0, 1, 2, 3, 4, 5]
        self.bank_i = 0
        RC2 = [RC, A.alloc("rc2", [128, 512], F32)]
        st = {"pti": 0, "poi": 0}

        def proj_head(h):
            par = h % 2
            qt, kt, va = QT[par], KT[par], VA[par]
            vo = 0 if par == 0 else 64
            wq, wqt = self.load_w(d["w_uq"][l][:, h * 96:(h + 1) * 96], 2, 96)
            ws, wst = self.load_w(d["w_uq_sw"][l][:, h * 96:(h + 1) * 96], 2, 96)
            wk, wkt = self.load_w(d["w_ukv"][l][:, h * 128:h * 128 + 64], 2, 64)
            wv, wvt = self.load_w(d["w_ukv"][l][:, h * 128 + 64:h * 128 + 128], 2, 64)
            for ni, (n0, n1) in enumerate(NT):
                n = n1 - n0
                pa, pat = self.bank()
                pb, pbt = self.bank()
                pc, pct = self.bank()
                for k in range(2):
                    E("pe", "matmul", pa[0:96, 0:n], wq[:, k, :], CN[:, k, n0:n1], start=(k == 0), stop=(k == 1), reads=[wqt, CN.T("all")], writes=[pat])
                for k in range(2):
                    E("pe", "matmul", pb[0:96, 0:n], ws[:, k, :], CN[:, k, n0:n1], start=(k == 0), stop=(k == 1), reads=[wst, CN.T("all")], writes=[pbt])
                for k in range(2):
                    E("pe", "matmul", pc[0:64, 0:n], wk[:, k, :], CN[:, 2 + k, n0:n1], start=(k == 0), stop=(k == 1), reads=[wkt, CN.T("all")], writes=[pct])
                E("act", "copy", qt[0:64, n0:n1], pa[0:64, 0:n], reads=[pat], writes=[qt.T()])
                E("dve", "tensor_tensor", T1[64:96, 0:n], pa[64:96, 0:n], self.TCOS[64:96, n0:n1], ALU.mult, reads=[pat, self.TCOS.T()], writes=[T1.T()])
                E("dve", "tensor_tensor", T2[64:96, 0:n], pb[64:96, 0:n], self.TSIN[64:96, n0:n1], ALU.mult, reads=[pbt, self.TSIN.T()], writes=[T2.T()])
                E("dve", "tensor_tensor", qt[64:96, n0:n1], T1[64:96, 0:n], T2[64:96, 0:n], ALU.add, reads=[T1.T(), T2.T()], writes=[qt.T()])
                E("act", "copy", kt[0:64, n0:n1], pc[0:64, 0:n], reads=[pct], writes=[kt.T()])
            E("pool", "tensor_copy", kt[64:96, :], KRO[64:96, :], reads=[KRO.T()], writes=[kt.T()])
            for ti, (c0, sz) in enumerate(TOKT):
                pv, pvt = self.bank()
                for k in range(2):
                    E("pe", "matmul", pv[0:sz, 0:64], CN[:, 2 + k, c0:c0 + sz], wv[:, k, :], start=(k == 0), stop=(k == 1), reads=[wvt, CN.T("all")], writes=[pvt])
                E("act", "copy", va[0:sz, ti, vo:vo + 64], pv[0:sz, 0:64], reads=[pvt], writes=[va.T()])

        def attn_head(h):
            par = h % 2
            qt, kt, va = QT[par], KT[par], VA[par]
            items = []
            for ni, (q0, q1) in enumerate(NT):
                keys = [(ti, c0, sz) for ti, (c0, sz) in enumerate(TOKT) if c0 < q1]
                pbk = 6 + (st["poi"] % 2)
                st["poi"] += 1
                for ji, (ti, c0, sz) in enumerate(keys):
                    items.append(dict(q0=q0, q1=q1, ti=ti, c0=c0, sz=sz, first=(ji == 0), last=(ji == len(keys) - 1), pbk=pbk))

            def emit_S(it):
                qa = max(it["q0"], it["c0"])
                it["qa"] = qa
                it["nq"] = it["q1"] - qa
                it["ps"], it["pst"] = self.bank()
                E("pe", "matmul", it["ps"][0:it["sz"], 0:it["nq"]], kt[0:96, it["c0"]:it["c0"] + it["sz"]], qt[0:96, qa:it["q1"]], start=True, stop=True,
                  reads=[kt.T(), qt.T()], writes=[it["pst"]])
            for i in range(min(2, len(items))):
                emit_S(items[i])
            for i, it in enumerate(items):
                sz, nq, qa, q0, q1 = it["sz"], it["nq"], it["qa"], it["q0"], it["q1"]
                ptb = PTB[st["pti"] % 3]
                st["pti"] += 1
                po, pot = self.P[it["pbk"]], self.PT[it["pbk"]]
                E("act", "activation", ptb[0:sz, 0:nq], it["ps"][0:sz, 0:nq], AF.Exp, scale=scale, reads=[it["pst"]], writes=[ptb.T()])
                if it["c0"] >= q0:
                    E("dve", "tensor_tensor", ptb[0:sz, 0:sz], ptb[0:sz, 0:sz], self.TRI[0:sz, 0:sz], ALU.mult,
                      reads=[ptb.T(), self.TRI.T()], writes=[ptb.T()])
                if i + 2 < len(items):
                    emit_S(items[i + 2])
                E("pe", "matmul", po[:, qa - q0:q1 - q0], va[0:sz, it["ti"], :], ptb[0:sz, 0:nq], start=it["first"], stop=it["last"],
                  reads=[va.T(), ptb.T()], writes=[pot])
                if it["last"]:
                    n = q1 - q0
                    rc = RC2[it["pbk"] % 2]
                    if par == 0:
                        E("dve", "reciprocal", rc[0:64, 0:n], po[64:128, 0:n], reads=[pot], writes=[rc.T()])
                        E("dve", "tensor_tensor", BRM[0:64, h // 2, q0:q1], po[0:64, 0:n], rc[0:64, 0:n], ALU.mult, reads=[pot, rc.T()], writes=[BRM.T("all")])
                    else:
                        E("dve", "reciprocal", rc[64:128, 0:n], po[0:64, 0:n], reads=[pot], writes=[rc.T()])
                        E("dve", "tensor_tensor", BRM[64:128, h // 2, q0:q1], po[64:128, 0:n], rc[64:128, 0:n], ALU.mult, reads=[pot, rc.T()], writes=[BRM.T("all")])

        proj_head(0)
        for h in range(8):
            if h + 1 < 8:
                proj_head(h + 1)
            attn_head(h)
        self.bank_list = [0, 1, 2, 3, 4, 5, 6, 7]
        self.bank_i = 0
        fw.barrier()
        A.release(m0)

    def exp_small(self, dst, x, tmp, F, halv=6):
        E = self.E
        E("dve", "tensor_scalar", tmp[:, 0:F], x[:, 0:F], 1.0 / (2 ** halv), None, ALU.mult, reads=[x.T()], writes=[tmp.T()])
        E("dve", "tensor_scalar", dst[:, 0:F], tmp[:, 0:F], 1.0 / 6, 1.0, ALU.mult, ALU.add, reads=[tmp.T()], writes=[dst.T()])
        for kdiv in (5.0, 4.0, 3.0, 2.0, 1.0):
            E("dve", "tensor_tensor", dst[:, 0:F], dst[:, 0:F], tmp[:, 0:F], ALU.mult, reads=[dst.T(), tmp.T()], writes=[dst.T()])
            E("dve", "tensor_scalar", dst[:, 0:F], dst[:, 0:F], 1.0 / kdiv, 1.0, ALU.mult, ALU.add, reads=[dst.T()], writes=[dst.T()])
        for _ in range(halv):
            E("dve", "tensor_tensor", dst[:, 0:F], dst[:, 0:F], dst[:, 0:F], ALU.mult, reads=[dst.T()], writes=[dst.T()])

    def s5_params(self, l, sfx, F):
        fw, A, d, E = self.fw, self.A, self.d, self.E
        t = {}
        for n in ("lr", "li", "dt", "rho", "th", "cr", "ci", "t1", "t2", "t3", "sn", "cs"):
            t[n] = A.alloc("s5p_" + n, [128, F], F32)
        fw.dma("sp", t["lr"][:], d["lamre_" + sfx][l], writes=[t["lr"].T()])
        fw.dma("sp", t["li"][:], d["lamim_" + sfx][l], writes=[t["li"].T()])
        fw.dma("sp", t["t1"][:], d["logdt_" + sfx][l], writes=[t["t1"].T()])
        E("dve", "tensor_scalar_min", t["lr"][:], t["lr"][:], -1e-4, reads=[t["lr"].T()], writes=[t["lr"].T()])
        self.exp_small(t["dt"], t["t1"], t["t2"], F)
        E("dve", "tensor_tensor", t["t1"][:], t["lr"][:], t["dt"][:], ALU.mult, reads=[t["lr"].T(), t["dt"].T()], writes=[t["t1"].T()])
        self.exp_small(t["rho"], t["t1"], t["t2"], F, halv=0)
        E("dve", "tensor_tensor", t["th"][:], t["li"][:], t["dt"][:], ALU.mult, reads=[t["li"].T(), t["dt"].T()], writes=[t["th"].T()])
        E("dve", "tensor_copy", t["t1"][:], t["th"][:], reads=[t["th"].T()], writes=[t["t1"].T()])
        self.sincos(t["t1"], t["t2"], t["sn"], t["cs"], F)
        E("dve", "tensor_tensor", t["cs"][:], t["cs"][:], t["rho"][:], ALU.mult, reads=[t["cs"].T(), t["rho"].T()], writes=[t["cs"].T()])
        E("dve", "tensor_tensor", t["sn"][:], t["sn"][:], t["rho"][:], ALU.mult, reads=[t["sn"].T(), t["rho"].T()], writes=[t["sn"].T()])
        E("dve", "tensor_tensor", t["t1"][:], t["lr"][:], t["lr"][:], ALU.mult, reads=[t["lr"].T()], writes=[t["t1"].T()])
        E("dve", "tensor_tensor", t["t2"][:], t["li"][:], t["li"][:], ALU.mult, reads=[t["li"].T()], writes=[t["t2"].T()])
        E("dve", "tensor_tensor", t["t1"][:], t["t1"][:], t["t2"][:], ALU.add, reads=[t["t1"].T(), t["t2"].T()], writes=[t["t1"].T()])
        E("dve", "reciprocal", t["t1"][:], t["t1"][:], reads=[t["t1"].T()], writes=[t["t1"].T()])
        E("dve", "tensor_scalar", t["t2"][:], t["cs"][:], -1.0, None, ALU.add, reads=[t["cs"].T()], writes=[t["t2"].T()])
        E("dve", "tensor_tensor", t["cr"][:], t["t2"][:], t["lr"][:], ALU.mult, reads=[t["t2"].T(), t["lr"].T()], writes=[t["cr"].T()])
        E("dve", "tensor_tensor", t["t3"][:], t["sn"][:], t["li"][:], ALU.mult, reads=[t["sn"].T(), t["li"].T()], writes=[t["t3"].T()])
        E("dve", "tensor_tensor", t["cr"][:], t["cr"][:], t["t3"][:], ALU.add, reads=[t["cr"].T(), t["t3"].T()], writes=[t["cr"].T()])
        E("dve", "tensor_tensor", t["cr"][:], t["cr"][:], t["t1"][:], ALU.mult, reads=[t["cr"].T(), t["t1"].T()], writes=[t["cr"].T()])
        E("dve", "tensor_tensor", t["ci"][:], t["sn"][:], t["lr"][:], ALU.mult, reads=[t["sn"].T(), t["lr"].T()], writes=[t["ci"].T()])
        E("dve", "tensor_tensor", t["t3"][:], t["t2"][:], t["li"][:], ALU.mult, reads=[t["t2"].T(), t["li"].T()], writes=[t["t3"].T()])
        E("dve", "tensor_tensor", t["ci"][:], t["ci"][:], t["t3"][:], ALU.subtract, reads=[t["ci"].T(), t["t3"].T()], writes=[t["ci"].T()])
        E("dve", "tensor_tensor", t["ci"][:], t["ci"][:], t["t1"][:], ALU.mult, reads=[t["ci"].T(), t["t1"].T()], writes=[t["ci"].T()])
        return t

    def s5(self, s, l, W, BRS):
        fw, A, d, E = self.fw, self.A, self.d, self.E
        m0 = A.mark()
        def cons_u(mi, ni, pb, pt, n0, n1):
            E("act", "copy", BRS[:, mi, n0:n1], pb[:, 0:n1 - n0], reads=[pt], writes=[BRS.T("all")])
        self.proj([(W[:, 544 + m * 128:544 + (m + 1) * 128], 8, 128) for m in range(4)], self.act_hi, cons_u)
        BBR = A.alloc("bbr", [128, 16, 128], BF16)
        BBI = A.alloc("bbi", [128, 16, 128], BF16)
        CRE = A.alloc("cre", [128, 16, 128], BF16)
        CNR = A.alloc("cnr", [128, 16, 128], BF16)
        CNI = A.alloc("cni", [128, 16, 128], BF16)
        WG = A.alloc("wglu", [128, 4, 512], BF16)
        DSK = A.alloc("dsk", [128, 4], F32)
        RHO = A.alloc("rho", [128, 16], F32)
        THI = A.alloc("thi", [128, 16], F32)
        TLO = A.alloc("tlo", [128, 16], F32)
        fw.dma("pool", CRE[:].rearrange("p j c -> p (j c)"), d["cre_pad"][l], writes=[CRE.T()])
        fw.dma("pool", CNI[:].rearrange("p j c -> p (j c)"), d["cim_pad"][l], writes=[CNI.T()])
        fw.dma("pool", WG[:], d["w_glu"][l].rearrange("(k p) c -> p k c", p=128), writes=[WG.T()])
        fw.dma("sp", DSK[:], d["s5_d"][l], writes=[DSK.T()])
        E("dve", "tensor_scalar", CNR[:], CRE[:], -1.0, None, ALU.mult, reads=[CRE.T()], writes=[CNR.T()])
        E("dve", "tensor_scalar", CNI[:], CNI[:], -1.0, None, ALU.mult, reads=[CNI.T()], writes=[CNI.T()])
        m1 = A.mark()
        pc = self.s5_params(l, "c", 256)
        BRE = A.alloc("bre", [128, 256], F32)
        BIM = A.alloc("bim", [128, 256], F32)
        BT1 = A.alloc("bt1", [128, 256], F32)
        BT2 = A.alloc("bt2", [128, 256], F32)
        BMASK = A.alloc("bmask", [128, 32], F32)
        fw.dma("sp", BRE[:], d["bre_c"][l], writes=[BRE.T()])
        fw.dma("sp", BIM[:], d["bim_c"][l], writes=[BIM.T()])
        fw.dma("sp", BMASK[:], d["c_bmask"], writes=[BMASK.T()])
        E("dve", "tensor_tensor", BT1[:], pc["cr"][:], BRE[:], ALU.mult, reads=[pc["cr"].T(), BRE.T()], writes=[BT1.T()])
        E("dve", "tensor_tensor", BT2[:], pc["ci"][:], BIM[:], ALU.mult, reads=[pc["ci"].T(), BIM.T()], writes=[BT2.T()])
        E("dve", "tensor_tensor", BT1[:], BT1[:], BT2[:], ALU.subtract, reads=[BT1.T(), BT2.T()], writes=[BT1.T()])
        E("dve", "tensor_tensor", BT2[:], pc["cr"][:], BIM[:], ALU.mult, reads=[pc["cr"].T(), BIM.T(), BT2.T()], writes=[BT2.T()])
        E("dve", "tensor_tensor", BRE[:], pc["ci"][:], BRE[:], ALU.mult, reads=[pc["ci"].T(), BRE.T()], writes=[BRE.T()])
        E("dve", "tensor_tensor", BT2[:], BT2[:], BRE[:], ALU.add, reads=[BT2.T(), BRE.T()], writes=[BT2.T()])
        for j in range(16):
            jc = j // 4
            for gl in range(2):
                mcol = BMASK[:, 2 * j + gl:2 * j + gl + 1]
                E("dve", "tensor_scalar", BBR[:, j, gl * 64:(gl + 1) * 64], BT1[:, jc * 64:(jc + 1) * 64], mcol, None, ALU.mult,
                  reads=[BT1.T(), BMASK.T()], writes=[BBR.T()])
                E("dve", "tensor_scalar", BBI[:, j, gl * 64:(gl + 1) * 64], BT2[:, jc * 64:(jc + 1) * 64], mcol, None, ALU.mult,
                  reads=[BT2.T(), BMASK.T()], writes=[BBI.T()])
        fw.barrier()
        A.release(m1)
        ps_ = self.s5_params(l, "s", 16)
        E("dve", "tensor_copy", RHO[:], ps_["rho"][:], reads=[ps_["rho"].T()], writes=[RHO.T()])
        E("dve", "tensor_single_scalar", THI[:].bitcast(I32), ps_["th"][:].bitcast(I32), -4096, ALU.bitwise_and, reads=[ps_["th"].T()], writes=[THI.T()])
        E("dve", "tensor_tensor", TLO[:], ps_["th"][:], THI[:], ALU.subtract, reads=[ps_["th"].T(), THI.T()], writes=[TLO.T()])
        fw.barrier()
        A.release(m1)
        self.dbg("s5rho%d" % l, RHO[:], [], [128, 16])
        self.dbg("s5bbr%d" % l, BBR[:, :, :], [], [128, 16, 128])
        TAU = A.alloc("tau", [128, TC + 1], F32)
        fw.dma("sp", TAU[:], d["c_tau"], writes=[TAU.T()])
        COS = A.alloc("s5cos", [128, TC + 1], F32)
        SIN = A.alloc("s5sin", [128, TC + 1], F32)
        ANG = A.alloc("s5ang", [128, TC + 1], F32)
        KK = A.alloc("s5kk", [128, TC + 1], F32)
        RHOB = A.alloc("rhob", [128, TC], F32)
        W1 = [A.alloc("s5w%d" % i, [128, TC], F32) for i in range(6)]
        WRI = [[A.alloc("s5wr%d_%d" % (b_, i), [128, TC], F32) for i in range(2)] for b_ in range(2)]
        PBB = [[A.alloc("s5p%d_%d" % (b_, i), [128, TC], BF16) for i in range(4)] for b_ in range(2)]
        BU = [[A.alloc("s5bu%d_%d" % (b_, i), [128, TC], F32) for i in range(2)] for b_ in range(2)]
        cnt = 0
        cntb = [0]
        INI = A.alloc("s5ini", [128, 4], F32)
        INI2 = Buf(INI[:, 2:4], "ini2")
        CS2 = A.alloc("s5cs2", [128, 2], F32)
        NSC = A.alloc("s5nsc", [128, 2], F32)
        YF = A.alloc("s5y", [128, TC], F32)
        YT1 = A.alloc("s5yt1", [128, TC], F32)
        n = TC
        for jc in range(4):
            for j in range(4 * jc, 4 * jc + 4):
                E("dve", "tensor_scalar", ANG[:], TAU[:], THI[:, j:j + 1], None, ALU.mult, reads=[TAU.T(), THI.T(), ANG.T()], writes=[ANG.T()])
                E("dve", "tensor_scalar", KK[:], ANG[:], 1.0 / TWO_PI, MAGIC, ALU.mult, ALU.add, reads=[ANG.T(), KK.T()], writes=[KK.T()])
                E("dve", "tensor_scalar", KK[:], KK[:], -MAGIC, None, ALU.add, reads=[KK.T()], writes=[KK.T()])
                E("dve", "scalar_tensor_tensor", ANG[:], KK[:], -C1, ANG[:], ALU.mult, ALU.add, reads=[KK.T(), ANG.T()], writes=[ANG.T()])
                E("dve", "scalar_tensor_tensor", ANG[:], KK[:], -C2, ANG[:], ALU.mult, ALU.add, reads=[KK.T(), ANG.T()], writes=[ANG.T()])
                E("dve", "scalar_tensor_tensor", ANG[:], TAU[:], TLO[:, j:j + 1], ANG[:], ALU.mult, ALU.add, reads=[TAU.T(), TLO.T(), ANG.T()], writes=[ANG.T()])
                E("dve", "tensor_scalar", ANG[:], ANG[:], -math.pi, math.pi, ALU.max, ALU.min, reads=[ANG.T()], writes=[ANG.T()])
                E("act", "activation", SIN[:], ANG[:], AF.Sin, reads=[ANG.T()], writes=[SIN.T()])
                E("act", "activation", KK[:], ANG[:], AF.Abs, reads=[ANG.T()], writes=[KK.T()])
                E("act", "activation", COS[:], KK[:], AF.Sin, bias=self.HALFPI[:, 0:1], scale=-1.0, reads=[KK.T(), self.HALFPI.T()], writes=[COS.T()])
                E("dve", "tensor_scalar", RHOB[:], self.ONESF[:, 0:TC], RHO[:, j:j + 1], None, ALU.mult, reads=[RHO.T(), self.ONESF.T()], writes=[RHOB.T()])
                E("dve", "memset", INI[:], 0.0, writes=[INI.T()])
                E("dve", "tensor_copy", CS2[:, 0:1], COS[:, TC:TC + 1], reads=[COS.T(), CS2.T()], writes=[CS2.T()])
                E("dve", "tensor_copy", CS2[:, 1:2], SIN[:, TC:TC + 1], reads=[SIN.T(), CS2.T()], writes=[CS2.T()])
                E("dve", "tensor_scalar", NSC[:, 0:1], SIN[:, TC:TC + 1], -1.0, None, ALU.mult, reads=[SIN.T(), NSC.T()], writes=[NSC.T()])
                E("dve", "tensor_copy", NSC[:, 1:2], COS[:, TC:TC + 1], reads=[COS.T(), NSC.T()], writes=[NSC.T()])
                def emitB(cc, j=j, jc=jc):
                    cs_, ce_ = S5CH[cc]
                    pr, prt = self.P[6], self.PT[6]
                    pi_, pit = self.P[7], self.PT[7]
                    E("pe", "matmul", pr[:, 0:n], BBR[:, j, :], BRS[:, jc, cs_:ce_], start=True, stop=True, reads=[BBR.T(), BRS.T((jc, cc)), BRS.T("all")], writes=[prt])
                    E("pe", "matmul", pi_[:, 0:n], BBI[:, j, :], BRS[:, jc, cs_:ce_], start=True, stop=True, reads=[BBI.T(), BRS.T((jc, cc)), BRS.T("all")], writes=[pit])
                    bur_, bui_ = BU[(cntb[0]) % 2]
                    cntb[0] += 1
                    E("act", "copy", bur_[:], pr[:, 0:n], reads=[prt], writes=[bur_.T()])
                    E("act", "copy", bui_[:], pi_[:, 0:n], reads=[pit], writes=[bui_.T()])
                emitB(0)
                for c, (cs, ce) in enumerate(S5CH):
                    if c + 1 < len(S5CH):
                        emitB(c + 1)
                    t1, t2, t3, t4, rr, ri = W1
                    wr, wi = WRI[cnt % 2]
                    PB_ = PBB[cnt % 2]
                    bur, bui = BU[cnt % 2]
                    cnt += 1
                    E("dve", "tensor_tensor", t1[:], bur[:], COS[:, 0:n], ALU.mult, reads=[bur.T(), COS.T()], writes=[t1.T()])
                    E("dve", "tensor_tensor", t2[:], bui[:], SIN[:, 0:n], ALU.mult, reads=[bui.T(), SIN.T()], writes=[t2.T()])
                    E("dve", "tensor_tensor", t3[:], bui[:], COS[:, 0:n], ALU.mult, reads=[bui.T(), COS.T()], writes=[t3.T()])
                    E("dve", "tensor_tensor", t4[:], bur[:], SIN[:, 0:n], ALU.mult, reads=[bur.T(), SIN.T()], writes=[t4.T()])
                    E("dve", "tensor_tensor", rr[:], t1[:], t2[:], ALU.add, reads=[t1.T(), t2.T()], writes=[rr.T()])
                    E("dve", "tensor_tensor", ri[:], t3[:], t4[:], ALU.subtract, reads=[t3.T(), t4.T()], writes=[ri.T()])
                    E("dve", "tensor_tensor_scan", wr[:], RHOB[:], rr[:], INI[:, 0:1], ALU.mult, ALU.add, reads=[RHOB.T(), rr.T(), INI.T()], writes=[wr.T()])
                    E("dve", "tensor_tensor_scan", wi[:], RHOB[:], ri[:], INI[:, 1:2], ALU.mult, ALU.add, reads=[RHOB.T(), ri.T(), INI.T()], writes=[wi.T()])
                    E("act", "activation", INI[:, 2:4], NSC[:, 0:2], AF.Copy, scale=wi[:, n - 1:n], reads=[wi.T(), NSC.T()], writes=[INI2.T()])
                    E("act", "activation", INI[:, 0:1], CS2[:, 0:1], AF.Identity, bias=INI[:, 2:3], scale=wr[:, n - 1:n], reads=[wr.T(), CS2.T(), INI2.T()], writes=[INI.T()])
                    E("act", "activation", INI[:, 1:2], CS2[:, 1:2], AF.Identity, bias=INI[:, 3:4], scale=wr[:, n - 1:n], reads=[wr.T(), CS2.T(), INI2.T()], writes=[INI.T()])
                    E("pool", "tensor_tensor", PB_[0][:], COS[:, 0:n], wr[:], ALU.mult, reads=[COS.T(), wr.T()], writes=[PB_[0].T()])
                    E("pool", "tensor_tensor", PB_[1][:], SIN[:, 0:n], wi[:], ALU.mult, reads=[SIN.T(), wi.T()], writes=[PB_[1].T()])
                    E("pool", "tensor_tensor", PB_[2][:], SIN[:, 0:n], wr[:], ALU.mult, reads=[SIN.T(), wr.T()], writes=[PB_[2].T()])
                    E("pool", "tensor_tensor", PB_[3][:], COS[:, 0:n], wi[:], ALU.mult, reads=[COS.T(), wi.T()], writes=[PB_[3].T()])
                    py, pyt = self.P[c], self.PT[c]
                    first = (j == 4 * jc)
                    last = (j == 4 * jc + 3)
                    for q, (cm, pbuf) in enumerate(((CRE, PB_[0]), (CNR, PB_[1]), (CNI, PB_[2]), (CNI, PB_[3]))):
                        E("pe", "matmul", py[:, 0:n], cm[:, j, :], pbuf[:], start=(first and q == 0), stop=(last and q == 3),
                          reads=[cm.T(), pbuf.T()], writes=[pyt])
            for c, (cs, ce) in enumerate(S5CH):
                py, pyt = self.P[c], self.PT[c]
                E("dve", "scalar_tensor_tensor", YF[:], BRS[:, jc, cs:ce], DSK[:, jc:jc + 1], py[:, 0:n], ALU.mult, ALU.add,
                  reads=[pyt, BRS.T((jc, c)), BRS.T("all"), DSK.T()], writes=[YF.T()])
                E("act", "activation", YT1[:], YF[:], AF.Square, reads=[YF.T()], writes=[YT1.T()])
                E("dve", "tensor_scalar", YT1[:], YT1[:], 0.044715, 1.0, ALU.mult, ALU.add, reads=[YT1.T()], writes=[YT1.T()])
                E("dve", "tensor_tensor", YT1[:], YT1[:], YF[:], ALU.mult, reads=[YT1.T(), YF.T()], writes=[YT1.T()])
                E("act", "activation", YT1[:], YT1[:], AF.Sigmoid, scale=GELU_K, reads=[YT1.T()], writes=[YT1.T()])
                E("dve", "tensor_tensor", BRS[:, jc, cs:ce], YF[:], YT1[:], ALU.mult, reads=[YT1.T(), YF.T()], writes=[BRS.T((jc, c)), BRS.T("all")])
        fw.barrier()
        BRS.tiles = {}
        self.dbg("s5yg%d" % l, BRS[:, :, :], [], [128, 4, L])
        SGT = A.alloc("s5sg", [128, 4, 512], BF16)
        for ni, (n0, n1) in enumerate(NT):
            nn = n1 - n0
            for m in range(4):
                pb, pt = self.bank()
                for k in range(4):
                    E("pe", "matmul", pb[:, 0:nn], WG[:, k, m * 128:(m + 1) * 128], BRS[:, k, n0:n1], start=(k == 0), stop=(k == 3),
                      reads=[WG.T(), BRS.T("all")], writes=[pt])
                E("act", "activation", SGT[:, m, 0:nn], pb[:, 0:nn], AF.Sigmoid, reads=[pt], writes=[SGT.T()])
            for m in range(4):
                E("dve", "tensor_tensor", BRS[:, m, n0:n1], BRS[:, m, n0:n1], SGT[:, m, 0:nn], ALU.mult, reads=[SGT.T(), BRS.T("all")], writes=[BRS.T("all")])
        fw.barrier()
        BRS.tiles = {}
        A.release(m0)

    def hgrn(self, s, l, W, BRH):
        fw, A, d, E = self.fw, self.A, self.d, self.E
        m0 = A.mark()
        FK = A.alloc("hgFK", [128, 2 * L], F32)
        Fb = Buf(FK[:, 0:L], "hgF")
        Kb = Buf(FK[:, L:2 * L], "hgK")
        XS = Buf(FK[:, 0:4096].rearrange("p (i v) -> p i v", v=128), "hgXS")
        Fb.tiles = FK.tiles
        Kb.tiles = FK.tiles
        XS.tiles = FK.tiles
        CUM = A.alloc("hgC", [128, L], F32)
        SBA = Buf(CUM[:, 0:2048].bitcast(BF16).rearrange("p (i v) -> p i v", v=128), "hgSBA")
        SBA.tiles = CUM.tiles
        Eb = A.alloc("hgE", [128, L], F32)
        QT = A.alloc("hgq", [128, L], BF16)
        KT = A.alloc("hgk", [128, L], BF16)
        SGt = A.alloc("hgsg", [128, L], BF16)
        VT = A.alloc("hgv", [64, 33, 128], BF16)
        REF = A.alloc("hgref", [128, 33], F32)
        DD = A.alloc("hgdd", [128, 32], F32)
        ATM = [A.alloc("hgatm%d" % i, [64, 64], BF16) for i in range(3)]
        KTOK = [A.alloc("hgktok%d" % i, [64, 128], BF16) for i in range(3)]
        ON = [A.alloc("hgon%d" % i, [64, 128], BF16) for i in range(3)]
        JUNK = A.alloc("hgjunk", [64, 128], F32)
        SSL = [A.alloc("hgss%d" % i, [64, 2], F32) for i in range(4)]
        base = 1056
        for h in range(4):
            lb = self.LBT[:, l, 0, h:h + 1]
            oml = self.LBT[:, l, 1, h:h + 1]
            def cons_zf(mi, ni, pb, pt, n0, n1):
                E("act", "activation", Fb[:, n0:n1], pb[:, 0:n1 - n0], AF.Sigmoid, reads=[pt], writes=[Fb.T()])
            self.proj([(W[:, base + 512 + h * 128: base + 512 + (h + 1) * 128], 8, 128)], self.act_hi, cons_zf)
            def cons_g(mi, ni, pb, pt, n0, n1):
                E("act", "activation", SGt[:, n0:n1], pb[:, 0:n1 - n0], AF.Silu, reads=[pt], writes=[SGt.T()])
            self.proj([(W[:, base + 1536 + h * 128: base + 1536 + (h + 1) * 128], 8, 128)], self.act_hi, cons_g)
            wv, wvt = self.load_w(W[:, base + 1024 + h * 128: base + 1024 + (h + 1) * 128], 8, 128)
            for i, (c0, sz) in enumerate(HCH):
                pv, pvt = self.bank()
                for k in range(8):
                    E("pe", "matmul", pv[0:sz, 0:128], self.HI[:, k, c0:c0 + sz], wv[:, k, :], start=(k == 0), stop=(k == 7), reads=[wvt, self.HI.T("all")], writes=[pvt])
                E("act", "copy", VT[0:sz, i, :], pv[0:sz, 0:128], reads=[pvt], writes=[VT.T()])
            E("dve", "tensor_scalar", Fb[:], Fb[:], oml, lb, ALU.mult, ALU.add, reads=[Fb.T(), self.LBT.T()], writes=[Fb.T()])
            E("dve", "tensor_scalar", Kb[:], Fb[:], -1.0, 1.0, ALU.mult, ALU.add, reads=[Fb.T(), Kb.T()], writes=[Kb.T()])
            E("dve", "tensor_scalar_max", Fb[:], Fb[:], 1e-6, reads=[Fb.T()], writes=[Fb.T()])
            E("act", "activation", Fb[:], Fb[:], AF.Ln, reads=[Fb.T()], writes=[Fb.T()])
            E("dve", "tensor_tensor_scan", CUM[:], self.ONESF[:, 0:L], Fb[:], 0.0, ALU.mult, ALU.add, reads=[Fb.T(), self.ONESF.T(), CUM.T()], writes=[CUM.T()])
            E("dve", "tensor_copy", REF[:, 0:1], CUM[:, 8:9], reads=[CUM.T(), REF.T()], writes=[REF.T()])
            E("dve", "tensor_copy", REF[:, 1:33], CUM[:, 48:L:64], reads=[CUM.T(), REF.T()], writes=[REF.T()])
            E("dve", "tensor_tensor", DD[:], REF[:, 1:33], REF[:, 0:32], ALU.subtract, reads=[REF.T(), DD.T()], writes=[DD.T()])
            E("act", "activation", DD[:], DD[:], AF.Exp, reads=[DD.T()], writes=[DD.T()])
            for i, (c0, sz) in enumerate(HCH):
                E("dve", "tensor_scalar", CUM[:, c0:c0 + sz], CUM[:, c0:c0 + sz], REF[:, i:i + 1], None, ALU.subtract, reads=[CUM.T(), REF.T()], writes=[CUM.T()])
            E("act", "activation", Eb[:], CUM[:], AF.Exp, reads=[CUM.T(), Eb.T()], writes=[Eb.T()])
            def cons_q(mi, ni, pb, pt, n0, n1):
                E("dve", "tensor_tensor", QT[:, n0:n1], pb[:, 0:n1 - n0], Eb[:, n0:n1], ALU.mult, reads=[pt, Eb.T()], writes=[QT.T()])
            self.proj([(W[:, base + h * 128: base + (h + 1) * 128], 8, 128)], self.act_hi, cons_q)
            E("act", "activation", Eb[:], CUM[:], AF.Exp, scale=-1.0, reads=[CUM.T(), Eb.T(), QT.T()], writes=[Eb.T()])
            E("dve", "tensor_tensor", KT[:], Kb[:], Eb[:], ALU.mult, reads=[Kb.T(), Eb.T(), KT.T()], writes=[KT.T()])
            NCH = len(HCH)

            def p1_tr(i):
                c0, sz = HCH[i]
                pk, pkt = self.bank()
                pkb = pk[:].bitcast(BF16)
                E("pe", "transpose", pkb[0:sz, 0:128], KT[:, c0:c0 + sz], self.IDB[:], reads=[KT.T(), self.IDB.T()], writes=[pkt])
                ktok = KTOK[i % 3]
                E("act", "copy", ktok[0:sz, :], pkb[0:sz, 0:128], reads=[pkt], writes=[ktok.T()])

            def p1_mm(i):
                c0, sz = HCH[i]
                ktok = KTOK[i % 3]
                pt_, ptt = self.bank()
                E("pe", "matmul", pt_[:, 0:128], ktok[0:sz, :], VT[0:sz, i, :], start=True, stop=True, reads=[ktok.T(), VT.T()], writes=[ptt])
                E("dve", "tensor_scalar", XS[:, i, :], pt_[:, 0:128], DD[:, i:i + 1], None, ALU.mult, reads=[ptt, DD.T(), XS.T()], writes=[XS.T()])
            for t in range(NCH):
                if t < NCH - 1:
                    p1_tr(t)
                if 1 <= t:
                    p1_mm(t - 1)
            for i in range(1, NCH - 1):
                E("dve", "scalar_tensor_tensor", XS[:, i, :], XS[:, i - 1, :], DD[:, i:i + 1], XS[:, i, :], ALU.mult, ALU.add,
                  reads=[XS.T(), DD.T()], writes=[XS.T()])
            for q in range(4):
                E("act", "copy", SBA[:, 8 * q:8 * q + 8, :], XS[:, 8 * q:8 * q + 8, :], reads=[XS.T(), SBA.T()], writes=[SBA.T()])
            stt = {}

            def stA(i):
                c0, sz = HCH[i]
                pa, pat = self.bank()
                atm = ATM[i % 3]
                E("pe", "matmul", pa[0:sz, 0:sz], KT[:, c0:c0 + sz], QT[:, c0:c0 + sz], start=True, stop=True, reads=[KT.T(), QT.T()], writes=[pat])
                E("dve", "tensor_tensor", atm[0:sz, 0:sz], pa[0:sz, 0:sz], self.TRI[0:sz, 0:sz], ALU.mult, reads=[pat, self.TRI.T()], writes=[atm.T()])

            def stB(i):
                c0, sz = HCH[i]
                atm, SS = ATM[i % 3], SSL[i % 4]
                po, pot = self.bank()
                stt[i] = (po, pot)
                E("pe", "matmul", po[0:sz, 0:128], atm[0:sz, 0:sz], VT[0:sz, i, :], start=True, stop=(i == 0), reads=[atm.T(), VT.T()], writes=[pot])
                if i > 0:
                    E("pe", "matmul", po[0:sz, 0:128], QT[:, c0:c0 + sz], SBA[:, i - 1, :], start=False, stop=True, reads=[QT.T(), SBA.T()], writes=[pot])
                E("dve", "memset", SS[:, 0:1], 0.0, reads=[SS.T()], writes=[SS.T()])
                E("act", "activation", JUNK[0:sz, :], po[0:sz, 0:128], AF.Square, accum_out=SS[0:sz, 0:1], reads=[pot, SS.T(), JUNK.T()], writes=[SS.T(), JUNK.T()])

            def stB2(i):
                c0, sz = HCH[i]
                on, SS = ON[i % 3], SSL[i % 4]
                po, pot = stt.pop(i)
                E("act", "activation", SS[0:sz, 1:2], SS[0:sz, 0:1], AF.Ln, bias=self.EPS6[0:sz, 0:1], scale=1.0 / 128, reads=[SS.T(), self.EPS6.T()], writes=[SS.T()])
                E("act", "activation", SS[0:sz, 1:2], SS[0:sz, 1:2], AF.Exp, scale=-0.5, reads=[SS.T()], writes=[SS.T()])
                E("dve", "tensor_scalar", on[0:sz, :], po[0:sz, 0:128], SS[0:sz, 1:2], None, ALU.mult, reads=[pot, SS.T(), on.T()], writes=[on.T()])

            def stC(i):
                c0, sz = HCH[i]
                on = ON[i % 3]
                pe_, pet = self.bank()
                peb = pe_[:].bitcast(BF16)
                E("pe", "transpose", peb[:, 0:sz], on[0:sz, :], self.IDB[0:sz, 0:sz], reads=[on.T(), self.IDB.T()], writes=[pet])
                E("dve", "scalar_tensor_tensor", BRH[:, h, c0:c0 + sz], peb[:, 0:sz], self.HGN[:, l, h:h + 1], SGt[:, c0:c0 + sz], ALU.mult, ALU.mult,
                  reads=[pet, self.HGN.T(), SGt.T()], writes=[BRH.T("all")])
            for t in range(NCH + 3):
                if t < NCH:
                    stA(t)
                if 0 <= t - 1 < NCH:
                    stB(t - 1)
                if 0 <= t - 2 < NCH:
                    stB2(t - 2)
                if 0 <= t - 3 < NCH:
                    stC(t - 3)
        fw.barrier()
        A.release(m0)


def host_layout(inp):
    f = np.float32
    o = {}

    def fm(v):
        return np.ascontiguousarray(np.asarray(v, f).reshape(8, 128).T)
    o["meta_tokens"] = np.ascontiguousarray(inp["meta_tokens"], f)
    o["ln_in_g"] = fm(inp["ln_in_g"]); o["ln_in_b"] = fm(inp["ln_in_b"])
    w_in = np.ascontiguousarray(inp["w_in"], f)
    o["w_in"] = w_in
    kr = np.zeros((2, 1024, 96), f); krs = np.zeros((2, 1024, 96), f)
    kr[:, :, 64:96] = w_in[:, :, 512:544]
    krs[:, :, 64:80] = w_in[:, :, 528:544]
    krs[:, :, 80:96] = w_in[:, :, 512:528]
    o["w_kr"] = kr; o["w_kr_sw"] = krs
    o["q_norm"] = np.ascontiguousarray(np.asarray(inp["mla_q_norm"], f).reshape(2, 2, 128).transpose(0, 2, 1))
    o["kv_norm"] = np.ascontiguousarray(np.asarray(inp["mla_kv_norm"], f).reshape(2, 2, 128).transpose(0, 2, 1))
    uq = np.asarray(inp["mla_w_uq"], f)
    o["w_uq"] = np.ascontiguousarray(uq)
    uqs = np.zeros_like(uq).reshape(2, 256, 8, 96)
    uq4 = uq.reshape(2, 256, 8, 96)
    uqs[:, :, :, 64:80] = uq4[:, :, :, 80:96]
    uqs[:, :, :, 80:96] = uq4[:, :, :, 64:80]
    o["w_uq_sw"] = np.ascontiguousarray(uqs.reshape(2, 256, 768))
    o["w_ukv"] = np.ascontiguousarray(inp["mla_w_ukv"], f)
    def sm(v):
        return np.ascontiguousarray(np.asarray(v, f).reshape(2, 16, 2, 64).transpose(0, 2, 3, 1).reshape(2, 128, 16))
    lam_re = np.asarray(inp["s5_lam_re"], f); lam_im = np.asarray(inp["s5_lam_im"], f)
    logdt = np.broadcast_to(np.asarray(inp["s5_log_dt"], f)[:, :, None], (2, 32, 64))
    o["lamre_s"] = sm(lam_re); o["lamim_s"] = sm(lam_im); o["logdt_s"] = sm(logdt)
    def cm(v):
        t = np.asarray(v, f).reshape(2, 4, 8, 1, 64)
        t = np.broadcast_to(t, (2, 4, 8, 16, 64))
        return np.ascontiguousarray(t.transpose(0, 2, 3, 1, 4).reshape(2, 128, 256))
    o["lamre_c"] = cm(lam_re); o["lamim_c"] = cm(lam_im); o["logdt_c"] = cm(logdt)
    def bcm(v):
        t = np.asarray(v, f).reshape(2, 4, 8, 64, 16)
        return np.ascontiguousarray(t.transpose(0, 2, 4, 1, 3).reshape(2, 128, 256))
    o["bre_c"] = bcm(inp["s5_b_re"]); o["bim_c"] = bcm(inp["s5_b_im"])
    def cpad(v):
        v = np.asarray(v, f)
        out = np.zeros((2, 128, 16, 128), f)
        for j in range(16):
            for gl in range(2):
                g = 2 * j + gl
                g8 = g % 8
                out[:, gl * 64:(gl + 1) * 64, j, g8 * 16:(g8 + 1) * 16] = v[:, g].transpose(0, 2, 1)
        return np.ascontiguousarray(out.reshape(2, 128, 2048))
    o["cre_pad"] = cpad(inp["s5_c_re"]); o["cim_pad"] = cpad(inp["s5_c_im"])
    o["s5_d"] = np.ascontiguousarray(np.asarray(inp["s5_d"], f).reshape(2, 4, 128).transpose(0, 2, 1))
    o["w_glu"] = np.ascontiguousarray(inp["s5_w_glu"], f)
    o["lb_logits"] = np.ascontiguousarray(np.asarray(inp["hg_lb_logits"], f).reshape(2, 4, 128).transpose(0, 2, 1))
    o["hg_norm"] = np.ascontiguousarray(np.asarray(inp["hg_out_norm"], f).reshape(2, 4, 128).transpose(0, 2, 1))
    for n in ("w_br_mla", "w_br_s5", "w_br_hg", "w_out", "w_ffn_gate", "w_ffn_up", "w_ffn_down"):
        o[n] = np.ascontiguousarray(inp[n], f)
    for n in ("ln1_g", "ln1_b", "ln2_g", "ln2_b"):
        o[n] = np.ascontiguousarray(np.asarray(inp[n], f).reshape(2, 8, 128).transpose(0, 2, 1))
    inv = (10000.0 ** (-(np.arange(0, 32, 2, dtype=np.float32) / 32))).astype(f)
    cinv = np.zeros((128, 1), f); csgn = np.zeros((128, 1), f)
    cinv[64:80, 0] = inv; cinv[80:96, 0] = inv
    csgn[64:80, 0] = -1.0; csgn[80:96, 0] = 1.0
    o["c_inv"] = cinv; o["c_sgn"] = csgn
    o["c_tau"] = np.ascontiguousarray(np.broadcast_to(np.arange(TC + 1, dtype=f)[None], (128, TC + 1)))
    o["c_metapos"] = np.ascontiguousarray(np.broadcast_to(np.arange(16, dtype=f)[None], (128, 16)))
    bm = np.zeros((128, 32), f)
    for j in range(16):
        for gl in range(2):
            g8 = (2 * j + gl) % 8
            bm[g8 * 16:(g8 + 1) * 16, 2 * j + gl] = 1.0
    o["c_bmask"] = bm
    return o


_CACHE = {}


def kernel(**inputs):
    n_cores = 8
    nseq = 2
    shared = host_layout(inputs)
    x = np.ascontiguousarray(inputs["x"], np.float32)
    pos = np.ascontiguousarray(inputs["positions"], np.int32)
    if "nc" not in _CACHE:
        _CACHE["nc"] = Builder(nseq).build()
    nc = _CACHE["nc"]
    in_maps = []
    for c in range(n_cores):
        m = dict(shared)
        m["x"] = x[c * nseq:(c + 1) * nseq]
        m["positions"] = pos[c * nseq:(c + 1) * nseq]
        in_maps.append(m)
    res = run_bass_kernel_spmd(nc, in_maps, core_ids=list(range(n_cores)))
    return np.concatenate([r["out"] for r in res.results], axis=0).astype(np.float32)
```

```python
import contextlib
import math
import numpy as np
import concourse.bass as bass
import concourse.mybir as mybir
from concourse.bass_utils import run_bass_kernel_spmd

F32 = mybir.dt.float32
BF16 = mybir.dt.bfloat16
I32 = mybir.dt.int32
AF = mybir.ActivationFunctionType
ALU = mybir.AluOpType

L = 2064
NMETA = 16
NT = [(0, 400), (400, 912), (912, 1424), (1424, 1936), (1936, 2064)]
TOKT = [(0, 16)] + [(16 + 128 * i, 128) for i in range(16)]
HCH = [(0, 16)] + [(16 + 64 * i, 64) for i in range(32)]
TC = 344
S5CH = [(TC * i, TC * (i + 1)) for i in range(6)]
ALPHA = 4 ** 0.25
MAGIC = 12582912.0
TWO_PI = 2.0 * math.pi
C1 = 6.28125
C2 = float(np.float32(TWO_PI - 6.28125))
C3 = float(TWO_PI - 6.28125 - float(np.float32(TWO_PI - 6.28125)))
GELU_K = 2.0 * math.sqrt(2.0 / math.pi)


INAMES = {}


class Tile:
    __slots__ = ("name", "w", "r")

    def __init__(self, name=""):
        self.name = name
        self.w = None
        self.r = {}


class Eng:
    def __init__(self, name, handle, sem):
        self.name = name
        self.h = handle
        self.sem = sem
        self.count = 0
        self.q = []
        self.waited = {}


class FW:
    NSLOT = 16

    def __init__(self, nc, stack):
        self.nc = nc
        self.sems = {}
        self.engs = {}
        for name, h in (("pe", nc.tensor), ("act", nc.scalar), ("dve", nc.vector),
                        ("pool", nc.gpsimd), ("sp", nc.sync)):
            sem = stack.enter_context(nc.semaphore("s_" + name))
            self.sems["s_" + name] = sem
            self.engs[name] = Eng(name, h, sem)
        self.slots = {}
        for qn in ("sp", "pool"):
            lst = []
            for i in range(self.NSLOT):
                key = "d_%s_%d" % (qn, i)
                sem = stack.enter_context(nc.semaphore(key))
                self.sems[key] = sem
                lst.append([key, 0])
            self.slots[qn] = [lst, 0]

    def _deps(self, reads, writes, self_key=None):
        deps = {}
        for t in reads:
            if t.w is not None and deps.get(t.w[0], 0) < t.w[1]:
                deps[t.w[0]] = t.w[1]
        for t in writes:
            if t.w is not None and deps.get(t.w[0], 0) < t.w[1]:
                deps[t.w[0]] = t.w[1]
            for k, v in t.r.items():
                if k == self_key:
                    continue
                if deps.get(k, 0) < v:
                    deps[k] = v
        return deps

    def _emit_waits(self, eng, deps, skip_self=False):
        for k, v in deps.items():
            if skip_self and k == "s_" + eng.name:
                continue
            if eng.waited.get(k, 0) >= v:
                continue
            eng.waited[k] = v
            eng.q.append(lambda e=eng.h, s=self.sems[k], v=v: e.wait_ge(s, v))

    def _mark(self, tok, reads, writes):
        k, v = tok
        for t in reads:
            if t.r.get(k, 0) < v:
                t.r[k] = v
        for t in writes:
            t.w = tok
            t.r = {}

    def op(self, engname, method, args, kw, reads=(), writes=()):
        eng = self.engs[engname]
        deps = self._deps(reads, writes, self_key="s_" + engname)
        self._emit_waits(eng, deps, skip_self=(engname == "pe"))
        eng.count += 1
        import sys as _sys
        ln = _sys._getframe(2).f_lineno

        def _mk(e=eng.h, s=eng.sem, m=method, a=args, kw=kw, ln=ln):
            ins = getattr(e, m)(*a, **kw)
            try:
                INAMES[ins.ins.name] = (m, ln)
            except Exception:
                pass
            return ins.then_inc(s, 1)
        eng.q.append(_mk)
        self._mark(("s_" + engname, eng.count), reads, writes)

    def dma(self, qname, out, in_, reads=(), writes=()):
        eng = self.engs[qname]
        lst, idx = self.slots[qname]
        slot = lst[idx % self.NSLOT]
        self.slots[qname][1] = idx + 1
        deps = self._deps(reads, writes)
        if slot[1] > 0:
            deps[slot[0]] = max(deps.get(slot[0], 0), slot[1])
        self._emit_waits(eng, deps)
        slot[1] += 16
        eng.q.append(lambda e=eng.h, s=self.sems[slot[0]], o=out, i=in_:
                     e.dma_start(out=o, in_=i).then_inc(s, 16))
        self._mark((slot[0], slot[1]), reads, writes)

    def barrier(self):
        deps = {}
        for n, e in self.engs.items():
            if e.count:
                deps["s_" + n] = e.count
        for qn, (lst, idx) in self.slots.items():
            for key, v in lst:
                if v:
                    deps[key] = v
        for n in self.engs:
            self._emit_waits(self.engs[n], deps)

    def finish(self):
        self.barrier()
        nc = self.nc
        with nc.Block() as block:
            @block.tensor
            def _(e):
                for f in self.engs["pe"].q:
                    f()

            @block.scalar
            def _(e):
                for f in self.engs["act"].q:
                    f()

            @block.vector
            def _(e):
                for f in self.engs["dve"].q:
                    f()

            @block.gpsimd
            def _(e):
                for f in self.engs["pool"].q:
                    f()

            @block.sync
            def _(e):
                for f in self.engs["sp"].q:
                    f()


class Buf:
    def __init__(self, t, name):
        self.t = t
        self.name = name
        self.tiles = {}

    def T(self, key=0):
        if key not in self.tiles:
            self.tiles[key] = Tile("%s/%s" % (self.name, key))
        return self.tiles[key]

    def __getitem__(self, idx):
        return self.t[idx]


class Alloc:
    def __init__(self, nc, limit):
        self.nc = nc
        self.off = 0
        self.limit = limit
        self.n = 0
        self.peak = 0
        self.big = nc.alloc_sbuf_tensor("bigbuf", [128, limit // 2], BF16)

    def alloc(self, name, shape, dtype):
        nel = int(np.prod(shape[1:]))
        esz = 4 if dtype in (F32, I32) else 2
        nbytes = (nel * esz + 63) // 64 * 64
        self.n += 1
        assert self.off + nbytes <= self.limit, "SBUF overflow at %s: %d + %d" % (name, self.off, nbytes)
        ap = self.big[:, self.off // 2:(self.off + nbytes) // 2]
        if dtype != BF16:
            ap = ap.bitcast(dtype)
        ap = ap[:, 0:nel]
        if len(shape) == 3:
            ap = ap.rearrange("p (a b) -> p a b", b=shape[2])
        elif len(shape) == 4:
            ap = ap.rearrange("p (a b c) -> p a b c", b=shape[2], c=shape[3])
        if shape[0] < 128:
            ap = ap[0:shape[0]]
        self.off += nbytes
        self.peak = max(self.peak, self.off)
        return Buf(ap, name)

    def mark(self):
        return self.off

    def release(self, m):
        self.off = m


class Builder:
    def __init__(self, nseq, debug=None):
        self.nseq = nseq
        self.debug = debug or []
        self.dbg_outs = {}

    def declare(self, nc):
        d = {}

        def inp(name, shape, dt=F32):
            d[name] = nc.dram_tensor(name, list(shape), dt, kind="ExternalInput").ap()
        ns = self.nseq
        inp("x", [ns, 2048, 1024]); inp("positions", [ns, 2048], I32); inp("meta_tokens", [16, 1024])
        inp("ln_in_g", [128, 8]); inp("ln_in_b", [128, 8])
        inp("w_in", [2, 1024, 6176]); inp("w_kr", [2, 1024, 96]); inp("w_kr_sw", [2, 1024, 96])
        inp("q_norm", [2, 128, 2]); inp("kv_norm", [2, 128, 2])
        inp("w_uq", [2, 256, 768]); inp("w_uq_sw", [2, 256, 768]); inp("w_ukv", [2, 256, 1024])
        for n in ("lamre_s", "lamim_s", "logdt_s"):
            inp(n, [2, 128, 16])
        for n in ("lamre_c", "lamim_c", "logdt_c"):
            inp(n, [2, 128, 256])
        inp("bre_c", [2, 128, 256]); inp("bim_c", [2, 128, 256])
        inp("cre_pad", [2, 128, 2048]); inp("cim_pad", [2, 128, 2048])
        inp("s5_d", [2, 128, 4]); inp("w_glu", [2, 512, 512])
        inp("lb_logits", [2, 128, 4]); inp("hg_norm", [2, 128, 4])
        inp("w_br_mla", [2, 512, 1024]); inp("w_br_s5", [2, 512, 1024]); inp("w_br_hg", [2, 512, 1024])
        inp("w_out", [2, 1024, 1024])
        for n in ("ln1_g", "ln1_b", "ln2_g", "ln2_b"):
            inp(n, [2, 128, 8])
        inp("w_ffn_gate", [2, 1024, 2816]); inp("w_ffn_up", [2, 1024, 2816]); inp("w_ffn_down", [2, 2816, 1024])
        inp("c_inv", [128, 1]); inp("c_sgn", [128, 1]); inp("c_tau", [128, TC + 1]); inp("c_metapos", [128, 16])
        inp("c_bmask", [128, 32])
        d["out"] = nc.dram_tensor("out", [ns, 2048, 1024], F32, kind="ExternalOutput").ap()
        self.d = d

    def dbg(self, name, ap, tiles, shape):
        if name not in self.debug:
            return
        o = self.nc.dram_tensor("dbg_" + name, list(shape), ap.dtype, kind="ExternalOutput").ap()
        self.dbg_outs[name] = shape
        self.fw.dma("sp", o, ap, reads=tiles, writes=[Tile()])

    def E(self, eng, method, *args, reads=(), writes=(), **kw):
        self.fw.op(eng, method, args, kw, reads, writes)

    def bank(self):
        i = self.bank_i
        self.bank_i = (i + 1) % len(self.bank_list)
        b = self.bank_list[i]
        return self.P[b], self.PT[b]

    def wbuf(self):
        i = self.w_i
        self.w_i = (i + 1) % len(self.WB)
        return self.WB[i]

    def load_w(self, src2d, kin, ncols, rows=128):
        wb = self.wbuf()
        view = wb.t[0:rows, 0:kin * ncols].rearrange("p (k c) -> p k c", c=ncols)
        self.fw.dma("pool", view, src2d.rearrange("(k p) c -> p k c", p=rows), writes=[wb.T()])
        return view, wb.T()

    def proj(self, specs, act, consume, ranges=NT):
        loaded = {}
        D = 2

        def ensure(i):
            if i < len(specs) and i not in loaded:
                s = specs[i]
                loaded[i] = self.load_w(s[0], s[1], s[2], s[3] if len(s) > 3 else 128)
        for i in range(min(D, len(specs))):
            ensure(i)
        for mi, s in enumerate(specs):
            ensure(mi + D)
            wv, wt = loaded.pop(mi)
            kin, ncols = s[1], s[2]
            for ni, (n0, n1) in enumerate(ranges):
                pb, pt = self.bank()
                for k in range(kin):
                    a, at = act(k)
                    self.E("pe", "matmul", pb[0:ncols, 0:n1 - n0], wv[:, k, :], a[:, n0:n1], start=(k == 0), stop=(k == kin - 1),
                           reads=[wt, at], writes=[pt])
                consume(mi, ni, pb, pt, n0, n1)

    def layer_norm(self, xk, xt, g, b, ranges, fill=None, ni_base=0):
        A = self.A
        m0 = A.mark()
        SQ = [A.alloc("lnsq", [128, 512], F32) for _ in range(2)]
        MEANS = [A.alloc("lnmean", [128, 512], F32) for _ in range(2)]
        RSTDS = [A.alloc("lnrstd", [128, 512], F32) for _ in range(2)]

        def stats(idx):
            n0, n1 = ranges[idx]
            ni = idx + ni_base
            n = n1 - n0
            MEAN, RSTD = MEANS[idx % 2], RSTDS[idx % 2]
            pa, pat = self.bank()
            pb, pbt = self.bank()
            if fill is not None:
                for k in range(8):
                    fill(k, ni, n0, n1)
            for k in range(8):
                sq = SQ[k % 2]
                x = xk(k, n0, n1)
                self.E("act", "activation", sq[:, 0:n], x, AF.Square,
                       reads=[xt(k, ni)], writes=[sq.T()])
                self.E("pe", "matmul", pa[:, 0:n], self.ONESF[:, 0:128], x, start=(k == 0), stop=(k == 7),
                       reads=[xt(k, ni), self.ONESF.T()], writes=[pat])
                self.E("pe", "matmul", pb[:, 0:n], self.ONESF[:, 0:128], sq[:, 0:n], start=(k == 0), stop=(k == 7),
                       reads=[sq.T(), self.ONESF.T()], writes=[pbt])
            self.E("act", "activation", MEAN[:, 0:n], pa[:, 0:n], AF.Copy, scale=1.0 / 1024,
                   reads=[pat], writes=[MEAN.T()])
            self.E("dve", "tensor_tensor", RSTD[:, 0:n], MEAN[:, 0:n], MEAN[:, 0:n], ALU.mult,
                   reads=[MEAN.T()], writes=[RSTD.T()])
            self.E("dve", "scalar_tensor_tensor", RSTD[:, 0:n], pb[:, 0:n], 1.0 / 1024, RSTD[:, 0:n], ALU.mult, ALU.subtract,
                   reads=[pbt, RSTD.T()], writes=[RSTD.T()])
            self.E("act", "activation", RSTD[:, 0:n], RSTD[:, 0:n], AF.Sqrt, bias=self.EPS5[:, 0:1], scale=1.0,
                   reads=[RSTD.T(), self.EPS5.T()], writes=[RSTD.T()])
            self.E("dve", "reciprocal", RSTD[:, 0:n], RSTD[:, 0:n], reads=[RSTD.T()], writes=[RSTD.T()])

        def apply(idx):
            n0, n1 = ranges[idx]
            ni = idx + ni_base
            n = n1 - n0
            MEAN, RSTD = MEANS[idx % 2], RSTDS[idx % 2]
            for k in range(8):
                x = xk(k, n0, n1)
                t = xt(k, ni)
                self.E("dve", "tensor_tensor", x, x, MEAN[:, 0:n], ALU.subtract, reads=[t, MEAN.T()], writes=[t])
                self.E("dve", "tensor_tensor", x, x, RSTD[:, 0:n], ALU.mult, reads=[t, RSTD.T()], writes=[t])
                self.E("act", "activation", x, x, AF.Identity, bias=b[:, k:k + 1], scale=g[:, k:k + 1],
                       reads=[t, g.T(), b.T()], writes=[t])
                hi = self.HI[:, k, n0:n1]
                lo = self.LO[:, k, n0:n1]
                self.E("act", "copy", hi, x, reads=[t], writes=[self.HI.T((k, ni))])
                self.E("pool", "tensor_tensor", lo, x, hi, ALU.subtract,
                       reads=[t, self.HI.T((k, ni))], writes=[self.LO.T((k, ni))])
        stats(0)
        for idx in range(len(ranges)):
            if idx + 1 < len(ranges):
                stats(idx + 1)
            apply(idx)
        A.release(m0)

    def act_hi(self, k):
        return self.HI[:, k, :], self.HI.T("all")

    def build(self):
        nc = bass.Bass("TRN2", target_bir_lowering=False)
        self.nc = nc
        self.declare(nc)
        d = self.d
        with contextlib.ExitStack() as st:
            fw = FW(nc, st)
            self.fw = fw
            A = Alloc(nc, 212480)
            self.A = A
            self.P = [st.enter_context(nc.psum_tensor("ps%d" % i, [128, 512], F32)) for i in range(8)]
            self.PT = [Tile("ps%d" % i) for i in range(8)]
            self.bank_list = [0, 1, 2, 3, 4, 5, 6, 7]
            self.bank_i = 0
            self.ONESF = A.alloc("onesf", [128, L], F32)
            self.IDF = A.alloc("identf", [128, 128], F32)
            self.IDB = A.alloc("identb", [128, 128], BF16)
            self.TRI = A.alloc("tri", [128, 128], BF16)
            self.EPS5 = A.alloc("eps5", [128, 1], F32)
            self.EPS6 = A.alloc("eps6", [128, 1], F32)
            self.HALFPI = A.alloc("halfpi", [128, 1], F32)
            self.LNP = A.alloc("lnp", [128, 10, 8], F32)
            self.LBT = A.alloc("lbt", [128, 2, 2, 4], F32)
            self.HGN = A.alloc("hgn", [128, 2, 4], F32)
            self.CINV = A.alloc("cinv", [128, 1], F32)
            self.CSGN = A.alloc("csgn", [128, 1], F32)
            self.QKN = A.alloc("qkn", [128, 2, 2, 2], F32)
            self.WB = [A.alloc("wb%d" % i, [128, 1408], BF16) for i in range(8)]
            self.w_i = 0
            self.HI = A.alloc("hi", [128, 8, L], BF16)
            self.LO = A.alloc("lo", [128, 8, L], BF16)
            E = self.E
            E("dve", "memset", self.ONESF[:], 1.0, writes=[self.ONESF.T()])
            E("pool", "memset", self.IDF[:], 0.0, writes=[self.IDF.T()])
            E("pool", "affine_select", self.IDF[:], self.IDF[:], [[-1, 128]], ALU.not_equal, 1.0, base=0, channel_multiplier=1,
              reads=[self.IDF.T()], writes=[self.IDF.T()])
            E("dve", "tensor_copy", self.IDB[:], self.IDF[:], reads=[self.IDF.T()], writes=[self.IDB.T()])
            E("pool", "memset", self.TRI[:], 1.0, writes=[self.TRI.T()])
            E("pool", "affine_select", self.TRI[:], self.TRI[:], [[1, 128]], ALU.is_ge, 0.0, base=0, channel_multiplier=-1,
              reads=[self.TRI.T()], writes=[self.TRI.T()])
            E("dve", "memset", self.EPS5[:], 1e-5, writes=[self.EPS5.T()])
            E("dve", "memset", self.EPS6[:], 1e-6, writes=[self.EPS6.T()])
            E("dve", "memset", self.HALFPI[:], math.pi / 2, writes=[self.HALFPI.T()])
            for i, n in enumerate(["ln_in_g", "ln_in_b"]):
                fw.dma("sp", self.LNP[:, i, :], d[n], writes=[self.LNP.T()])
            for l in range(2):
                for i, n in enumerate(["ln1_g", "ln1_b", "ln2_g", "ln2_b"]):
                    fw.dma("sp", self.LNP[:, 2 + 4 * l + i, :], d[n][l], writes=[self.LNP.T()])
                fw.dma("sp", self.HGN[:, l, :], d["hg_norm"][l], writes=[self.HGN.T()])
                fw.dma("sp", self.QKN[:, l, 0, :], d["q_norm"][l], writes=[self.QKN.T()])
                fw.dma("sp", self.QKN[:, l, 1, :], d["kv_norm"][l], writes=[self.QKN.T()])
            fw.dma("sp", self.CINV[:], d["c_inv"], writes=[self.CINV.T()])
            fw.dma("sp", self.CSGN[:], d["c_sgn"], writes=[self.CSGN.T()])
            m0 = A.mark()
            LG = A.alloc("lg", [128, 2, 4], F32)
            for l in range(2):
                fw.dma("sp", LG[:, l, :], d["lb_logits"][l], writes=[LG.T()])
            E("dve", "memset", self.LBT[:, 0, 0, :], 0.0, writes=[self.LBT.T()])
            E("dve", "memset", self.LBT[:, 0, 1, :], 1.0, writes=[self.LBT.T()])
            E("dve", "tensor_tensor", LG[:, 1, :], LG[:, 1, :], LG[:, 0, :], ALU.subtract, reads=[LG.T()], writes=[LG.T()])
            E("act", "activation", self.LBT[:, 1, 0, :], LG[:, 1, :], AF.Sigmoid, reads=[LG.T()], writes=[self.LBT.T()])
            E("dve", "tensor_scalar", self.LBT[:, 1, 1, :], self.LBT[:, 1, 0, :], -1.0, 1.0, ALU.mult, ALU.add,
              reads=[self.LBT.T()], writes=[self.LBT.T()])
            fw.barrier()
            A.release(m0)

            for s in range(self.nseq):
                self.seq(s)
            fw.finish()
        print("SBUF peak bytes/partition:", A.peak, " instr counts:", {n: e.count for n, e in fw.engs.items()})
        return nc

    def seq(self, s):
        fw, A, d, E = self.fw, self.A, self.d, self.E
        self.HI.tiles = {}
        self.LO.tiles = {}
        ms = A.mark()
        m0 = A.mark()
        X = A.alloc("xin_fm", [128, 8, 512], F32)
        XIN = [A.alloc("xin%d" % i, [128, 1024], F32) for i in range(2)]
        xi = 0
        for ni, (n0, n1) in enumerate(NT):
            for (c0, sz) in TOKT:
                if not (n0 <= c0 < n1):
                    continue
                xb = XIN[xi % 2]
                xi += 1
                src = d["meta_tokens"] if c0 == 0 else d["x"][s, c0 - 16:c0 - 16 + sz, :]
                fw.dma("sp", xb[0:sz, :], src, writes=[xb.T()])
                for k in range(8):
                    pb, pt = self.bank()
                    E("pe", "transpose", pb[:, 0:sz], xb[0:sz, k * 128:(k + 1) * 128], self.IDF[0:sz, 0:sz],
                      reads=[xb.T(), self.IDF.T()], writes=[pt])
                    eng = "act" if k % 2 else "dve"
                    dst = X[:, k, c0 - n0:c0 - n0 + sz]
                    if eng == "act":
                        E("act", "copy", dst, pb[:, 0:sz], reads=[pt], writes=[X.T(k)])
                    else:
                        E("dve", "tensor_copy", dst, pb[:, 0:sz], reads=[pt], writes=[X.T(k)])
            self.layer_norm(lambda k, a, b, X=X, n0=n0: X[:, k, a - n0:b - n0], lambda k, ni_, X=X: X.T(k),
                            Buf(self.LNP[:, 0, :], "g0"), Buf(self.LNP[:, 1, :], "b0"), [(n0, n1)])
        fw.barrier()
        A.release(m0)
        self.dbg("h0", self.HI[:, :, :], [], [128, 8, L])
        for l in range(2):
            self.layer(s, l)
        m0 = A.mark()
        YT = [A.alloc("yt%d" % i, [128, 128], F32) for i in range(2)]
        OB = [A.alloc("ob%d" % i, [128, 1024], F32) for i in range(2)]
        for ti, (c0, sz) in enumerate(TOKT[1:]):
            ob = OB[ti % 2]
            for k in range(8):
                yt = YT[k % 2]
                E("dve", "tensor_tensor", yt[:], self.HI[:, k, c0:c0 + 128], self.LO[:, k, c0:c0 + 128], ALU.add,
                  reads=[], writes=[yt.T()])
                pb, pt = self.bank()
                E("pe", "transpose", pb[:, 0:128], yt[:], self.IDF[:], reads=[yt.T(), self.IDF.T()], writes=[pt])
                E("act", "copy", ob[:, k * 128:(k + 1) * 128], pb[:, 0:128], reads=[pt], writes=[ob.T()])
            fw.dma("sp", d["out"][s, c0 - 16:c0 - 16 + 128, :], ob[:], reads=[ob.T()], writes=[Tile()])
        fw.barrier()
        A.release(m0)
        A.release(ms)

    def sincos(self, ANG, KK, SIN, COS, n):
        E = self.E
        E("dve", "tensor_scalar", KK[:, 0:n], ANG[:, 0:n], 1.0 / TWO_PI, MAGIC, ALU.mult, ALU.add, reads=[ANG.T()], writes=[KK.T()])
        E("dve", "tensor_scalar", KK[:, 0:n], KK[:, 0:n], -MAGIC, None, ALU.add, reads=[KK.T()], writes=[KK.T()])
        E("dve", "scalar_tensor_tensor", ANG[:, 0:n], KK[:, 0:n], -C1, ANG[:, 0:n], ALU.mult, ALU.add, reads=[KK.T(), ANG.T()], writes=[ANG.T()])
        E("dve", "scalar_tensor_tensor", ANG[:, 0:n], KK[:, 0:n], -C2, ANG[:, 0:n], ALU.mult, ALU.add, reads=[KK.T(), ANG.T()], writes=[ANG.T()])
        E("dve", "scalar_tensor_tensor", ANG[:, 0:n], KK[:, 0:n], -C3, ANG[:, 0:n], ALU.mult, ALU.add, reads=[KK.T(), ANG.T()], writes=[ANG.T()])
        E("dve", "tensor_scalar", ANG[:, 0:n], ANG[:, 0:n], -math.pi, math.pi, ALU.max, ALU.min, reads=[ANG.T()], writes=[ANG.T()])
        E("act", "activation", SIN[:, 0:n], ANG[:, 0:n], AF.Sin, reads=[ANG.T()], writes=[SIN.T()])
        E("act", "activation", KK[:, 0:n], ANG[:, 0:n], AF.Abs, reads=[ANG.T()], writes=[KK.T()])
        E("act", "activation", COS[:, 0:n], KK[:, 0:n], AF.Sin, bias=self.HALFPI[:, 0:1], scale=-1.0,
          reads=[KK.T(), self.HALFPI.T()], writes=[COS.T()])

    def layer(self, s, l):
        fw, A, d, E = self.fw, self.A, self.d, self.E
        W = d["w_in"][l]
        ml = A.mark()
        BRM = A.alloc("brm", [128, 4, L], BF16)
        self.mla(s, l, W, BRM)
        fw.barrier()
        self.dbg("brm%d" % l, BRM[:, :, :], [], [128, 4, L])
        BRS = A.alloc("brs", [128, 4, L], BF16)
        self.s5(s, l, W, BRS)
        fw.barrier()
        self.dbg("brs%d" % l, BRS[:, :, :], [], [128, 4, L])
        BRH = A.alloc("brh", [128, 4, L], BF16)
        self.hgrn(s, l, W, BRH)
        fw.barrier()
        self.dbg("brh%d" % l, BRH[:, :, :], [], [128, 4, L])
        MIX = A.alloc("mix", [128, 8, L], BF16)
        m0 = A.mark()
        GT = [A.alloc("gt%d" % i, [128, 512], F32) for i in range(3)]
        PR = [A.alloc("pr%d" % i, [128, 512], F32) for i in range(3)]
        brs = [(BRM, d["w_br_mla"][l]), (BRS, d["w_br_s5"][l]), (BRH, d["w_br_hg"][l])]
        for m in range(8):
            gw = [self.load_w(W[:, 3104 + b * 1024 + m * 128: 3104 + b * 1024 + (m + 1) * 128], 8, 128) for b in range(3)]
            bw = [self.load_w(brs[b][1][:, m * 128:(m + 1) * 128], 4, 128) for b in range(3)]
            for ni, (n0, n1) in enumerate(NT):
                n = n1 - n0
                for b in range(3):
                    pg, pgt = self.bank()
                    for k in range(8):
                        E("pe", "matmul", pg[:, 0:n1 - n0], gw[b][0][:, k, :], self.HI[:, k, n0:n1], start=(k == 0), stop=(k == 7),
                          reads=[gw[b][1], self.HI.T("all")], writes=[pgt])
                    g = GT[b]
                    p = PR[b]
                    src = brs[b][0]
                    E("act", "activation", g[:, 0:n], pg[:, 0:n], AF.Sigmoid, reads=[pgt], writes=[GT[b].T()])
                    py, pyt = self.bank()
                    for k in range(4):
                        E("pe", "matmul", py[:, 0:n1 - n0], bw[b][0][:, k, :], src[:, k, n0:n1], start=(k == 0), stop=(k == 3),
                          reads=[bw[b][1], brs[b][0].T("all")], writes=[pyt])
                    E("dve", "tensor_tensor", p[:, 0:n], py[:, 0:n], g[:, 0:n], ALU.mult,
                      reads=[pyt, GT[b].T()], writes=[PR[b].T()])
                E("dve", "tensor_tensor", PR[0][:, 0:n], PR[0][:, 0:n], PR[1][:, 0:n], ALU.add, reads=[PR[0].T(), PR[1].T()], writes=[PR[0].T()])
                E("dve", "tensor_tensor", MIX[:, m, n0:n1], PR[0][:, 0:n], PR[2][:, 0:n], ALU.add,
                  reads=[PR[0].T(), PR[2].T()], writes=[MIX.T("all")])
        fw.barrier()
        A.release(m0)
        self.dbg("mix%d" % l, MIX[:, :, :], [], [128, 8, L])
        T32 = [A.alloc("t32_%d" % i, [128, 512], F32) for i in range(2)]
        self.HI.tiles = {}
        self.LO.tiles = {}

        def cons_out(mi, ni, pb, pt, n0, n1):
            n = n1 - n0
            t = T32[ni % 2]
            hk = self.HI.T(("w", mi, ni))
            lk = self.LO.T(("w", mi, ni))
            E("dve", "scalar_tensor_tensor", t[:, 0:n], self.HI[:, mi, n0:n1], ALPHA, pb[:, 0:n], ALU.mult, ALU.add,
              reads=[pt, hk], writes=[t.T()])
            E("dve", "scalar_tensor_tensor", t[:, 0:n], self.LO[:, mi, n0:n1], ALPHA, t[:, 0:n], ALU.mult, ALU.add,
              reads=[t.T(), lk], writes=[t.T()])
            E("act", "copy", self.HI[:, mi, n0:n1], t[:, 0:n], reads=[t.T()], writes=[hk])
            E("dve", "tensor_tensor", self.LO[:, mi, n0:n1], t[:, 0:n], self.HI[:, mi, n0:n1], ALU.subtract, reads=[t.T(), hk], writes=[lk])
        self.proj([(d["w_out"][l][:, m * 128:(m + 1) * 128], 8, 128) for m in range(8)],
                  lambda k: (MIX[:, k, :], MIX.T("all")), cons_out)
        fw.barrier()
        A.release(ml)
        self.HI.tiles = {}
        self.LO.tiles = {}
        XX = [A.alloc("ln1x%d" % i, [128, 8, 512], F32) for i in range(2)]

        def fill1(k, ni, n0, n1):
            X = XX[ni % 2]
            E("dve", "tensor_tensor", X[:, k, 0:n1 - n0], self.HI[:, k, n0:n1], self.LO[:, k, n0:n1], ALU.add,
              reads=[self.HI.T((k, ni)), self.LO.T((k, ni))], writes=[X.T(k)])
        nidx = {r: i for i, r in enumerate(NT)}
        self.layer_norm(lambda k, a, b: XX[nidx[(a, b)] % 2][:, k, 0:b - a], lambda k, ni_: XX[ni_ % 2].T(k),
                        Buf(self.LNP[:, 2 + 4 * l, :], "g1"), Buf(self.LNP[:, 3 + 4 * l, :], "b1"), NT, fill=fill1)
        fw.barrier()
        A.release(ml)
        self.HI.tiles = {}
        self.LO.tiles = {}
        self.dbg("h1_%d" % l, self.HI[:, :, :], [], [128, 8, L])
        RACC = A.alloc("racc", [128, 8, L], F32)
        mf = A.mark()
        ACTT = A.alloc("actt", [128, 8, L], BF16)
        SG = [A.alloc("sg%d" % i, [128, 512], F32) for i in range(2)]
        for k in range(8):
            E("dve", "tensor_tensor", RACC[:, k, :], self.HI[:, k, :], self.LO[:, k, :], ALU.add, reads=[], writes=[RACC.T("all")])
            E("dve", "tensor_scalar", RACC[:, k, :], RACC[:, k, :], ALPHA, None, ALU.mult, reads=[RACC.T("all")], writes=[RACC.T("all")])
        fw.barrier()
        groups = [list(range(g, min(g + 8, 22))) for g in range(0, 22, 8)]
        for grp in groups:
            for fi, f in enumerate(grp):
                wg, wgt = self.load_w(d["w_ffn_gate"][l][:, f * 128:(f + 1) * 128], 8, 128)
                wu, wut = self.load_w(d["w_ffn_up"][l][:, f * 128:(f + 1) * 128], 8, 128)
                for ni, (n0, n1) in enumerate(NT):
                    n = n1 - n0
                    pg, pgt = self.bank()
                    pu, put = self.bank()
                    for k in range(8):
                        E("pe", "matmul", pg[:, 0:n1 - n0], wg[:, k, :], self.HI[:, k, n0:n1], start=(k == 0), stop=(k == 7),
                          reads=[wgt, self.HI.T("all")], writes=[pgt])
                    for k in range(8):
                        E("pe", "matmul", pu[:, 0:n1 - n0], wu[:, k, :], self.HI[:, k, n0:n1], start=(k == 0), stop=(k == 7),
                          reads=[wut, self.HI.T("all")], writes=[put])
                    sg = SG[ni % 2]
                    E("act", "activation", sg[:, 0:n], pg[:, 0:n], AF.Silu, reads=[pgt], writes=[sg.T()])
                    E("dve", "tensor_tensor", ACTT[:, fi, n0:n1], pu[:, 0:n], sg[:, 0:n], ALU.mult,
                      reads=[put, sg.T()], writes=[ACTT.T("all")])
            ng = len(grp)
            f0 = grp[0]

            def cons_dn(mi, ni, pb, pt, n0, n1):
                n = n1 - n0
                E("dve", "tensor_tensor", RACC[:, mi, n0:n1], RACC[:, mi, n0:n1], pb[:, 0:n], ALU.add,
                  reads=[pt], writes=[RACC.T((mi, ni))])
            self.proj([(d["w_ffn_down"][l][f0 * 128:(f0 + ng) * 128, m * 128:(m + 1) * 128], ng, 128) for m in range(8)],
                      lambda k: (ACTT[:, k, :], ACTT.T("all")), cons_dn)
        fw.barrier()
        A.release(mf)
        self.HI.tiles = {}
        self.LO.tiles = {}
        self.layer_norm(lambda k, a, b: RACC[:, k, a:b], lambda k, ni: RACC.T((k, ni)),
                        Buf(self.LNP[:, 4 + 4 * l, :], "g2"), Buf(self.LNP[:, 5 + 4 * l, :], "b2"), NT)
        fw.barrier()
        self.HI.tiles = {}
        self.LO.tiles = {}
        self.dbg("h2_%d" % l, self.HI[:, :, :], [], [128, 8, L])
        A.release(ml)

    def rope_tables(self, s):
        fw, A, d, E = self.fw, self.A, self.d, self.E
        self.TCOS = A.alloc("tcos", [128, L], F32)
        self.TSIN = A.alloc("tsin", [128, L], F32)
        m0 = A.mark()
        PI = A.alloc("posi", [128, 2048], I32)
        ANG = A.alloc("ang", [128, L], F32)
        KK = A.alloc("kk", [128, L], F32)
        fw.dma("sp", PI[:], d["positions"][s:s + 1, :].partition_broadcast(128), writes=[PI.T()])
        fw.dma("sp", ANG[:, 0:16], d["c_metapos"], writes=[ANG.T()])
        E("dve", "tensor_copy", ANG[:, 16:L], PI[:], reads=[PI.T(), ANG.T()], writes=[ANG.T()])
        E("dve", "tensor_scalar", ANG[:, 16:L], ANG[:, 16:L], 16.0, None, ALU.add, reads=[ANG.T()], writes=[ANG.T()])
        E("dve", "tensor_scalar", ANG[:], ANG[:], self.CINV[:, 0:1], None, ALU.mult, reads=[ANG.T(), self.CINV.T()], writes=[ANG.T()])
        self.sincos(ANG, KK, self.TSIN, self.TCOS, L)
        E("dve", "tensor_scalar", self.TSIN[:], self.TSIN[:], self.CSGN[:, 0:1], None, ALU.mult,
          reads=[self.TSIN.T(), self.CSGN.T()], writes=[self.TSIN.T()])
        fw.barrier()
        A.release(m0)

    def mla(self, s, l, W, BRM):
        fw, A, d, E = self.fw, self.A, self.d, self.E
        m0 = A.mark()
        self.rope_tables(s)
        CN = A.alloc("cn", [128, 4, L], BF16)
        RAW = A.alloc("craw", [128, 2, L], F32)
        KRO = A.alloc("kro", [128, L], BF16)
        SQ = A.alloc("msq", [128, 512], F32)
        R = A.alloc("mr", [128, 512], F32)
        T1 = A.alloc("mt1", [128, 512], F32)
        T2 = A.alloc("mt2", [128, 512], F32)
        for which in range(2):
            def cons(mi, ni, pb, pt, n0, n1):
                E("act", "copy", RAW[:, mi, n0:n1], pb[:, 0:n1 - n0], reads=[pt], writes=[RAW.T((mi, ni))])
            self.proj([(W[:, which * 256 + m * 128: which * 256 + (m + 1) * 128], 8, 128) for m in range(2)], self.act_hi, cons)
            for ni, (n0, n1) in enumerate(NT):
                n = n1 - n0
                pb, pt = self.bank()
                for m in range(2):
                    E("act", "activation", SQ[:, 0:n], RAW[:, m, n0:n1], AF.Square, reads=[RAW.T((m, ni))], writes=[SQ.T()])
                    E("pe", "matmul", pb[:, 0:n], self.ONESF[:, 0:128], SQ[:, 0:n], start=(m == 0), stop=(m == 1),
                      reads=[SQ.T(), self.ONESF.T()], writes=[pt])
                E("act", "activation", R[:, 0:n], pb[:, 0:n], AF.Sqrt, bias=self.EPS6[:, 0:1], scale=1.0 / 256,
                  reads=[pt, self.EPS6.T()], writes=[R.T()])
                E("dve", "reciprocal", R[:, 0:n], R[:, 0:n], reads=[R.T()], writes=[R.T()])
                for m in range(2):
                    E("dve", "scalar_tensor_tensor", CN[:, 2 * which + m, n0:n1], RAW[:, m, n0:n1], self.QKN[:, l, which, m:m + 1], R[:, 0:n], ALU.mult, ALU.mult,
                      reads=[RAW.T((m, ni)), R.T(), self.QKN.T()], writes=[CN.T("all")])
        wa, wat = self.load_w(d["w_kr"][l], 8, 96)
        wb_, wbt = self.load_w(d["w_kr_sw"][l], 8, 96)
        for ni, (n0, n1) in enumerate(NT):
            n = n1 - n0
            pa, pat = self.bank()
            pb, pbt = self.bank()
            for k in range(8):
                E("pe", "matmul", pa[0:96, 0:n], wa[:, k, :], self.HI[:, k, n0:n1], start=(k == 0), stop=(k == 7), reads=[wat, self.HI.T("all")], writes=[pat])
            for k in range(8):
                E("pe", "matmul", pb[0:96, 0:n], wb_[:, k, :], self.HI[:, k, n0:n1], start=(k == 0), stop=(k == 7), reads=[wbt, self.HI.T("all")], writes=[pbt])
            E("dve", "tensor_tensor", T1[64:96, 0:n], pa[64:96, 0:n], self.TCOS[64:96, n0:n1], ALU.mult, reads=[pat, self.TCOS.T()], writes=[T1.T()])
            E("dve", "tensor_tensor", T2[64:96, 0:n], pb[64:96, 0:n], self.TSIN[64:96, n0:n1], ALU.mult, reads=[pbt, self.TSIN.T()], writes=[T2.T()])
            E("dve", "tensor_tensor", KRO[64:96, n0:n1], T1[64:96, 0:n], T2[64:96, 0:n], ALU.add, reads=[T1.T(), T2.T()], writes=[KRO.T()])
        QT = [A.alloc("qt%d" % i, [128, L], BF16) for i in range(2)]
        KT = [A.alloc("kt%d" % i, [128, L], BF16) for i in range(2)]
        VA = [A.alloc("va%d" % i, [128, 17, 128], BF16) for i in range(2)]
        PTB = [A.alloc("ptb%d" % i, [128, 512], BF16) for i in range(3)]
        RC = A.alloc("rc", [128, 512], F32)
        E("pool", "memset", VA[0][:, :, 64:128], 1.0, writes=[VA[0].T()])
        E("pool", "memset", VA[1][:, :, 0:64], 1.0, writes=[VA[1].T()])
        scale = 96.0 ** -0.5
        pti = 0
        poi = 0
        self.bank_list = [0, 1, 2, 3, 4, 5]
        self.bank_i = 0
        RC2 = [RC, A.alloc("rc2", [128, 512], F32)]
        st = {"pti": 0, "poi": 0}

        def proj_head(h):
            par = h % 2
            qt, kt, va = QT[par], KT[par], VA[par]
            vo = 0 if par == 0 else 64
            wq, wqt = self.load_w(d["w_uq"][l][:, h * 96:(h + 1) * 96], 2, 96)
            ws, wst = self.load_w(d["w_uq_sw"][l][:, h * 96:(h + 1) * 96], 2, 96)
            wk, wkt = self.load_w(d["w_ukv"][l][:, h * 128:h * 128 + 64], 2, 64)
            wv, wvt = self.load_w(d["w_ukv"][l][:, h * 128 + 64:h * 128 + 128], 2, 64)
            for ni, (n0, n1) in enumerate(NT):
                n = n1 - n0
                pa, pat = self.bank()
                pb, pbt = self.bank()
                pc, pct = self.bank()
                for k in range(2):
                    E("pe", "matmul", pa[0:96, 0:n], wq[:, k, :], CN[:, k, n0:n1], start=(k == 0), stop=(k == 1), reads=[wqt, CN.T("all")], writes=[pat])
                for k in range(2):
                    E("pe", "matmul", pb[0:96, 0:n], ws[:, k, :], CN[:, k, n0:n1], start=(k == 0), stop=(k == 1), reads=[wst, CN.T("all")], writes=[pbt])
                for k in range(2):
                    E("pe", "matmul", pc[0:64, 0:n], wk[:, k, :], CN[:, 2 + k, n0:n1], start=(k == 0), stop=(k == 1), reads=[wkt, CN.T("all")], writes=[pct])
                E("act", "copy", qt[0:64, n0:n1], pa[0:64, 0:n], reads=[pat], writes=[qt.T()])
                E("dve", "tensor_tensor", T1[64:96, 0:n], pa[64:96, 0:n], self.TCOS[64:96, n0:n1], ALU.mult, reads=[pat, self.TCOS.T()], writes=[T1.T()])
                E("dve", "tensor_tensor", T2[64:96, 0:n], pb[64:96, 0:n], self.TSIN[64:96, n0:n1], ALU.mult, reads=[pbt, self.TSIN.T()], writes=[T2.T()])
                E("dve", "tensor_tensor", qt[64:96, n0:n1], T1[64:96, 0:n], T2[64:96, 0:n], ALU.add, reads=[T1.T(), T2.T()], writes=[qt.T()])
                E("act", "copy", kt[0:64, n0:n1], pc[0:64, 0:n], reads=[pct], writes=[kt.T()])
            E("pool", "tensor_copy", kt[64:96, :], KRO[64:96, :], reads=[KRO.T()], writes=[kt.T()])
            for ti, (c0, sz) in enumerate(TOKT):
                pv, pvt = self.bank()
                for k in range(2):
                    E("pe", "matmul", pv[0:sz, 0:64], CN[:, 2 + k, c0:c0 + sz], wv[:, k, :], start=(k == 0), stop=(k == 1), reads=[wvt, CN.T("all")], writes=[pvt])
                E("act", "copy", va[0:sz, ti, vo:vo + 64], pv[0:sz, 0:64], reads=[pvt], writes=[va.T()])

        def attn_head(h):
            par = h % 2
            qt, kt, va = QT[par], KT[par], VA[par]
            items = []
            for ni, (q0, q1) in enumerate(NT):
                keys = [(ti, c0, sz) for ti, (c0, sz) in enumerate(TOKT) if c0 < q1]
                pbk = 6 + (st["poi"] % 2)
                st["poi"] += 1
                for ji, (ti, c0, sz) in enumerate(keys):
                    items.append(dict(q0=q0, q1=q1, ti=ti, c0=c0, sz=sz, first=(ji == 0), last=(ji == len(keys) - 1), pbk=pbk))

            def emit_S(it):
                qa = max(it["q0"], it["c0"])
                it["qa"] = qa
                it["nq"] = it["q1"] - qa
                it["ps"], it["pst"] = self.bank()
                E("pe", "matmul", it["ps"][0:it["sz"], 0:it["nq"]], kt[0:96, it["c0"]:it["c0"] + it["sz"]], qt[0:96, qa:it["q1"]], start=True, stop=True,
                  reads=[kt.T(), qt.T()], writes=[it["pst"]])
            for i in range(min(2, len(items))):
                emit_S(items[i])
            for i, it in enumerate(items):
                sz, nq, qa, q0, q1 = it["sz"], it["nq"], it["qa"], it["q0"], it["q1"]
                ptb = PTB[st["pti"] % 3]
                st["pti"] += 1
                po, pot = self.P[it["pbk"]], self.PT[it["pbk"]]
                E("act", "activation", ptb[0:sz, 0:nq], it["ps"][0:sz, 0:nq], AF.Exp, scale=scale, reads=[it["pst"]], writes=[ptb.T()])
                if it["c0"] >= q0:
                    E("dve", "tensor_tensor", ptb[0:sz, 0:sz], ptb[0:sz, 0:sz], self.TRI[0:sz, 0:sz], ALU.mult,
                      reads=[ptb.T(), self.TRI.T()], writes=[ptb.T()])
                if i + 2 < len(items):
                    emit_S(items[i + 2])
                E("pe", "matmul", po[:, qa - q0:q1 - q0], va[0:sz, it["ti"], :], ptb[0:sz, 0:nq], start=it["first"], stop=it["last"],
                  reads=[va.T(), ptb.T()], writes=[pot])
                if it["last"]:
                    n = q1 - q0
                    rc = RC2[it["pbk"] % 2]
                    if par == 0:
                        E("dve", "reciprocal", rc[0:64, 0:n], po[64:128, 0:n], reads=[pot], writes=[rc.T()])
                        E("dve", "tensor_tensor", BRM[0:64, h // 2, q0:q1], po[0:64, 0:n], rc[0:64, 0:n], ALU.mult, reads=[pot, rc.T()], writes=[BRM.T("all")])
                    else:
                        E("dve", "reciprocal", rc[64:128, 0:n], po[0:64, 0:n], reads=[pot], writes=[rc.T()])
                        E("dve", "tensor_tensor", BRM[64:128, h // 2, q0:q1], po[64:128, 0:n], rc[64:128, 0:n], ALU.mult, reads=[pot, rc.T()], writes=[BRM.T("all")])

        proj_head(0)
        for h in range(8):
            if h + 1 < 8:
                proj_head(h + 1)
            attn_head(h)
        self.bank_list = [0, 1, 2, 3, 4, 5, 6, 7]
        self.bank_i = 0
        fw.barrier()
        A.release(m0)

    def exp_small(self, dst, x, tmp, F, halv=6):
        E = self.E
        E("dve", "tensor_scalar", tmp[:, 0:F], x[:, 0:F], 1.0 / (2 ** halv), None, ALU.mult, reads=[x.T()], writes=[tmp.T()])
        E("dve", "tensor_scalar", dst[:, 0:F], tmp[:, 0:F], 1.0 / 6, 1.0, ALU.mult, ALU.add, reads=[tmp.T()], writes=[dst.T()])
        for kdiv in (5.0, 4.0, 3.0, 2.0, 1.0):
            E("dve", "tensor_tensor", dst[:, 0:F], dst[:, 0:F], tmp[:, 0:F], ALU.mult, reads=[dst.T(), tmp.T()], writes=[dst.T()])
            E("dve", "tensor_scalar", dst[:, 0:F], dst[:, 0:F], 1.0 / kdiv, 1.0, ALU.mult, ALU.add, reads=[dst.T()], writes=[dst.T()])
        for _ in range(halv):
            E("dve", "tensor_tensor", dst[:, 0:F], dst[:, 0:F], dst[:, 0:F], ALU.mult, reads=[dst.T()], writes=[dst.T()])

    def s5_params(self, l, sfx, F):
        fw, A, d, E = self.fw, self.A, self.d, self.E
        t = {}
        for n in ("lr", "li", "dt", "rho", "th", "cr", "ci", "t1", "t2", "t3", "sn", "cs"):
            t[n] = A.alloc("s5p_" + n, [128, F], F32)
        fw.dma("sp", t["lr"][:], d["lamre_" + sfx][l], writes=[t["lr"].T()])
        fw.dma("sp", t["li"][:], d["lamim_" + sfx][l], writes=[t["li"].T()])
        fw.dma("sp", t["t1"][:], d["logdt_" + sfx][l], writes=[t["t1"].T()])
        E("dve", "tensor_scalar_min", t["lr"][:], t["lr"][:], -1e-4, reads=[t["lr"].T()], writes=[t["lr"].T()])
        self.exp_small(t["dt"], t["t1"], t["t2"], F)
        E("dve", "tensor_tensor", t["t1"][:], t["lr"][:], t["dt"][:], ALU.mult, reads=[t["lr"].T(), t["dt"].T()], writes=[t["t1"].T()])
        self.exp_small(t["rho"], t["t1"], t["t2"], F, halv=0)
        E("dve", "tensor_tensor", t["th"][:], t["li"][:], t["dt"][:], ALU.mult, reads=[t["li"].T(), t["dt"].T()], writes=[t["th"].T()])
        E("dve", "tensor_copy", t["t1"][:], t["th"][:], reads=[t["th"].T()], writes=[t["t1"].T()])
        self.sincos(t["t1"], t["t2"], t["sn"], t["cs"], F)
        E("dve", "tensor_tensor", t["cs"][:], t["cs"][:], t["rho"][:], ALU.mult, reads=[t["cs"].T(), t["rho"].T()], writes=[t["cs"].T()])
        E("dve", "tensor_tensor", t["sn"][:], t["sn"][:], t["rho"][:], ALU.mult, reads=[t["sn"].T(), t["rho"].T()], writes=[t["sn"].T()])
        E("dve", "tensor_tensor", t["t1"][:], t["lr"][:], t["lr"][:], ALU.mult, reads=[t["lr"].T()], writes=[t["t1"].T()])
        E("dve", "tensor_tensor", t["t2"][:], t["li"][:], t["li"][:], ALU.mult, reads=[t["li"].T()], writes=[t["t2"].T()])
        E("dve", "tensor_tensor", t["t1"][:], t["t1"][:], t["t2"][:], ALU.add, reads=[t["t1"].T(), t["t2"].T()], writes=[t["t1"].T()])
        E("dve", "reciprocal", t["t1"][:], t["t1"][:], reads=[t["t1"].T()], writes=[t["t1"].T()])
        E("dve", "tensor_scalar", t["t2"][:], t["cs"][:], -1.0, None, ALU.add, reads=[t["cs"].T()], writes=[t["t2"].T()])
        E("dve", "tensor_tensor", t["cr"][:], t["t2"][:], t["lr"][:], ALU.mult, reads=[t["t2"].T(), t["lr"].T()], writes=[t["cr"].T()])
        E("dve", "tensor_tensor", t["t3"][:], t["sn"][:], t["li"][:], ALU.mult, reads=[t["sn"].T(), t["li"].T()], writes=[t["t3"].T()])
        E("dve", "tensor_tensor", t["cr"][:], t["cr"][:], t["t3"][:], ALU.add, reads=[t["cr"].T(), t["t3"].T()], writes=[t["cr"].T()])
        E("dve", "tensor_tensor", t["cr"][:], t["cr"][:], t["t1"][:], ALU.mult, reads=[t["cr"].T(), t["t1"].T()], writes=[t["cr"].T()])
        E("dve", "tensor_tensor", t["ci"][:], t["sn"][:], t["lr"][:], ALU.mult, reads=[t["sn"].T(), t["lr"].T()], writes=[t["ci"].T()])
        E("dve", "tensor_tensor", t["t3"][:], t["t2"][:], t["li"][:], ALU.mult, reads=[t["t2"].T(), t["li"].T()], writes=[t["t3"].T()])
        E("dve", "tensor_tensor", t["ci"][:], t["ci"][:], t["t3"][:], ALU.subtract, reads=[t["ci"].T(), t["t3"].T()], writes=[t["ci"].T()])
        E("dve", "tensor_tensor", t["ci"][:], t["ci"][:], t["t1"][:], ALU.mult, reads=[t["ci"].T(), t["t1"].T()], writes=[t["ci"].T()])
        return t

    def s5(self, s, l, W, BRS):
        fw, A, d, E = self.fw, self.A, self.d, self.E
        m0 = A.mark()
        def cons_u(mi, ni, pb, pt, n0, n1):
            E("act", "copy", BRS[:, mi, n0:n1], pb[:, 0:n1 - n0], reads=[pt], writes=[BRS.T("all")])
        self.proj([(W[:, 544 + m * 128:544 + (m + 1) * 128], 8, 128) for m in range(4)], self.act_hi, cons_u)
        BBR = A.alloc("bbr", [128, 16, 128], BF16)
        BBI = A.alloc("bbi", [128, 16, 128], BF16)
        CRE = A.alloc("cre", [128, 16, 128], BF16)
        CNR = A.alloc("cnr", [128, 16, 128], BF16)
        CNI = A.alloc("cni", [128, 16, 128], BF16)
        WG = A.alloc("wglu", [128, 4, 512], BF16)
        DSK = A.alloc("dsk", [128, 4], F32)
        RHO = A.alloc("rho", [128, 16], F32)
        THI = A.alloc("thi", [128, 16], F32)
        TLO = A.alloc("tlo", [128, 16], F32)
        fw.dma("pool", CRE[:].rearrange("p j c -> p (j c)"), d["cre_pad"][l], writes=[CRE.T()])
        fw.dma("pool", CNI[:].rearrange("p j c -> p (j c)"), d["cim_pad"][l], writes=[CNI.T()])
        fw.dma("pool", WG[:], d["w_glu"][l].rearrange("(k p) c -> p k c", p=128), writes=[WG.T()])
        fw.dma("sp", DSK[:], d["s5_d"][l], writes=[DSK.T()])
        E("dve", "tensor_scalar", CNR[:], CRE[:], -1.0, None, ALU.mult, reads=[CRE.T()], writes=[CNR.T()])
        E("dve", "tensor_scalar", CNI[:], CNI[:], -1.0, None, ALU.mult, reads=[CNI.T()], writes=[CNI.T()])
        m1 = A.mark()
        pc = self.s5_params(l, "c", 256)
        BRE = A.alloc("bre", [128, 256], F32)
        BIM = A.alloc("bim", [128, 256], F32)
        BT1 = A.alloc("bt1", [128, 256], F32)
        BT2 = A.alloc("bt2", [128, 256], F32)
        BMASK = A.alloc("bmask", [128, 32], F32)
        fw.dma("sp", BRE[:], d["bre_c"][l], writes=[BRE.T()])
        fw.dma("sp", BIM[:], d["bim_c"][l], writes=[BIM.T()])
        fw.dma("sp", BMASK[:], d["c_bmask"], writes=[BMASK.T()])
        E("dve", "tensor_tensor", BT1[:], pc["cr"][:], BRE[:], ALU.mult, reads=[pc["cr"].T(), BRE.T()], writes=[BT1.T()])
        E("dve", "tensor_tensor", BT2[:], pc["ci"][:], BIM[:], ALU.mult, reads=[pc["ci"].T(), BIM.T()], writes=[BT2.T()])
        E("dve", "tensor_tensor", BT1[:], BT1[:], BT2[:], ALU.subtract, reads=[BT1.T(), BT2.T()], writes=[BT1.T()])
        E("dve", "tensor_tensor", BT2[:], pc["cr"][:], BIM[:], ALU.mult, reads=[pc["cr"].T(), BIM.T(), BT2.T()], writes=[BT2.T()])
        E("dve", "tensor_tensor", BRE[:], pc["ci"][:], BRE[:], ALU.mult, reads=[pc["ci"].T(), BRE.T()], writes=[BRE.T()])
        E("dve", "tensor_tensor", BT2[:], BT2[:], BRE[:], ALU.add, reads=[BT2.T(), BRE.T()], writes=[BT2.T()])
        for j in range(16):
            jc = j // 4
            for gl in range(2):
                mcol = BMASK[:, 2 * j + gl:2 * j + gl + 1]
                E("dve", "tensor_scalar", BBR[:, j, gl * 64:(gl + 1) * 64], BT1[:, jc * 64:(jc + 1) * 64], mcol, None, ALU.mult,
                  reads=[BT1.T(), BMASK.T()], writes=[BBR.T()])
                E("dve", "tensor_scalar", BBI[:, j, gl * 64:(gl + 1) * 64], BT2[:, jc * 64:(jc + 1) * 64], mcol, None, ALU.mult,
                  reads=[BT2.T(), BMASK.T()], writes=[BBI.T()])
        fw.barrier()
        A.release(m1)
        ps_ = self.s5_params(l, "s", 16)
        E("dve", "tensor_copy", RHO[:], ps_["rho"][:], reads=[ps_["rho"].T()], writes=[RHO.T()])
        E("dve", "tensor_single_scalar", THI[:].bitcast(I32), ps_["th"][:].bitcast(I32), -4096, ALU.bitwise_and, reads=[ps_["th"].T()], writes=[THI.T()])
        E("dve", "tensor_tensor", TLO[:], ps_["th"][:], THI[:], ALU.subtract, reads=[ps_["th"].T(), THI.T()], writes=[TLO.T()])
        fw.barrier()
        A.release(m1)
        self.dbg("s5rho%d" % l, RHO[:], [], [128, 16])
        self.dbg("s5bbr%d" % l, BBR[:, :, :], [], [128, 16, 128])
        TAU = A.alloc("tau", [128, TC + 1], F32)
        fw.dma("sp", TAU[:], d["c_tau"], writes=[TAU.T()])
        COS = A.alloc("s5cos", [128, TC + 1], F32)
        SIN = A.alloc("s5sin", [128, TC + 1], F32)
        ANG = A.alloc("s5ang", [128, TC + 1], F32)
        KK = A.alloc("s5kk", [128, TC + 1], F32)
        RHOB = A.alloc("rhob", [128, TC], F32)
        W1 = [A.alloc("s5w%d" % i, [128, TC], F32) for i in range(6)]
        WRI = [[A.alloc("s5wr%d_%d" % (b_, i), [128, TC], F32) for i in range(2)] for b_ in range(2)]
        PBB = [[A.alloc("s5p%d_%d" % (b_, i), [128, TC], BF16) for i in range(4)] for b_ in range(2)]
        BU = [[A.alloc("s5bu%d_%d" % (b_, i), [128, TC], F32) for i in range(2)] for b_ in range(2)]
        cnt = [0]
        cntb = [0]
        COSB = A.alloc("s5cosb", [128, TC], BF16)
        SINB = A.alloc("s5sinb", [128, TC], BF16)
        WRB = [[A.alloc("s5wrb%d_%d" % (b_, i), [128, TC], BF16) for i in range(2)] for b_ in range(2)]
        INI = A.alloc("s5ini", [128, 4], F32)
        INI2 = Buf(INI[:, 2:4], "ini2")
        CS2 = A.alloc("s5cs2", [128, 2], F32)
        NSC = A.alloc("s5nsc", [128, 2], F32)
        YF = A.alloc("s5y", [128, TC], F32)
        YT1 = A.alloc("s5yt1", [128, TC], F32)
        n = TC
        for jc in range(4):
            for j in range(4 * jc, 4 * jc + 4):
                E("dve", "tensor_scalar", ANG[:], TAU[:], THI[:, j:j + 1], None, ALU.mult, reads=[TAU.T(), THI.T(), ANG.T()], writes=[ANG.T()])
                E("dve", "tensor_scalar", KK[:], ANG[:], 1.0 / TWO_PI, MAGIC, ALU.mult, ALU.add, reads=[ANG.T(), KK.T()], writes=[KK.T()])
                E("dve", "tensor_scalar", KK[:], KK[:], -MAGIC, None, ALU.add, reads=[KK.T()], writes=[KK.T()])
                E("dve", "scalar_tensor_tensor", ANG[:], KK[:], -C1, ANG[:], ALU.mult, ALU.add, reads=[KK.T(), ANG.T()], writes=[ANG.T()])
                E("dve", "scalar_tensor_tensor", ANG[:], KK[:], -C2, ANG[:], ALU.mult, ALU.add, reads=[KK.T(), ANG.T()], writes=[ANG.T()])
                E("dve", "scalar_tensor_tensor", ANG[:], TAU[:], TLO[:, j:j + 1], ANG[:], ALU.mult, ALU.add, reads=[TAU.T(), TLO.T(), ANG.T()], writes=[ANG.T()])
                E("dve", "tensor_scalar", ANG[:], ANG[:], -math.pi, math.pi, ALU.max, ALU.min, reads=[ANG.T()], writes=[ANG.T()])
                E("act", "activation", SIN[:], ANG[:], AF.Sin, reads=[ANG.T()], writes=[SIN.T()])
                E("act", "activation", KK[:], ANG[:], AF.Abs, reads=[ANG.T()], writes=[KK.T()])
                E("act", "activation", COS[:], KK[:], AF.Sin, bias=self.HALFPI[:, 0:1], scale=-1.0, reads=[KK.T(), self.HALFPI.T()], writes=[COS.T()])
                E("dve", "tensor_scalar", RHOB[:], self.ONESF[:, 0:TC], RHO[:, j:j + 1], None, ALU.mult, reads=[RHO.T(), self.ONESF.T()], writes=[RHOB.T()])
                E("dve", "memset", INI[:], 0.0, writes=[INI.T()])
                E("dve", "tensor_copy", CS2[:, 0:1], COS[:, TC:TC + 1], reads=[COS.T(), CS2.T()], writes=[CS2.T()])
                E("dve", "tensor_copy", CS2[:, 1:2], SIN[:, TC:TC + 1], reads=[SIN.T(), CS2.T()], writes=[CS2.T()])
                E("dve", "tensor_scalar", NSC[:, 0:1], SIN[:, TC:TC + 1], -1.0, None, ALU.mult, reads=[SIN.T(), NSC.T()], writes=[NSC.T()])
                E("dve", "tensor_copy", NSC[:, 1:2], COS[:, TC:TC + 1], reads=[COS.T(), NSC.T()], writes=[NSC.T()])
                def emitB(cc, j=j, jc=jc):
                    cs_, ce_ = S5CH[cc]
                    pr, prt = self.P[6], self.PT[6]
                    pi_, pit = self.P[7], self.PT[7]
                    E("pe", "matmul", pr[:, 0:n], BBR[:, j, :], BRS[:, jc, cs_:ce_], start=True, stop=True, reads=[BBR.T(), BRS.T((jc, cc)), BRS.T("all")], writes=[prt])
                    E("pe", "matmul", pi_[:, 0:n], BBI[:, j, :], BRS[:, jc, cs_:ce_], start=True, stop=True, reads=[BBI.T(), BRS.T((jc, cc)), BRS.T("all")], writes=[pit])
                    bur_, bui_ = BU[(cntb[0]) % 2]
                    cntb[0] += 1
                    E("act", "copy", bur_[:], pr[:, 0:n], reads=[prt], writes=[bur_.T()])
                    E("act", "copy", bui_[:], pi_[:, 0:n], reads=[pit], writes=[bui_.T()])
                emitB(0)
                E("act", "copy", COSB[:], COS[:, 0:n], reads=[COS.T(), COSB.T()], writes=[COSB.T()])
                E("act", "copy", SINB[:], SIN[:, 0:n], reads=[SIN.T(), SINB.T()], writes=[SINB.T()])

                def front(c, j=j):
                    t1, t2, t3, t4, rr, ri = W1
                    k_ = cnt[0]
                    cnt[0] += 1
                    wr, wi = WRI[k_ % 2]
                    wrb, wib = WRB[k_ % 2]
                    bur, bui = BU[k_ % 2]
                    E("dve", "tensor_tensor", t1[:], bur[:], COS[:, 0:n], ALU.mult, reads=[bur.T(), COS.T()], writes=[t1.T()])
                    E("dve", "tensor_tensor", t2[:], bui[:], SIN[:, 0:n], ALU.mult, reads=[bui.T(), SIN.T()], writes=[t2.T()])
                    E("dve", "tensor_tensor", t3[:], bui[:], COS[:, 0:n], ALU.mult, reads=[bui.T(), COS.T()], writes=[t3.T()])
                    E("dve", "tensor_tensor", t4[:], bur[:], SIN[:, 0:n], ALU.mult, reads=[bur.T(), SIN.T()], writes=[t4.T()])
                    E("dve", "tensor_tensor", rr[:], t1[:], t2[:], ALU.add, reads=[t1.T(), t2.T()], writes=[rr.T()])
                    E("dve", "tensor_tensor", ri[:], t3[:], t4[:], ALU.subtract, reads=[t3.T(), t4.T()], writes=[ri.T()])
                    E("dve", "tensor_tensor_scan", wr[:], RHOB[:], rr[:], INI[:, 0:1], ALU.mult, ALU.add, reads=[RHOB.T(), rr.T(), INI.T()], writes=[wr.T()])
                    E("dve", "tensor_tensor_scan", wi[:], RHOB[:], ri[:], INI[:, 1:2], ALU.mult, ALU.add, reads=[RHOB.T(), ri.T(), INI.T()], writes=[wi.T()])
                    E("act", "activation", INI[:, 2:4], NSC[:, 0:2], AF.Copy, scale=wi[:, n - 1:n], reads=[wi.T(), NSC.T()], writes=[INI2.T()])
                    E("act", "activation", INI[:, 0:1], CS2[:, 0:1], AF.Identity, bias=INI[:, 2:3], scale=wr[:, n - 1:n], reads=[wr.T(), CS2.T(), INI2.T()], writes=[INI.T()])
                    E("act", "activation", INI[:, 1:2], CS2[:, 1:2], AF.Identity, bias=INI[:, 3:4], scale=wr[:, n - 1:n], reads=[wr.T(), CS2.T(), INI2.T()], writes=[INI.T()])
                    E("act", "copy", wrb[:], wr[:], reads=[wr.T(), wrb.T()], writes=[wrb.T()])
                    E("act", "copy", wib[:], wi[:], reads=[wi.T(), wib.T()], writes=[wib.T()])
                    return k_

                def back(c, k_, j=j, jc=jc):
                    wrb, wib = WRB[k_ % 2]
                    PB_ = PBB[k_ % 2]
                    E("dve", "tensor_tensor", PB_[0][:], COSB[:], wrb[:], ALU.mult, reads=[COSB.T(), wrb.T()], writes=[PB_[0].T()])
                    E("dve", "tensor_tensor", PB_[1][:], SINB[:], wib[:], ALU.mult, reads=[SINB.T(), wib.T()], writes=[PB_[1].T()])
                    E("dve", "tensor_tensor", PB_[2][:], SINB[:], wrb[:], ALU.mult, reads=[SINB.T(), wrb.T()], writes=[PB_[2].T()])
                    E("dve", "tensor_tensor", PB_[3][:], COSB[:], wib[:], ALU.mult, reads=[COSB.T(), wib.T()], writes=[PB_[3].T()])
                    py, pyt = self.P[c], self.PT[c]
                    first = (j == 4 * jc)
                    last = (j == 4 * jc + 3)
                    for q, (cm, pbuf) in enumerate(((CRE, PB_[0]), (CNR, PB_[1]), (CNI, PB_[2]), (CNI, PB_[3]))):
                        E("pe", "matmul", py[:, 0:n], cm[:, j, :], pbuf[:], start=(first and q == 0), stop=(last and q == 3),
                          reads=[cm.T(), pbuf.T()], writes=[pyt])
                prev = None
                for c, (cs, ce) in enumerate(S5CH):
                    if c + 1 < len(S5CH):
                        emitB(c + 1)
                    k_ = front(c)
                    if prev is not None:
                        back(*prev)
                    prev = (c, k_)
                back(*prev)
            for c, (cs, ce) in enumerate(S5CH):
                py, pyt = self.P[c], self.PT[c]
                E("dve", "scalar_tensor_tensor", YF[:], BRS[:, jc, cs:ce], DSK[:, jc:jc + 1], py[:, 0:n], ALU.mult, ALU.add,
                  reads=[pyt, BRS.T((jc, c)), BRS.T("all"), DSK.T()], writes=[YF.T()])
                E("act", "activation", YT1[:], YF[:], AF.Square, reads=[YF.T()], writes=[YT1.T()])
                E("dve", "tensor_scalar", YT1[:], YT1[:], 0.044715, 1.0, ALU.mult, ALU.add, reads=[YT1.T()], writes=[YT1.T()])
                E("dve", "tensor_tensor", YT1[:], YT1[:], YF[:], ALU.mult, reads=[YT1.T(), YF.T()], writes=[YT1.T()])
                E("act", "activation", YT1[:], YT1[:], AF.Sigmoid, scale=GELU_K, reads=[YT1.T()], writes=[YT1.T()])
                E("dve", "tensor_tensor", BRS[:, jc, cs:ce], YF[:], YT1[:], ALU.mult, reads=[YT1.T(), YF.T()], writes=[BRS.T((jc, c)), BRS.T("all")])
        fw.barrier()
        BRS.tiles = {}
        self.dbg("s5yg%d" % l, BRS[:, :, :], [], [128, 4, L])
        SGT = A.alloc("s5sg", [128, 4, 512], BF16)
        for ni, (n0, n1) in enumerate(NT):
            nn = n1 - n0
            for m in range(4):
                pb, pt = self.bank()
                for k in range(4):
                    E("pe", "matmul", pb[:, 0:nn], WG[:, k, m * 128:(m + 1) * 128], BRS[:, k, n0:n1], start=(k == 0), stop=(k == 3),
                      reads=[WG.T(), BRS.T("all")], writes=[pt])
                E("act", "activation", SGT[:, m, 0:nn], pb[:, 0:nn], AF.Sigmoid, reads=[pt], writes=[SGT.T()])
            for m in range(4):
                E("dve", "tensor_tensor", BRS[:, m, n0:n1], BRS[:, m, n0:n1], SGT[:, m, 0:nn], ALU.mult, reads=[SGT.T(), BRS.T("all")], writes=[BRS.T("all")])
        fw.barrier()
        BRS.tiles = {}
        A.release(m0)

    def hgrn(self, s, l, W, BRH):
        fw, A, d, E = self.fw, self.A, self.d, self.E
        m0 = A.mark()
        FK = A.alloc("hgFK", [128, 2 * L], F32)
        Fb = Buf(FK[:, 0:L], "hgF")
        Kb = Buf(FK[:, L:2 * L], "hgK")
        XS = Buf(FK[:, 0:4096].rearrange("p (i v) -> p i v", v=128), "hgXS")
        Fb.tiles = FK.tiles
        Kb.tiles = FK.tiles
        XS.tiles = FK.tiles
        CUM = A.alloc("hgC", [128, L], F32)
        SBA = Buf(CUM[:, 0:2048].bitcast(BF16).rearrange("p (i v) -> p i v", v=128), "hgSBA")
        SBA.tiles = CUM.tiles
        Eb = A.alloc("hgE", [128, L], F32)
        QT = A.alloc("hgq", [128, L], BF16)
        KT = A.alloc("hgk", [128, L], BF16)
        SGt = A.alloc("hgsg", [128, L], BF16)
        VT = A.alloc("hgv", [64, 33, 128], BF16)
        REF = A.alloc("hgref", [128, 33], F32)
        DD = A.alloc("hgdd", [128, 32], F32)
        ATM = [A.alloc("hgatm%d" % i, [64, 64], BF16) for i in range(3)]
        KTOK = [A.alloc("hgktok%d" % i, [64, 128], BF16) for i in range(3)]
        ON = [A.alloc("hgon%d" % i, [64, 128], BF16) for i in range(3)]
        JUNK = A.alloc("hgjunk", [64, 128], F32)
        SSL = [A.alloc("hgss%d" % i, [64, 2], F32) for i in range(4)]
        base = 1056
        for h in range(4):
            lb = self.LBT[:, l, 0, h:h + 1]
            oml = self.LBT[:, l, 1, h:h + 1]
            def cons_zf(mi, ni, pb, pt, n0, n1):
                E("act", "activation", Fb[:, n0:n1], pb[:, 0:n1 - n0], AF.Sigmoid, reads=[pt], writes=[Fb.T()])
            self.proj([(W[:, base + 512 + h * 128: base + 512 + (h + 1) * 128], 8, 128)], self.act_hi, cons_zf)
            def cons_g(mi, ni, pb, pt, n0, n1):
                E("act", "activation", SGt[:, n0:n1], pb[:, 0:n1 - n0], AF.Silu, reads=[pt], writes=[SGt.T()])
            self.proj([(W[:, base + 1536 + h * 128: base + 1536 + (h + 1) * 128], 8, 128)], self.act_hi, cons_g)
            wv, wvt = self.load_w(W[:, base + 1024 + h * 128: base + 1024 + (h + 1) * 128], 8, 128)
            for i, (c0, sz) in enumerate(HCH):
                pv, pvt = self.bank()
                for k in range(8):
                    E("pe", "matmul", pv[0:sz, 0:128], self.HI[:, k, c0:c0 + sz], wv[:, k, :], start=(k == 0), stop=(k == 7), reads=[wvt, self.HI.T("all")], writes=[pvt])
                E("act", "copy", VT[0:sz, i, :], pv[0:sz, 0:128], reads=[pvt], writes=[VT.T()])
            E("dve", "tensor_scalar", Fb[:], Fb[:], oml, lb, ALU.mult, ALU.add, reads=[Fb.T(), self.LBT.T()], writes=[Fb.T()])
            E("dve", "tensor_scalar", Kb[:], Fb[:], -1.0, 1.0, ALU.mult, ALU.add, reads=[Fb.T(), Kb.T()], writes=[Kb.T()])
            E("dve", "tensor_scalar_max", Fb[:], Fb[:], 1e-6, reads=[Fb.T()], writes=[Fb.T()])
            E("act", "activation", Fb[:], Fb[:], AF.Ln, reads=[Fb.T()], writes=[Fb.T()])
            E("dve", "tensor_tensor_scan", CUM[:], self.ONESF[:, 0:L], Fb[:], 0.0, ALU.mult, ALU.add, reads=[Fb.T(), self.ONESF.T(), CUM.T()], writes=[CUM.T()])
            E("dve", "tensor_copy", REF[:, 0:1], CUM[:, 8:9], reads=[CUM.T(), REF.T()], writes=[REF.T()])
            E("dve", "tensor_copy", REF[:, 1:33], CUM[:, 48:L:64], reads=[CUM.T(), REF.T()], writes=[REF.T()])
            E("dve", "tensor_tensor", DD[:], REF[:, 1:33], REF[:, 0:32], ALU.subtract, reads=[REF.T(), DD.T()], writes=[DD.T()])
            E("act", "activation", DD[:], DD[:], AF.Exp, reads=[DD.T()], writes=[DD.T()])
            for i, (c0, sz) in enumerate(HCH):
                E("dve", "tensor_scalar", CUM[:, c0:c0 + sz], CUM[:, c0:c0 + sz], REF[:, i:i + 1], None, ALU.subtract, reads=[CUM.T(), REF.T()], writes=[CUM.T()])
            E("act", "activation", Eb[:], CUM[:], AF.Exp, reads=[CUM.T(), Eb.T()], writes=[Eb.T()])
            def cons_q(mi, ni, pb, pt, n0, n1):
                E("dve", "tensor_tensor", QT[:, n0:n1], pb[:, 0:n1 - n0], Eb[:, n0:n1], ALU.mult, reads=[pt, Eb.T()], writes=[QT.T()])
            self.proj([(W[:, base + h * 128: base + (h + 1) * 128], 8, 128)], self.act_hi, cons_q)
            E("act", "activation", Eb[:], CUM[:], AF.Exp, scale=-1.0, reads=[CUM.T(), Eb.T(), QT.T()], writes=[Eb.T()])
            E("dve", "tensor_tensor", KT[:], Kb[:], Eb[:], ALU.mult, reads=[Kb.T(), Eb.T(), KT.T()], writes=[KT.T()])
            NCH = len(HCH)

            def p1_tr(i):
                c0, sz = HCH[i]
                pk, pkt = self.bank()
                pkb = pk[:].bitcast(BF16)
                E("pe", "transpose", pkb[0:sz, 0:128], KT[:, c0:c0 + sz], self.IDB[:], reads=[KT.T(), self.IDB.T()], writes=[pkt])
                ktok = KTOK[i % 3]
                E("act", "copy", ktok[0:sz, :], pkb[0:sz, 0:128], reads=[pkt], writes=[ktok.T()])

            def p1_mm(i):
                c0, sz = HCH[i]
                ktok = KTOK[i % 3]
                pt_, ptt = self.bank()
                E("pe", "matmul", pt_[:, 0:128], ktok[0:sz, :], VT[0:sz, i, :], start=True, stop=True, reads=[ktok.T(), VT.T()], writes=[ptt])
                E("dve", "tensor_scalar", XS[:, i, :], pt_[:, 0:128], DD[:, i:i + 1], None, ALU.mult, reads=[ptt, DD.T(), XS.T()], writes=[XS.T()])
            for t in range(NCH):
                if t < NCH - 1:
                    p1_tr(t)
                if 1 <= t:
                    p1_mm(t - 1)
            for i in range(1, NCH - 1):
                E("dve", "scalar_tensor_tensor", XS[:, i, :], XS[:, i - 1, :], DD[:, i:i + 1], XS[:, i, :], ALU.mult, ALU.add,
                  reads=[XS.T(), DD.T()], writes=[XS.T()])
            for q in range(4):
                E("act", "copy", SBA[:, 8 * q:8 * q + 8, :], XS[:, 8 * q:8 * q + 8, :], reads=[XS.T(), SBA.T()], writes=[SBA.T()])
            stt = {}

            def stA(i):
                c0, sz = HCH[i]
                pa, pat = self.bank()
                atm = ATM[i % 3]
                E("pe", "matmul", pa[0:sz, 0:sz], KT[:, c0:c0 + sz], QT[:, c0:c0 + sz], start=True, stop=True, reads=[KT.T(), QT.T()], writes=[pat])
                E("dve", "tensor_tensor", atm[0:sz, 0:sz], pa[0:sz, 0:sz], self.TRI[0:sz, 0:sz], ALU.mult, reads=[pat, self.TRI.T()], writes=[atm.T()])

            def stB(i):
                c0, sz = HCH[i]
                atm, SS = ATM[i % 3], SSL[i % 4]
                po, pot = self.bank()
                stt[i] = (po, pot)
                E("pe", "matmul", po[0:sz, 0:128], atm[0:sz, 0:sz], VT[0:sz, i, :], start=True, stop=(i == 0), reads=[atm.T(), VT.T()], writes=[pot])
                if i > 0:
                    E("pe", "matmul", po[0:sz, 0:128], QT[:, c0:c0 + sz], SBA[:, i - 1, :], start=False, stop=True, reads=[QT.T(), SBA.T()], writes=[pot])
                E("dve", "memset", SS[:, 0:1], 0.0, reads=[SS.T()], writes=[SS.T()])
                E("act", "activation", JUNK[0:sz, :], po[0:sz, 0:128], AF.Square, accum_out=SS[0:sz, 0:1], reads=[pot, SS.T(), JUNK.T()], writes=[SS.T(), JUNK.T()])

            def stB2(i):
                c0, sz = HCH[i]
                on, SS = ON[i % 3], SSL[i % 4]
                po, pot = stt.pop(i)
                E("act", "activation", SS[0:sz, 1:2], SS[0:sz, 0:1], AF.Ln, bias=self.EPS6[0:sz, 0:1], scale=1.0 / 128, reads=[SS.T(), self.EPS6.T()], writes=[SS.T()])
                E("act", "activation", SS[0:sz, 1:2], SS[0:sz, 1:2], AF.Exp, scale=-0.5, reads=[SS.T()], writes=[SS.T()])
                E("dve", "tensor_scalar", on[0:sz, :], po[0:sz, 0:128], SS[0:sz, 1:2], None, ALU.mult, reads=[pot, SS.T(), on.T()], writes=[on.T()])

            def stC(i):
                c0, sz = HCH[i]
                on = ON[i % 3]
                pe_, pet = self.bank()
                peb = pe_[:].bitcast(BF16)
                E("pe", "transpose", peb[:, 0:sz], on[0:sz, :], self.IDB[0:sz, 0:sz], reads=[on.T(), self.IDB.T()], writes=[pet])
                E("dve", "scalar_tensor_tensor", BRH[:, h, c0:c0 + sz], peb[:, 0:sz], self.HGN[:, l, h:h + 1], SGt[:, c0:c0 + sz], ALU.mult, ALU.mult,
                  reads=[pet, self.HGN.T(), SGt.T()], writes=[BRH.T("all")])
            for t in range(NCH + 3):
                if t < NCH:
                    stA(t)
                if 0 <= t - 1 < NCH:
                    stB(t - 1)
                if 0 <= t - 2 < NCH:
                    stB2(t - 2)
                if 0 <= t - 3 < NCH:
                    stC(t - 3)
        fw.barrier()
        A.release(m0)


def host_layout(inp):
    f = np.float32
    o = {}

    def fm(v):
        return np.ascontiguousarray(np.asarray(v, f).reshape(8, 128).T)
    o["meta_tokens"] = np.ascontiguousarray(inp["meta_tokens"], f)
    o["ln_in_g"] = fm(inp["ln_in_g"]); o["ln_in_b"] = fm(inp["ln_in_b"])
    w_in = np.ascontiguousarray(inp["w_in"], f)
    o["w_in"] = w_in
    kr = np.zeros((2, 1024, 96), f); krs = np.zeros((2, 1024, 96), f)
    kr[:, :, 64:96] = w_in[:, :, 512:544]
    krs[:, :, 64:80] = w_in[:, :, 528:544]
    krs[:, :, 80:96] = w_in[:, :, 512:528]
    o["w_kr"] = kr; o["w_kr_sw"] = krs
    o["q_norm"] = np.ascontiguousarray(np.asarray(inp["mla_q_norm"], f).reshape(2, 2, 128).transpose(0, 2, 1))
    o["kv_norm"] = np.ascontiguousarray(np.asarray(inp["mla_kv_norm"], f).reshape(2, 2, 128).transpose(0, 2, 1))
    uq = np.asarray(inp["mla_w_uq"], f)
    o["w_uq"] = np.ascontiguousarray(uq)
    uqs = np.zeros_like(uq).reshape(2, 256, 8, 96)
    uq4 = uq.reshape(2, 256, 8, 96)
    uqs[:, :, :, 64:80] = uq4[:, :, :, 80:96]
    uqs[:, :, :, 80:96] = uq4[:, :, :, 64:80]
    o["w_uq_sw"] = np.ascontiguousarray(uqs.reshape(2, 256, 768))
    o["w_ukv"] = np.ascontiguousarray(inp["mla_w_ukv"], f)
    def sm(v):
        return np.ascontiguousarray(np.asarray(v, f).reshape(2, 16, 2, 64).transpose(0, 2, 3, 1).reshape(2, 128, 16))
    lam_re = np.asarray(inp["s5_lam_re"], f); lam_im = np.asarray(inp["s5_lam_im"], f)
    logdt = np.broadcast_to(np.asarray(inp["s5_log_dt"], f)[:, :, None], (2, 32, 64))
    o["lamre_s"] = sm(lam_re); o["lamim_s"] = sm(lam_im); o["logdt_s"] = sm(logdt)
    def cm(v):
        t = np.asarray(v, f).reshape(2, 4, 8, 1, 64)
        t = np.broadcast_to(t, (2, 4, 8, 16, 64))
        return np.ascontiguousarray(t.transpose(0, 2, 3, 1, 4).reshape(2, 128, 256))
    o["lamre_c"] = cm(lam_re); o["lamim_c"] = cm(lam_im); o["logdt_c"] = cm(logdt)
    def bcm(v):
        t = np.asarray(v, f).reshape(2, 4, 8, 64, 16)
        return np.ascontiguousarray(t.transpose(0, 2, 4, 1, 3).reshape(2, 128, 256))
    o["bre_c"] = bcm(inp["s5_b_re"]); o["bim_c"] = bcm(inp["s5_b_im"])
    def cpad(v):
        v = np.asarray(v, f)
        out = np.zeros((2, 128, 16, 128), f)
        for j in range(16):
            for gl in range(2):
                g = 2 * j + gl
                g8 = g % 8
                out[:, gl * 64:(gl + 1) * 64, j, g8 * 16:(g8 + 1) * 16] = v[:, g].transpose(0, 2, 1)
        return np.ascontiguousarray(out.reshape(2, 128, 2048))
    o["cre_pad"] = cpad(inp["s5_c_re"]); o["cim_pad"] = cpad(inp["s5_c_im"])
    o["s5_d"] = np.ascontiguousarray(np.asarray(inp["s5_d"], f).reshape(2, 4, 128).transpose(0, 2, 1))
    o["w_glu"] = np.ascontiguousarray(inp["s5_w_glu"], f)
    o["lb_logits"] = np.ascontiguousarray(np.asarray(inp["hg_lb_logits"], f).reshape(2, 4, 128).transpose(0, 2, 1))
    o["hg_norm"] = np.ascontiguousarray(np.asarray(inp["hg_out_norm"], f).reshape(2, 4, 128).transpose(0, 2, 1))
    for n in ("w_br_mla", "w_br_s5", "w_br_hg", "w_out", "w_ffn_gate", "w_ffn_up", "w_ffn_down"):
        o[n] = np.ascontiguousarray(inp[n], f)
    for n in ("ln1_g", "ln1_b", "ln2_g", "ln2_b"):
        o[n] = np.ascontiguousarray(np.asarray(inp[n], f).reshape(2, 8, 128).transpose(0, 2, 1))
    inv = (10000.0 ** (-(np.arange(0, 32, 2, dtype=np.float32) / 32))).astype(f)
    cinv = np.zeros((128, 1), f); csgn = np.zeros((128, 1), f)
    cinv[64:80, 0] = inv; cinv[80:96, 0] = inv
    csgn[64:80, 0] = -1.0; csgn[80:96, 0] = 1.0
    o["c_inv"] = cinv; o["c_sgn"] = csgn
    o["c_tau"] = np.ascontiguousarray(np.broadcast_to(np.arange(TC + 1, dtype=f)[None], (128, TC + 1)))
    o["c_metapos"] = np.ascontiguousarray(np.broadcast_to(np.arange(16, dtype=f)[None], (128, 16)))
    bm = np.zeros((128, 32), f)
    for j in range(16):
        for gl in range(2):
            g8 = (2 * j + gl) % 8
            bm[g8 * 16:(g8 + 1) * 16, 2 * j + gl] = 1.0
    o["c_bmask"] = bm
    return o


_CACHE = {}


def kernel(**inputs):
    n_cores = 8
    nseq = 2
    shared = host_layout(inputs)
    x = np.ascontiguousarray(inputs["x"], np.float32)
    pos = np.ascontiguousarray(inputs["positions"], np.int32)
    if "nc" not in _CACHE:
        _CACHE["nc"] = Builder(nseq).build()
    nc = _CACHE["nc"]
    in_maps = []
    for c in range(n_cores):
        m = dict(shared)
        m["x"] = x[c * nseq:(c + 1) * nseq]
        m["positions"] = pos[c * nseq:(c + 1) * nseq]
        in_maps.append(m)
    res = run_bass_kernel_spmd(nc, in_maps, core_ids=list(range(n_cores)))
    return np.concatenate([r["out"] for r in res.results], axis=0).astype(np.float32)
```

```python
import contextlib
import math
import numpy as np
import concourse.bass as bass
import concourse.mybir as mybir
from concourse.bass_utils import run_bass_kernel_spmd

F32 = mybir.dt.float32
BF16 = mybir.dt.bfloat16
I32 = mybir.dt.int32
AF = mybir.ActivationFunctionType
ALU = mybir.AluOpType

L = 2064
NMETA = 16
NT = [(0, 400), (400, 912), (912, 1424), (1424, 1936), (1936, 2064)]
TOKT = [(0, 16)] + [(16 + 128 * i, 128) for i in range(16)]
HCH = [(0, 16)] + [(16 + 64 * i, 64) for i in range(32)]
TC = 344
S5CH = [(TC * i, TC * (i + 1)) for i in range(6)]
ALPHA = 4 ** 0.25
MAGIC = 12582912.0
TWO_PI = 2.0 * math.pi
C1 = 6.28125
C2 = float(np.float32(TWO_PI - 6.28125))
C3 = float(TWO_PI - 6.28125 - float(np.float32(TWO_PI - 6.28125)))
GELU_K = 2.0 * math.sqrt(2.0 / math.pi)


INAMES = {}


class Tile:
    __slots__ = ("name", "w", "r")

    def __init__(self, name=""):
        self.name = name
        self.w = None
        self.r = {}


class Eng:
    def __init__(self, name, handle, sem):
        self.name = name
        self.h = handle
        self.sem = sem
        self.count = 0
        self.q = []
        self.waited = {}


class FW:
    NSLOT = 16

    def __init__(self, nc, stack):
        self.nc = nc
        self.sems = {}
        self.engs = {}
        for name, h in (("pe", nc.tensor), ("act", nc.scalar), ("dve", nc.vector),
                        ("pool", nc.gpsimd), ("sp", nc.sync)):
            sem = stack.enter_context(nc.semaphore("s_" + name))
            self.sems["s_" + name] = sem
            self.engs[name] = Eng(name, h, sem)
        self.slots = {}
        for qn in ("sp", "pool"):
            lst = []
            for i in range(self.NSLOT):
                key = "d_%s_%d" % (qn, i)
                sem = stack.enter_context(nc.semaphore(key))
                self.sems[key] = sem
                lst.append([key, 0])
            self.slots[qn] = [lst, 0]

    def _deps(self, reads, writes, self_key=None):
        deps = {}
        for t in reads:
            if t.w is not None and deps.get(t.w[0], 0) < t.w[1]:
                deps[t.w[0]] = t.w[1]
        for t in writes:
            if t.w is not None and deps.get(t.w[0], 0) < t.w[1]:
                deps[t.w[0]] = t.w[1]
            for k, v in t.r.items():
                if k == self_key:
                    continue
                if deps.get(k, 0) < v:
                    deps[k] = v
        return deps

    def _emit_waits(self, eng, deps, skip_self=False):
        for k, v in deps.items():
            if skip_self and k == "s_" + eng.name:
                continue
            if eng.waited.get(k, 0) >= v:
                continue
            eng.waited[k] = v
            eng.q.append(lambda e=eng.h, s=self.sems[k], v=v: e.wait_ge(s, v))

    def _mark(self, tok, reads, writes):
        k, v = tok
        for t in reads:
            if t.r.get(k, 0) < v:
                t.r[k] = v
        for t in writes:
            t.w = tok
            t.r = {}

    def op(self, engname, method, args, kw, reads=(), writes=()):
        eng = self.engs[engname]
        deps = self._deps(reads, writes, self_key="s_" + engname)
        self._emit_waits(eng, deps, skip_self=(engname == "pe"))
        eng.count += 1
        import sys as _sys
        ln = _sys._getframe(2).f_lineno

        def _mk(e=eng.h, s=eng.sem, m=method, a=args, kw=kw, ln=ln):
            ins = getattr(e, m)(*a, **kw)
            try:
                INAMES[ins.ins.name] = (m, ln)
            except Exception:
                pass
            return ins.then_inc(s, 1)
        eng.q.append(_mk)
        self._mark(("s_" + engname, eng.count), reads, writes)

    def dma(self, qname, out, in_, reads=(), writes=()):
        eng = self.engs[qname]
        lst, idx = self.slots[qname]
        slot = lst[idx % self.NSLOT]
        self.slots[qname][1] = idx + 1
        deps = self._deps(reads, writes)
        if slot[1] > 0:
            deps[slot[0]] = max(deps.get(slot[0], 0), slot[1])
        self._emit_waits(eng, deps)
        slot[1] += 16
        eng.q.append(lambda e=eng.h, s=self.sems[slot[0]], o=out, i=in_:
                     e.dma_start(out=o, in_=i).then_inc(s, 16))
        self._mark((slot[0], slot[1]), reads, writes)

    def barrier(self):
        deps = {}
        for n, e in self.engs.items():
            if e.count:
                deps["s_" + n] = e.count
        for qn, (lst, idx) in self.slots.items():
            for key, v in lst:
                if v:
                    deps[key] = v
        for n in self.engs:
            self._emit_waits(self.engs[n], deps)

    def finish(self):
        self.barrier()
        nc = self.nc
        with nc.Block() as block:
            @block.tensor
            def _(e):
                for f in self.engs["pe"].q:
                    f()

            @block.scalar
            def _(e):
                for f in self.engs["act"].q:
                    f()

            @block.vector
            def _(e):
                for f in self.engs["dve"].q:
                    f()

            @block.gpsimd
            def _(e):
                for f in self.engs["pool"].q:
                    f()

            @block.sync
            def _(e):
                for f in self.engs["sp"].q:
                    f()


class Buf:
    def __init__(self, t, name):
        self.t = t
        self.name = name
        self.tiles = {}

    def T(self, key=0):
        if key not in self.tiles:
            self.tiles[key] = Tile("%s/%s" % (self.name, key))
        return self.tiles[key]

    def __getitem__(self, idx):
        return self.t[idx]


class Alloc:
    def __init__(self, nc, limit):
        self.nc = nc
        self.off = 0
        self.limit = limit
        self.n = 0
        self.peak = 0
        self.big = nc.alloc_sbuf_tensor("bigbuf", [128, limit // 2], BF16)

    def alloc(self, name, shape, dtype):
        nel = int(np.prod(shape[1:]))
        esz = 4 if dtype in (F32, I32) else 2
        nbytes = (nel * esz + 63) // 64 * 64
        self.n += 1
        assert self.off + nbytes <= self.limit, "SBUF overflow at %s: %d + %d" % (name, self.off, nbytes)
        ap = self.big[:, self.off // 2:(self.off + nbytes) // 2]
        if dtype != BF16:
            ap = ap.bitcast(dtype)
        ap = ap[:, 0:nel]
        if len(shape) == 3:
            ap = ap.rearrange("p (a b) -> p a b", b=shape[2])
        elif len(shape) == 4:
            ap = ap.rearrange("p (a b c) -> p a b c", b=shape[2], c=shape[3])
        if shape[0] < 128:
            ap = ap[0:shape[0]]
        self.off += nbytes
        self.peak = max(self.peak, self.off)
        return Buf(ap, name)

    def mark(self):
        return self.off

    def release(self, m):
        self.off = m


class Builder:
    def __init__(self, nseq, debug=None):
        self.nseq = nseq
        self.debug = debug or []
        self.dbg_outs = {}

    def declare(self, nc):
        d = {}

        def inp(name, shape, dt=F32):
            d[name] = nc.dram_tensor(name, list(shape), dt, kind="ExternalInput").ap()
        ns = self.nseq
        inp("x", [ns, 2048, 1024]); inp("positions", [ns, 2048], I32); inp("meta_tokens", [16, 1024])
        inp("ln_in_g", [128, 8]); inp("ln_in_b", [128, 8])
        inp("w_in", [2, 1024, 6176]); inp("w_kr", [2, 1024, 96]); inp("w_kr_sw", [2, 1024, 96])
        inp("q_norm", [2, 128, 2]); inp("kv_norm", [2, 128, 2])
        inp("w_uq", [2, 256, 768]); inp("w_uq_sw", [2, 256, 768]); inp("w_ukv", [2, 256, 1024])
        for n in ("lamre_s", "lamim_s", "logdt_s"):
            inp(n, [2, 128, 16])
        for n in ("lamre_c", "lamim_c", "logdt_c"):
            inp(n, [2, 128, 256])
        inp("bre_c", [2, 128, 256]); inp("bim_c", [2, 128, 256])
        inp("cre_pad", [2, 128, 2048]); inp("cim_pad", [2, 128, 2048])
        inp("s5_d", [2, 128, 4]); inp("w_glu", [2, 512, 512])
        inp("lb_logits", [2, 128, 4]); inp("hg_norm", [2, 128, 4])
        inp("w_br_mla", [2, 512, 1024]); inp("w_br_s5", [2, 512, 1024]); inp("w_br_hg", [2, 512, 1024])
        inp("w_out", [2, 1024, 1024])
        for n in ("ln1_g", "ln1_b", "ln2_g", "ln2_b"):
            inp(n, [2, 128, 8])
        inp("w_ffn_gate", [2, 1024, 2816]); inp("w_ffn_up", [2, 1024, 2816]); inp("w_ffn_down", [2, 2816, 1024])
        inp("c_inv", [128, 1]); inp("c_sgn", [128, 1]); inp("c_tau", [128, TC + 1]); inp("c_metapos", [128, 16])
        inp("c_bmask", [128, 32])
        d["out"] = nc.dram_tensor("out", [ns, 2048, 1024], F32, kind="ExternalOutput").ap()
        self.d = d

    def dbg(self, name, ap, tiles, shape):
        if name not in self.debug:
            return
        o = self.nc.dram_tensor("dbg_" + name, list(shape), ap.dtype, kind="ExternalOutput").ap()
        self.dbg_outs[name] = shape
        self.fw.dma("sp", o, ap, reads=tiles, writes=[Tile()])

    def E(self, eng, method, *args, reads=(), writes=(), **kw):
        self.fw.op(eng, method, args, kw, reads, writes)

    def bank(self):
        i = self.bank_i
        self.bank_i = (i + 1) % len(self.bank_list)
        b = self.bank_list[i]
        return self.P[b], self.PT[b]

    def wbuf(self):
        i = self.w_i
        self.w_i = (i + 1) % len(self.WB)
        return self.WB[i]

    def load_w(self, src2d, kin, ncols, rows=128):
        wb = self.wbuf()
        view = wb.t[0:rows, 0:kin * ncols].rearrange("p (k c) -> p k c", c=ncols)
        self.fw.dma("pool", view, src2d.rearrange("(k p) c -> p k c", p=rows), writes=[wb.T()])
        return view, wb.T()

    def proj(self, specs, act, consume, ranges=NT):
        loaded = {}
        D = 2

        def ensure(i):
            if i < len(specs) and i not in loaded:
                s = specs[i]
                loaded[i] = self.load_w(s[0], s[1], s[2], s[3] if len(s) > 3 else 128)
        for i in range(min(D, len(specs))):
            ensure(i)
        for mi, s in enumerate(specs):
            ensure(mi + D)
            wv, wt = loaded.pop(mi)
            kin, ncols = s[1], s[2]
            for ni, (n0, n1) in enumerate(ranges):
                pb, pt = self.bank()
                for k in range(kin):
                    a, at = act(k)
                    self.E("pe", "matmul", pb[0:ncols, 0:n1 - n0], wv[:, k, :], a[:, n0:n1], start=(k == 0), stop=(k == kin - 1),
                           reads=[wt, at], writes=[pt])
                consume(mi, ni, pb, pt, n0, n1)

    def layer_norm(self, xk, xt, g, b, ranges, fill=None, ni_base=0):
        A = self.A
        m0 = A.mark()
        SQ = [A.alloc("lnsq", [128, 512], F32) for _ in range(2)]
        MEANS = [A.alloc("lnmean", [128, 512], F32) for _ in range(2)]
        RSTDS = [A.alloc("lnrstd", [128, 512], F32) for _ in range(2)]

        def stats(idx):
            n0, n1 = ranges[idx]
            ni = idx + ni_base
            n = n1 - n0
            MEAN, RSTD = MEANS[idx % 2], RSTDS[idx % 2]
            pa, pat = self.bank()
            pb, pbt = self.bank()
            if fill is not None:
                for k in range(8):
                    fill(k, ni, n0, n1)
            for k in range(8):
                sq = SQ[k % 2]
                x = xk(k, n0, n1)
                self.E("act", "activation", sq[:, 0:n], x, AF.Square,
                       reads=[xt(k, ni)], writes=[sq.T()])
                self.E("pe", "matmul", pa[:, 0:n], self.ONESF[:, 0:128], x, start=(k == 0), stop=(k == 7),
                       reads=[xt(k, ni), self.ONESF.T()], writes=[pat])
                self.E("pe", "matmul", pb[:, 0:n], self.ONESF[:, 0:128], sq[:, 0:n], start=(k == 0), stop=(k == 7),
                       reads=[sq.T(), self.ONESF.T()], writes=[pbt])
            self.E("act", "activation", MEAN[:, 0:n], pa[:, 0:n], AF.Copy, scale=1.0 / 1024,
                   reads=[pat], writes=[MEAN.T()])
            self.E("dve", "tensor_tensor", RSTD[:, 0:n], MEAN[:, 0:n], MEAN[:, 0:n], ALU.mult,
                   reads=[MEAN.T()], writes=[RSTD.T()])
            self.E("dve", "scalar_tensor_tensor", RSTD[:, 0:n], pb[:, 0:n], 1.0 / 1024, RSTD[:, 0:n], ALU.mult, ALU.subtract,
                   reads=[pbt, RSTD.T()], writes=[RSTD.T()])
            self.E("act", "activation", RSTD[:, 0:n], RSTD[:, 0:n], AF.Sqrt, bias=self.EPS5[:, 0:1], scale=1.0,
                   reads=[RSTD.T(), self.EPS5.T()], writes=[RSTD.T()])
            self.E("dve", "reciprocal", RSTD[:, 0:n], RSTD[:, 0:n], reads=[RSTD.T()], writes=[RSTD.T()])

        def apply(idx):
            n0, n1 = ranges[idx]
            ni = idx + ni_base
            n = n1 - n0
            MEAN, RSTD = MEANS[idx % 2], RSTDS[idx % 2]
            for k in range(8):
                x = xk(k, n0, n1)
                t = xt(k, ni)
                self.E("dve", "tensor_tensor", x, x, MEAN[:, 0:n], ALU.subtract, reads=[t, MEAN.T()], writes=[t])
                self.E("dve", "tensor_tensor", x, x, RSTD[:, 0:n], ALU.mult, reads=[t, RSTD.T()], writes=[t])
                self.E("act", "activation", x, x, AF.Identity, bias=b[:, k:k + 1], scale=g[:, k:k + 1],
                       reads=[t, g.T(), b.T()], writes=[t])
                hi = self.HI[:, k, n0:n1]
                lo = self.LO[:, k, n0:n1]
                self.E("act", "copy", hi, x, reads=[t], writes=[self.HI.T((k, ni))])
                self.E("pool", "tensor_tensor", lo, x, hi, ALU.subtract,
                       reads=[t, self.HI.T((k, ni))], writes=[self.LO.T((k, ni))])
        stats(0)
        for idx in range(len(ranges)):
            if idx + 1 < len(ranges):
                stats(idx + 1)
            apply(idx)
        A.release(m0)

    def act_hi(self, k):
        return self.HI[:, k, :], self.HI.T("all")

    def build(self):
        nc = bass.Bass("TRN2", target_bir_lowering=False)
        self.nc = nc
        self.declare(nc)
        d = self.d
        with contextlib.ExitStack() as st:
            fw = FW(nc, st)
            self.fw = fw
            A = Alloc(nc, 212480)
            self.A = A
            self.P = [st.enter_context(nc.psum_tensor("ps%d" % i, [128, 512], F32)) for i in range(8)]
            self.PT = [Tile("ps%d" % i) for i in range(8)]
            self.bank_list = [0, 1, 2, 3, 4, 5, 6, 7]
            self.bank_i = 0
            self.ONESF = A.alloc("onesf", [128, L], F32)
            self.IDF = A.alloc("identf", [128, 128], F32)
            self.IDB = A.alloc("identb", [128, 128], BF16)
            self.TRI = A.alloc("tri", [128, 128], BF16)
            self.EPS5 = A.alloc("eps5", [128, 1], F32)
            self.EPS6 = A.alloc("eps6", [128, 1], F32)
            self.HALFPI = A.alloc("halfpi", [128, 1], F32)
            self.LNP = A.alloc("lnp", [128, 10, 8], F32)
            self.LBT = A.alloc("lbt", [128, 2, 2, 4], F32)
            self.HGN = A.alloc("hgn", [128, 2, 4], F32)
            self.CINV = A.alloc("cinv", [128, 1], F32)
            self.CSGN = A.alloc("csgn", [128, 1], F32)
            self.QKN = A.alloc("qkn", [128, 2, 2, 2], F32)
            self.WB = [A.alloc("wb%d" % i, [128, 1408], BF16) for i in range(8)]
            self.w_i = 0
            self.HI = A.alloc("hi", [128, 8, L], BF16)
            self.LO = A.alloc("lo", [128, 8, L], BF16)
            E = self.E
            E("dve", "memset", self.ONESF[:], 1.0, writes=[self.ONESF.T()])
            E("pool", "memset", self.IDF[:], 0.0, writes=[self.IDF.T()])
            E("pool", "affine_select", self.IDF[:], self.IDF[:], [[-1, 128]], ALU.not_equal, 1.0, base=0, channel_multiplier=1,
              reads=[self.IDF.T()], writes=[self.IDF.T()])
            E("dve", "tensor_copy", self.IDB[:], self.IDF[:], reads=[self.IDF.T()], writes=[self.IDB.T()])
            E("pool", "memset", self.TRI[:], 1.0, writes=[self.TRI.T()])
            E("pool", "affine_select", self.TRI[:], self.TRI[:], [[1, 128]], ALU.is_ge, 0.0, base=0, channel_multiplier=-1,
              reads=[self.TRI.T()], writes=[self.TRI.T()])
            E("dve", "memset", self.EPS5[:], 1e-5, writes=[self.EPS5.T()])
            E("dve", "memset", self.EPS6[:], 1e-6, writes=[self.EPS6.T()])
            E("dve", "memset", self.HALFPI[:], math.pi / 2, writes=[self.HALFPI.T()])
            for i, n in enumerate(["ln_in_g", "ln_in_b"]):
                fw.dma("sp", self.LNP[:, i, :], d[n], writes=[self.LNP.T()])
            for l in range(2):
                for i, n in enumerate(["ln1_g", "ln1_b", "ln2_g", "ln2_b"]):
                    fw.dma("sp", self.LNP[:, 2 + 4 * l + i, :], d[n][l], writes=[self.LNP.T()])
                fw.dma("sp", self.HGN[:, l, :], d["hg_norm"][l], writes=[self.HGN.T()])
                fw.dma("sp", self.QKN[:, l, 0, :], d["q_norm"][l], writes=[self.QKN.T()])
                fw.dma("sp", self.QKN[:, l, 1, :], d["kv_norm"][l], writes=[self.QKN.T()])
            fw.dma("sp", self.CINV[:], d["c_inv"], writes=[self.CINV.T()])
            fw.dma("sp", self.CSGN[:], d["c_sgn"], writes=[self.CSGN.T()])
            m0 = A.mark()
            LG = A.alloc("lg", [128, 2, 4], F32)
            for l in range(2):
                fw.dma("sp", LG[:, l, :], d["lb_logits"][l], writes=[LG.T()])
            E("dve", "memset", self.LBT[:, 0, 0, :], 0.0, writes=[self.LBT.T()])
            E("dve", "memset", self.LBT[:, 0, 1, :], 1.0, writes=[self.LBT.T()])
            E("dve", "tensor_tensor", LG[:, 1, :], LG[:, 1, :], LG[:, 0, :], ALU.subtract, reads=[LG.T()], writes=[LG.T()])
            E("act", "activation", self.LBT[:, 1, 0, :], LG[:, 1, :], AF.Sigmoid, reads=[LG.T()], writes=[self.LBT.T()])
            E("dve", "tensor_scalar", self.LBT[:, 1, 1, :], self.LBT[:, 1, 0, :], -1.0, 1.0, ALU.mult, ALU.add,
              reads=[self.LBT.T()], writes=[self.LBT.T()])
            fw.barrier()
            A.release(m0)

            for s in range(self.nseq):
                self.seq(s)
            fw.finish()
        print("SBUF peak bytes/partition:", A.peak, " instr counts:", {n: e.count for n, e in fw.engs.items()})
        return nc

    def seq(self, s):
        fw, A, d, E = self.fw, self.A, self.d, self.E
        self.HI.tiles = {}
        self.LO.tiles = {}
        ms = A.mark()
        m0 = A.mark()
        X = A.alloc("xin_fm", [128, 8, 512], F32)
        XIN = [A.alloc("xin%d" % i, [128, 1024], F32) for i in range(2)]
        xi = 0
        for ni, (n0, n1) in enumerate(NT):
            for (c0, sz) in TOKT:
                if not (n0 <= c0 < n1):
                    continue
                xb = XIN[xi % 2]
                xi += 1
                src = d["meta_tokens"] if c0 == 0 else d["x"][s, c0 - 16:c0 - 16 + sz, :]
                fw.dma("sp", xb[0:sz, :], src, writes=[xb.T()])
                for k in range(8):
                    pb, pt = self.bank()
                    E("pe", "transpose", pb[:, 0:sz], xb[0:sz, k * 128:(k + 1) * 128], self.IDF[0:sz, 0:sz],
                      reads=[xb.T(), self.IDF.T()], writes=[pt])
                    eng = "act" if k % 2 else "dve"
                    dst = X[:, k, c0 - n0:c0 - n0 + sz]
                    if eng == "act":
                        E("act", "copy", dst, pb[:, 0:sz], reads=[pt], writes=[X.T(k)])
                    else:
                        E("dve", "tensor_copy", dst, pb[:, 0:sz], reads=[pt], writes=[X.T(k)])
            self.layer_norm(lambda k, a, b, X=X, n0=n0: X[:, k, a - n0:b - n0], lambda k, ni_, X=X: X.T(k),
                            Buf(self.LNP[:, 0, :], "g0"), Buf(self.LNP[:, 1, :], "b0"), [(n0, n1)])
        fw.barrier()
        A.release(m0)
        self.dbg("h0", self.HI[:, :, :], [], [128, 8, L])
        for l in range(2):
            self.layer(s, l)
        m0 = A.mark()
        YT = [A.alloc("yt%d" % i, [128, 128], F32) for i in range(2)]
        OB = [A.alloc("ob%d" % i, [128, 1024], F32) for i in range(2)]
        for ti, (c0, sz) in enumerate(TOKT[1:]):
            ob = OB[ti % 2]
            for k in range(8):
                yt = YT[k % 2]
                E("dve", "tensor_tensor", yt[:], self.HI[:, k, c0:c0 + 128], self.LO[:, k, c0:c0 + 128], ALU.add,
                  reads=[], writes=[yt.T()])
                pb, pt = self.bank()
                E("pe", "transpose", pb[:, 0:128], yt[:], self.IDF[:], reads=[yt.T(), self.IDF.T()], writes=[pt])
                E("act", "copy", ob[:, k * 128:(k + 1) * 128], pb[:, 0:128], reads=[pt], writes=[ob.T()])
            fw.dma("sp", d["out"][s, c0 - 16:c0 - 16 + 128, :], ob[:], reads=[ob.T()], writes=[Tile()])
        fw.barrier()
        A.release(m0)
        A.release(ms)

    def sincos(self, ANG, KK, SIN, COS, n):
        E = self.E
        E("dve", "tensor_scalar", KK[:, 0:n], ANG[:, 0:n], 1.0 / TWO_PI, MAGIC, ALU.mult, ALU.add, reads=[ANG.T()], writes=[KK.T()])
        E("dve", "tensor_scalar", KK[:, 0:n], KK[:, 0:n], -MAGIC, None, ALU.add, reads=[KK.T()], writes=[KK.T()])
        E("dve", "scalar_tensor_tensor", ANG[:, 0:n], KK[:, 0:n], -C1, ANG[:, 0:n], ALU.mult, ALU.add, reads=[KK.T(), ANG.T()], writes=[ANG.T()])
        E("dve", "scalar_tensor_tensor", ANG[:, 0:n], KK[:, 0:n], -C2, ANG[:, 0:n], ALU.mult, ALU.add, reads=[KK.T(), ANG.T()], writes=[ANG.T()])
        E("dve", "scalar_tensor_tensor", ANG[:, 0:n], KK[:, 0:n], -C3, ANG[:, 0:n], ALU.mult, ALU.add, reads=[KK.T(), ANG.T()], writes=[ANG.T()])
        E("dve", "tensor_scalar", ANG[:, 0:n], ANG[:, 0:n], -math.pi, math.pi, ALU.max, ALU.min, reads=[ANG.T()], writes=[ANG.T()])
        E("act", "activation", SIN[:, 0:n], ANG[:, 0:n], AF.Sin, reads=[ANG.T()], writes=[SIN.T()])
        E("act", "activation", KK[:, 0:n], ANG[:, 0:n], AF.Abs, reads=[ANG.T()], writes=[KK.T()])
        E("act", "activation", COS[:, 0:n], KK[:, 0:n], AF.Sin, bias=self.HALFPI[:, 0:1], scale=-1.0,
          reads=[KK.T(), self.HALFPI.T()], writes=[COS.T()])

    def layer(self, s, l):
        fw, A, d, E = self.fw, self.A, self.d, self.E
        W = d["w_in"][l]
        ml = A.mark()
        BRM = A.alloc("brm", [128, 4, L], BF16)
        self.mla(s, l, W, BRM)
        fw.barrier()
        self.dbg("brm%d" % l, BRM[:, :, :], [], [128, 4, L])
        BRS = A.alloc("brs", [128, 4, L], BF16)
        self.s5(s, l, W, BRS)
        fw.barrier()
        self.dbg("brs%d" % l, BRS[:, :, :], [], [128, 4, L])
        BRH = A.alloc("brh", [128, 4, L], BF16)
        self.hgrn(s, l, W, BRH)
        fw.barrier()
        self.dbg("brh%d" % l, BRH[:, :, :], [], [128, 4, L])
        MIX = A.alloc("mix", [128, 8, L], BF16)
        m0 = A.mark()
        GT = [A.alloc("gt%d" % i, [128, 512], F32) for i in range(3)]
        PR = [A.alloc("pr%d" % i, [128, 512], F32) for i in range(3)]
        brs = [(BRM, d["w_br_mla"][l]), (BRS, d["w_br_s5"][l]), (BRH, d["w_br_hg"][l])]
        for m in range(8):
            gw = [self.load_w(W[:, 3104 + b * 1024 + m * 128: 3104 + b * 1024 + (m + 1) * 128], 8, 128) for b in range(3)]
            bw = [self.load_w(brs[b][1][:, m * 128:(m + 1) * 128], 4, 128) for b in range(3)]
            for ni, (n0, n1) in enumerate(NT):
                n = n1 - n0
                for b in range(3):
                    pg, pgt = self.bank()
                    for k in range(8):
                        E("pe", "matmul", pg[:, 0:n1 - n0], gw[b][0][:, k, :], self.HI[:, k, n0:n1], start=(k == 0), stop=(k == 7),
                          reads=[gw[b][1], self.HI.T("all")], writes=[pgt])
                    g = GT[b]
                    p = PR[b]
                    src = brs[b][0]
                    E("act", "activation", g[:, 0:n], pg[:, 0:n], AF.Sigmoid, reads=[pgt], writes=[GT[b].T()])
                    py, pyt = self.bank()
                    for k in range(4):
                        E("pe", "matmul", py[:, 0:n1 - n0], bw[b][0][:, k, :], src[:, k, n0:n1], start=(k == 0), stop=(k == 3),
                          reads=[bw[b][1], brs[b][0].T("all")], writes=[pyt])
                    E("dve", "tensor_tensor", p[:, 0:n], py[:, 0:n], g[:, 0:n], ALU.mult,
                      reads=[pyt, GT[b].T()], writes=[PR[b].T()])
                E("dve", "tensor_tensor", PR[0][:, 0:n], PR[0][:, 0:n], PR[1][:, 0:n], ALU.add, reads=[PR[0].T(), PR[1].T()], writes=[PR[0].T()])
                E("dve", "tensor_tensor", MIX[:, m, n0:n1], PR[0][:, 0:n], PR[2][:, 0:n], ALU.add,
                  reads=[PR[0].T(), PR[2].T()], writes=[MIX.T("all")])
        fw.barrier()
        A.release(m0)
        self.dbg("mix%d" % l, MIX[:, :, :], [], [128, 8, L])
        T32 = [A.alloc("t32_%d" % i, [128, 512], F32) for i in range(2)]
        self.HI.tiles = {}
        self.LO.tiles = {}

        def cons_out(mi, ni, pb, pt, n0, n1):
            n = n1 - n0
            t = T32[ni % 2]
            hk = self.HI.T(("w", mi, ni))
            lk = self.LO.T(("w", mi, ni))
            E("dve", "scalar_tensor_tensor", t[:, 0:n], self.HI[:, mi, n0:n1], ALPHA, pb[:, 0:n], ALU.mult, ALU.add,
              reads=[pt, hk], writes=[t.T()])
            E("dve", "scalar_tensor_tensor", t[:, 0:n], self.LO[:, mi, n0:n1], ALPHA, t[:, 0:n], ALU.mult, ALU.add,
              reads=[t.T(), lk], writes=[t.T()])
            E("act", "copy", self.HI[:, mi, n0:n1], t[:, 0:n], reads=[t.T()], writes=[hk])
            E("dve", "tensor_tensor", self.LO[:, mi, n0:n1], t[:, 0:n], self.HI[:, mi, n0:n1], ALU.subtract, reads=[t.T(), hk], writes=[lk])
        self.proj([(d["w_out"][l][:, m * 128:(m + 1) * 128], 8, 128) for m in range(8)],
                  lambda k: (MIX[:, k, :], MIX.T("all")), cons_out)
        fw.barrier()
        A.release(ml)
        self.HI.tiles = {}
        self.LO.tiles = {}
        XX = [A.alloc("ln1x%d" % i, [128, 8, 512], F32) for i in range(2)]

        def fill1(k, ni, n0, n1):
            X = XX[ni % 2]
            E("dve", "tensor_tensor", X[:, k, 0:n1 - n0], self.HI[:, k, n0:n1], self.LO[:, k, n0:n1], ALU.add,
              reads=[self.HI.T((k, ni)), self.LO.T((k, ni))], writes=[X.T(k)])
        nidx = {r: i for i, r in enumerate(NT)}
        self.layer_norm(lambda k, a, b: XX[nidx[(a, b)] % 2][:, k, 0:b - a], lambda k, ni_: XX[ni_ % 2].T(k),
                        Buf(self.LNP[:, 2 + 4 * l, :], "g1"), Buf(self.LNP[:, 3 + 4 * l, :], "b1"), NT, fill=fill1)
        fw.barrier()
        A.release(ml)
        self.HI.tiles = {}
        self.LO.tiles = {}
        self.dbg("h1_%d" % l, self.HI[:, :, :], [], [128, 8, L])
        RACC = A.alloc("racc", [128, 8, L], F32)
        mf = A.mark()
        ACTT = A.alloc("actt", [128, 8, L], BF16)
        SG = [A.alloc("sg%d" % i, [128, 512], F32) for i in range(2)]
        for k in range(8):
            E("dve", "tensor_tensor", RACC[:, k, :], self.HI[:, k, :], self.LO[:, k, :], ALU.add, reads=[], writes=[RACC.T("all")])
            E("dve", "tensor_scalar", RACC[:, k, :], RACC[:, k, :], ALPHA, None, ALU.mult, reads=[RACC.T("all")], writes=[RACC.T("all")])
        fw.barrier()
        groups = [list(range(g, min(g + 8, 22))) for g in range(0, 22, 8)]
        for grp in groups:
            for fi, f in enumerate(grp):
                wg, wgt = self.load_w(d["w_ffn_gate"][l][:, f * 128:(f + 1) * 128], 8, 128)
                wu, wut = self.load_w(d["w_ffn_up"][l][:, f * 128:(f + 1) * 128], 8, 128)
                for ni, (n0, n1) in enumerate(NT):
                    n = n1 - n0
                    pg, pgt = self.bank()
                    pu, put = self.bank()
                    for k in range(8):
                        E("pe", "matmul", pg[:, 0:n1 - n0], wg[:, k, :], self.HI[:, k, n0:n1], start=(k == 0), stop=(k == 7),
                          reads=[wgt, self.HI.T("all")], writes=[pgt])
                    for k in range(8):
                        E("pe", "matmul", pu[:, 0:n1 - n0], wu[:, k, :], self.HI[:, k, n0:n1], start=(k == 0), stop=(k == 7),
                          reads=[wut, self.HI.T("all")], writes=[put])
                    sg = SG[ni % 2]
                    E("act", "activation", sg[:, 0:n], pg[:, 0:n], AF.Silu, reads=[pgt], writes=[sg.T()])
                    E("dve", "tensor_tensor", ACTT[:, fi, n0:n1], pu[:, 0:n], sg[:, 0:n], ALU.mult,
                      reads=[put, sg.T()], writes=[ACTT.T("all")])
            ng = len(grp)
            f0 = grp[0]

            def cons_dn(mi, ni, pb, pt, n0, n1):
                n = n1 - n0
                E("dve", "tensor_tensor", RACC[:, mi, n0:n1], RACC[:, mi, n0:n1], pb[:, 0:n], ALU.add,
                  reads=[pt], writes=[RACC.T((mi, ni))])
            self.proj([(d["w_ffn_down"][l][f0 * 128:(f0 + ng) * 128, m * 128:(m + 1) * 128], ng, 128) for m in range(8)],
                      lambda k: (ACTT[:, k, :], ACTT.T("all")), cons_dn)
        fw.barrier()
        A.release(mf)
        self.HI.tiles = {}
        self.LO.tiles = {}
        self.layer_norm(lambda k, a, b: RACC[:, k, a:b], lambda k, ni: RACC.T((k, ni)),
                        Buf(self.LNP[:, 4 + 4 * l, :], "g2"), Buf(self.LNP[:, 5 + 4 * l, :], "b2"), NT)
        fw.barrier()
        self.HI.tiles = {}
        self.LO.tiles = {}
        self.dbg("h2_%d" % l, self.HI[:, :, :], [], [128, 8, L])
        A.release(ml)

    def rope_tables(self, s):
        fw, A, d, E = self.fw, self.A, self.d, self.E
        self.TCOS = A.alloc("tcos", [128, L], F32)
        self.TSIN = A.alloc("tsin", [128, L], F32)
        m0 = A.mark()
        PI = A.alloc("posi", [128, 2048], I32)
        ANG = A.alloc("ang", [128, L], F32)
        KK = A.alloc("kk", [128, L], F32)
        fw.dma("sp", PI[:], d["positions"][s:s + 1, :].partition_broadcast(128), writes=[PI.T()])
        fw.dma("sp", ANG[:, 0:16], d["c_metapos"], writes=[ANG.T()])
        E("dve", "tensor_copy", ANG[:, 16:L], PI[:], reads=[PI.T(), ANG.T()], writes=[ANG.T()])
        E("dve", "tensor_scalar", ANG[:, 16:L], ANG[:, 16:L], 16.0, None, ALU.add, reads=[ANG.T()], writes=[ANG.T()])
        E("dve", "tensor_scalar", ANG[:], ANG[:], self.CINV[:, 0:1], None, ALU.mult, reads=[ANG.T(), self.CINV.T()], writes=[ANG.T()])
        self.sincos(ANG, KK, self.TSIN, self.TCOS, L)
        E("dve", "tensor_scalar", self.TSIN[:], self.TSIN[:], self.CSGN[:, 0:1], None, ALU.mult,
          reads=[self.TSIN.T(), self.CSGN.T()], writes=[self.TSIN.T()])
        fw.barrier()
        A.release(m0)

    def mla(self, s, l, W, BRM):
        fw, A, d, E = self.fw, self.A, self.d, self.E
        m0 = A.mark()
        self.rope_tables(s)
        CN = A.alloc("cn", [128, 4, L], BF16)
        RAW = A.alloc("craw", [128, 2, L], F32)
        KRO = A.alloc("kro", [128, L], BF16)
        SQ = A.alloc("msq", [128, 512], F32)
        R = A.alloc("mr", [128, 512], F32)
        T1 = A.alloc("mt1", [128, 512], F32)
        T2 = A.alloc("mt2", [128, 512], F32)
        for which in range(2):
            def cons(mi, ni, pb, pt, n0, n1):
                E("act", "copy", RAW[:, mi, n0:n1], pb[:, 0:n1 - n0], reads=[pt], writes=[RAW.T((mi, ni))])
            self.proj([(W[:, which * 256 + m * 128: which * 256 + (m + 1) * 128], 8, 128) for m in range(2)], self.act_hi, cons)
            for ni, (n0, n1) in enumerate(NT):
                n = n1 - n0
                pb, pt = self.bank()
                for m in range(2):
                    E("act", "activation", SQ[:, 0:n], RAW[:, m, n0:n1], AF.Square, reads=[RAW.T((m, ni))], writes=[SQ.T()])
                    E("pe", "matmul", pb[:, 0:n], self.ONESF[:, 0:128], SQ[:, 0:n], start=(m == 0), stop=(m == 1),
                      reads=[SQ.T(), self.ONESF.T()], writes=[pt])
                E("act", "activation", R[:, 0:n], pb[:, 0:n], AF.Sqrt, bias=self.EPS6[:, 0:1], scale=1.0 / 256,
                  reads=[pt, self.EPS6.T()], writes=[R.T()])
                E("dve", "reciprocal", R[:, 0:n], R[:, 0:n], reads=[R.T()], writes=[R.T()])
                for m in range(2):
                    E("dve", "scalar_tensor_tensor", CN[:, 2 * which + m, n0:n1], RAW[:, m, n0:n1], self.QKN[:, l, which, m:m + 1], R[:, 0:n], ALU.mult, ALU.mult,
                      reads=[RAW.T((m, ni)), R.T(), self.QKN.T()], writes=[CN.T("all")])
        wa, wat = self.load_w(d["w_kr"][l], 8, 96)
        wb_, wbt = self.load_w(d["w_kr_sw"][l], 8, 96)
        for ni, (n0, n1) in enumerate(NT):
            n = n1 - n0
            pa, pat = self.bank()
            pb, pbt = self.bank()
            for k in range(8):
                E("pe", "matmul", pa[0:96, 0:n], wa[:, k, :], self.HI[:, k, n0:n1], start=(k == 0), stop=(k == 7), reads=[wat, self.HI.T("all")], writes=[pat])
            for k in range(8):
                E("pe", "matmul", pb[0:96, 0:n], wb_[:, k, :], self.HI[:, k, n0:n1], start=(k == 0), stop=(k == 7), reads=[wbt, self.HI.T("all")], writes=[pbt])
            E("dve", "tensor_tensor", T1[64:96, 0:n], pa[64:96, 0:n], self.TCOS[64:96, n0:n1], ALU.mult, reads=[pat, self.TCOS.T()], writes=[T1.T()])
            E("dve", "tensor_tensor", T2[64:96, 0:n], pb[64:96, 0:n], self.TSIN[64:96, n0:n1], ALU.mult, reads=[pbt, self.TSIN.T()], writes=[T2.T()])
            E("dve", "tensor_tensor", KRO[64:96, n0:n1], T1[64:96, 0:n], T2[64:96, 0:n], ALU.add, reads=[T1.T(), T2.T()], writes=[KRO.T()])
        QT = [A.alloc("qt%d" % i, [128, L], BF16) for i in range(2)]
        KT = [A.alloc("kt%d" % i, [128, L], BF16) for i in range(2)]
        VA = [A.alloc("va%d" % i, [128, 17, 128], BF16) for i in range(2)]
        PTB = [A.alloc("ptb%d" % i, [128, 512], BF16) for i in range(3)]
        RC = A.alloc("rc", [128, 512], F32)
        E("pool", "memset", VA[0][:, :, 64:128], 1.0, writes=[VA[0].T()])
        E("pool", "memset", VA[1][:, :, 0:64], 1.0, writes=[VA[1].T()])
        scale = 96.0 ** -0.5
        pti = 0
        poi = 0
        self.bank_list = [0, 1, 2, 3, 4, 5]
        self.bank_i = 0
        RC2 = [RC, A.alloc("rc2", [128, 512], F32)]
        st = {"pti": 0, "poi": 0}

        def proj_head(h):
            par = h % 2
            qt, kt, va = QT[par], KT[par], VA[par]
            vo = 0 if par == 0 else 64
            wq, wqt = self.load_w(d["w_uq"][l][:, h * 96:(h + 1) * 96], 2, 96)
            ws, wst = self.load_w(d["w_uq_sw"][l][:, h * 96:(h + 1) * 96], 2, 96)
            wk, wkt = self.load_w(d["w_ukv"][l][:, h * 128:h * 128 + 64], 2, 64)
            wv, wvt = self.load_w(d["w_ukv"][l][:, h * 128 + 64:h * 128 + 128], 2, 64)
            for ni, (n0, n1) in enumerate(NT):
                n = n1 - n0
                pa, pat = self.bank()
                pb, pbt = self.bank()
                pc, pct = self.bank()
                for k in range(2):
                    E("pe", "matmul", pa[0:96, 0:n], wq[:, k, :], CN[:, k, n0:n1], start=(k == 0), stop=(k == 1), reads=[wqt, CN.T("all")], writes=[pat])
                for k in range(2):
                    E("pe", "matmul", pb[0:96, 0:n], ws[:, k, :], CN[:, k, n0:n1], start=(k == 0), stop=(k == 1), reads=[wst, CN.T("all")], writes=[pbt])
                for k in range(2):
                    E("pe", "matmul", pc[0:64, 0:n], wk[:, k, :], CN[:, 2 + k, n0:n1], start=(k == 0), stop=(k == 1), reads=[wkt, CN.T("all")], writes=[pct])
                E("act", "copy", qt[0:64, n0:n1], pa[0:64, 0:n], reads=[pat], writes=[qt.T()])
                E("dve", "tensor_tensor", T1[64:96, 0:n], pa[64:96, 0:n], self.TCOS[64:96, n0:n1], ALU.mult, reads=[pat, self.TCOS.T()], writes=[T1.T()])
                E("dve", "tensor_tensor", T2[64:96, 0:n], pb[64:96, 0:n], self.TSIN[64:96, n0:n1], ALU.mult, reads=[pbt, self.TSIN.T()], writes=[T2.T()])
                E("dve", "tensor_tensor", qt[64:96, n0:n1], T1[64:96, 0:n], T2[64:96, 0:n], ALU.add, reads=[T1.T(), T2.T()], writes=[qt.T()])
                E("act", "copy", kt[0:64, n0:n1], pc[0:64, 0:n], reads=[pct], writes=[kt.T()])
            E("pool", "tensor_copy", kt[64:96, :], KRO[64:96, :], reads=[KRO.T()], writes=[kt.T()])
            for ti, (c0, sz) in enumerate(TOKT):
                pv, pvt = self.bank()
                for k in range(2):
                    E("pe", "matmul", pv[0:sz, 0:64], CN[:, 2 + k, c0:c0 + sz], wv[:, k, :], start=(k == 0), stop=(k == 1), reads=[wvt, CN.T("all")], writes=[pvt])
                E("act", "copy", va[0:sz, ti, vo:vo + 64], pv[0:sz, 0:64], reads=[pvt], writes=[va.T()])

        def attn_head(h):
            par = h % 2
            qt, kt, va = QT[par], KT[par], VA[par]
            items = []
            for ni, (q0, q1) in enumerate(NT):
                keys = [(ti, c0, sz) for ti, (c0, sz) in enumerate(TOKT) if c0 < q1]
                pbk = 6 + (st["poi"] % 2)
                st["poi"] += 1
                for ji, (ti, c0, sz) in enumerate(keys):
                    items.append(dict(q0=q0, q1=q1, ti=ti, c0=c0, sz=sz, first=(ji == 0), last=(ji == len(keys) - 1), pbk=pbk))

            def emit_S(it):
                qa = max(it["q0"], it["c0"])
                it["qa"] = qa
                it["nq"] = it["q1"] - qa
                it["ps"], it["pst"] = self.bank()
                E("pe", "matmul", it["ps"][0:it["sz"], 0:it["nq"]], kt[0:96, it["c0"]:it["c0"] + it["sz"]], qt[0:96, qa:it["q1"]], start=True, stop=True,
                  reads=[kt.T(), qt.T()], writes=[it["pst"]])
            for i in range(min(2, len(items))):
                emit_S(items[i])
            for i, it in enumerate(items):
                sz, nq, qa, q0, q1 = it["sz"], it["nq"], it["qa"], it["q0"], it["q1"]
                ptb = PTB[st["pti"] % 3]
                st["pti"] += 1
                po, pot = self.P[it["pbk"]], self.PT[it["pbk"]]
                E("act", "activation", ptb[0:sz, 0:nq], it["ps"][0:sz, 0:nq], AF.Exp, scale=scale, reads=[it["pst"]], writes=[ptb.T()])
                if it["c0"] >= q0:
                    E("dve", "tensor_tensor", ptb[0:sz, 0:sz], ptb[0:sz, 0:sz], self.TRI[0:sz, 0:sz], ALU.mult,
                      reads=[ptb.T(), self.TRI.T()], writes=[ptb.T()])
                if i + 2 < len(items):
                    emit_S(items[i + 2])
                E("pe", "matmul", po[:, qa - q0:q1 - q0], va[0:sz, it["ti"], :], ptb[0:sz, 0:nq], start=it["first"], stop=it["last"],
                  reads=[va.T(), ptb.T()], writes=[pot])
                if it["last"]:
                    n = q1 - q0
                    rc = RC2[it["pbk"] % 2]
                    if par == 0:
                        E("dve", "reciprocal", rc[0:64, 0:n], po[64:128, 0:n], reads=[pot], writes=[rc.T()])
                        E("dve", "tensor_tensor", BRM[0:64, h // 2, q0:q1], po[0:64, 0:n], rc[0:64, 0:n], ALU.mult, reads=[pot, rc.T()], writes=[BRM.T("all")])
                    else:
                        E("dve", "reciprocal", rc[64:128, 0:n], po[0:64, 0:n], reads=[pot], writes=[rc.T()])
                        E("dve", "tensor_tensor", BRM[64:128, h // 2, q0:q1], po[64:128, 0:n], rc[64:128, 0:n], ALU.mult, reads=[pot, rc.T()], writes=[BRM.T("all")])

        proj_head(0)
        for h in range(8):
            if h + 1 < 8:
                proj_head(h + 1)
            attn_head(h)
        self.bank_list = [0, 1, 2, 3, 4, 5, 6, 7]
        self.bank_i = 0
        fw.barrier()
        A.release(m0)

    def exp_small(self, dst, x, tmp, F, halv=6):
        E = self.E
        E("dve", "tensor_scalar", tmp[:, 0:F], x[:, 0:F], 1.0 / (2 ** halv), None, ALU.mult, reads=[x.T()], writes=[tmp.T()])
        E("dve", "tensor_scalar", dst[:, 0:F], tmp[:, 0:F], 1.0 / 6, 1.0, ALU.mult, ALU.add, reads=[tmp.T()], writes=[dst.T()])
        for kdiv in (5.0, 4.0, 3.0, 2.0, 1.0):
            E("dve", "tensor_tensor", dst[:, 0:F], dst[:, 0:F], tmp[:, 0:F], ALU.mult, reads=[dst.T(), tmp.T()], writes=[dst.T()])
            E("dve", "tensor_scalar", dst[:, 0:F], dst[:, 0:F], 1.0 / kdiv, 1.0, ALU.mult, ALU.add, reads=[dst.T()], writes=[dst.T()])
        for _ in range(halv):
            E("dve", "tensor_tensor", dst[:, 0:F], dst[:, 0:F], dst[:, 0:F], ALU.mult, reads=[dst.T()], writes=[dst.T()])

    def s5_params(self, l, sfx, F):
        fw, A, d, E = self.fw, self.A, self.d, self.E
        t = {}
        for n in ("lr", "li", "dt", "rho", "th", "cr", "ci", "t1", "t2", "t3", "sn", "cs"):
            t[n] = A.alloc("s5p_" + n, [128, F], F32)
        fw.dma("sp", t["lr"][:], d["lamre_" + sfx][l], writes=[t["lr"].T()])
        fw.dma("sp", t["li"][:], d["lamim_" + sfx][l], writes=[t["li"].T()])
        fw.dma("sp", t["t1"][:], d["logdt_" + sfx][l], writes=[t["t1"].T()])
        E("dve", "tensor_scalar_min", t["lr"][:], t["lr"][:], -1e-4, reads=[t["lr"].T()], writes=[t["lr"].T()])
        self.exp_small(t["dt"], t["t1"], t["t2"], F)
        E("dve", "tensor_tensor", t["t1"][:], t["lr"][:], t["dt"][:], ALU.mult, reads=[t["lr"].T(), t["dt"].T()], writes=[t["t1"].T()])
        self.exp_small(t["rho"], t["t1"], t["t2"], F, halv=0)
        E("dve", "tensor_tensor", t["th"][:], t["li"][:], t["dt"][:], ALU.mult, reads=[t["li"].T(), t["dt"].T()], writes=[t["th"].T()])
        E("dve", "tensor_copy", t["t1"][:], t["th"][:], reads=[t["th"].T()], writes=[t["t1"].T()])
        self.sincos(t["t1"], t["t2"], t["sn"], t["cs"], F)
        E("dve", "tensor_tensor", t["cs"][:], t["cs"][:], t["rho"][:], ALU.mult, reads=[t["cs"].T(), t["rho"].T()], writes=[t["cs"].T()])
        E("dve", "tensor_tensor", t["sn"][:], t["sn"][:], t["rho"][:], ALU.mult, reads=[t["sn"].T(), t["rho"].T()], writes=[t["sn"].T()])
        E("dve", "tensor_tensor", t["t1"][:], t["lr"][:], t["lr"][:], ALU.mult, reads=[t["lr"].T()], writes=[t["t1"].T()])
        E("dve", "tensor_tensor", t["t2"][:], t["li"][:], t["li"][:], ALU.mult, reads=[t["li"].T()], writes=[t["t2"].T()])
        E("dve", "tensor_tensor", t["t1"][:], t["t1"][:], t["t2"][:], ALU.add, reads=[t["t1"].T(), t["t2"].T()], writes=[t["t1"].T()])
        E("dve", "reciprocal", t["t1"][:], t["t1"][:], reads=[t["t1"].T()], writes=[t["t1"].T()])
        E("dve", "tensor_scalar", t["t2"][:], t["cs"][:], -1.0, None, ALU.add, reads=[t["cs"].T()], writes=[t["t2"].T()])
        E("dve", "tensor_tensor", t["cr"][:], t["t2"][:], t["lr"][:], ALU.mult, reads=[t["t2"].T(), t["lr"].T()], writes=[t["cr"].T()])
        E("dve", "tensor_tensor", t["t3"][:], t["sn"][:], t["li"][:], ALU.mult, reads=[t["sn"].T(), t["li"].T()], writes=[t["t3"].T()])
        E("dve", "tensor_tensor", t["cr"][:], t["cr"][:], t["t3"][:], ALU.add, reads=[t["cr"].T(), t["t3"].T()], writes=[t["cr"].T()])
        E("dve", "tensor_tensor", t["cr"][:], t["cr"][:], t["t1"][:], ALU.mult, reads=[t["cr"].T(), t["t1"].T()], writes=[t["cr"].T()])
        E("dve", "tensor_tensor", t["ci"][:], t["sn"][:], t["lr"][:], ALU.mult, reads=[t["sn"].T(), t["lr"].T()], writes=[t["ci"].T()])
        E("dve", "tensor_tensor", t["t3"][:], t["t2"][:], t["li"][:], ALU.mult, reads=[t["t2"].T(), t["li"].T()], writes=[t["t3"].T()])
        E("dve", "tensor_tensor", t["ci"][:], t["ci"][:], t["t3"][:], ALU.subtract, reads=[t["ci"].T(), t["t3"].T()], writes=[t["ci"].T()])
        E("dve", "tensor_tensor", t["ci"][:], t["ci"][:], t["t1"][:], ALU.mult, reads=[t["ci"].T(), t["t1"].T()], writes=[t["ci"].T()])
        return t

    def s5(self, s, l, W, BRS):
        fw, A, d, E = self.fw, self.A, self.d, self.E
        m0 = A.mark()
        def cons_u(mi, ni, pb, pt, n0, n1):
            E("act", "copy", BRS[:, mi, n0:n1], pb[:, 0:n1 - n0], reads=[pt], writes=[BRS.T("all")])
        self.proj([(W[:, 544 + m * 128:544 + (m + 1) * 128], 8, 128) for m in range(4)], self.act_hi, cons_u)
        BBR = A.alloc("bbr", [128, 16, 128], BF16)
        BBI = A.alloc("bbi", [128, 16, 128], BF16)
        CRE = A.alloc("cre", [128, 16, 128], BF16)
        CNR = A.alloc("cnr", [128, 16, 128], BF16)
        CNI = A.alloc("cni", [128, 16, 128], BF16)
        WG = A.alloc("wglu", [128, 4, 512], BF16)
        DSK = A.alloc("dsk", [128, 4], F32)
        RHO = A.alloc("rho", [128, 16], F32)
        THI = A.alloc("thi", [128, 16], F32)
        TLO = A.alloc("tlo", [128, 16], F32)
        fw.dma("pool", CRE[:].rearrange("p j c -> p (j c)"), d["cre_pad"][l], writes=[CRE.T()])
        fw.dma("pool", CNI[:].rearrange("p j c -> p (j c)"), d["cim_pad"][l], writes=[CNI.T()])
        fw.dma("pool", WG[:], d["w_glu"][l].rearrange("(k p) c -> p k c", p=128), writes=[WG.T()])
        fw.dma("sp", DSK[:], d["s5_d"][l], writes=[DSK.T()])
        E("dve", "tensor_scalar", CNR[:], CRE[:], -1.0, None, ALU.mult, reads=[CRE.T()], writes=[CNR.T()])
        E("dve", "tensor_scalar", CNI[:], CNI[:], -1.0, None, ALU.mult, reads=[CNI.T()], writes=[CNI.T()])
        m1 = A.mark()
        pc = self.s5_params(l, "c", 256)
        BRE = A.alloc("bre", [128, 256], F32)
        BIM = A.alloc("bim", [128, 256], F32)
        BT1 = A.alloc("bt1", [128, 256], F32)
        BT2 = A.alloc("bt2", [128, 256], F32)
        BMASK = A.alloc("bmask", [128, 32], F32)
        fw.dma("sp", BRE[:], d["bre_c"][l], writes=[BRE.T()])
        fw.dma("sp", BIM[:], d["bim_c"][l], writes=[BIM.T()])
        fw.dma("sp", BMASK[:], d["c_bmask"], writes=[BMASK.T()])
        E("dve", "tensor_tensor", BT1[:], pc["cr"][:], BRE[:], ALU.mult, reads=[pc["cr"].T(), BRE.T()], writes=[BT1.T()])
        E("dve", "tensor_tensor", BT2[:], pc["ci"][:], BIM[:], ALU.mult, reads=[pc["ci"].T(), BIM.T()], writes=[BT2.T()])
        E("dve", "tensor_tensor", BT1[:], BT1[:], BT2[:], ALU.subtract, reads=[BT1.T(), BT2.T()], writes=[BT1.T()])
        E("dve", "tensor_tensor", BT2[:], pc["cr"][:], BIM[:], ALU.mult, reads=[pc["cr"].T(), BIM.T(), BT2.T()], writes=[BT2.T()])
        E("dve", "tensor_tensor", BRE[:], pc["ci"][:], BRE[:], ALU.mult, reads=[pc["ci"].T(), BRE.T()], writes=[BRE.T()])
        E("dve", "tensor_tensor", BT2[:], BT2[:], BRE[:], ALU.add, reads=[BT2.T(), BRE.T()], writes=[BT2.T()])
        for j in range(16):
            jc = j // 4
            for gl in range(2):
                mcol = BMASK[:, 2 * j + gl:2 * j + gl + 1]
                E("dve", "tensor_scalar", BBR[:, j, gl * 64:(gl + 1) * 64], BT1[:, jc * 64:(jc + 1) * 64], mcol, None, ALU.mult,
                  reads=[BT1.T(), BMASK.T()], writes=[BBR.T()])
                E("dve", "tensor_scalar", BBI[:, j, gl * 64:(gl + 1) * 64], BT2[:, jc * 64:(jc + 1) * 64], mcol, None, ALU.mult,
                  reads=[BT2.T(), BMASK.T()], writes=[BBI.T()])
        fw.barrier()
        A.release(m1)
        ps_ = self.s5_params(l, "s", 16)
        E("dve", "tensor_copy", RHO[:], ps_["rho"][:], reads=[ps_["rho"].T()], writes=[RHO.T()])
        E("dve", "tensor_single_scalar", THI[:].bitcast(I32), ps_["th"][:].bitcast(I32), -4096, ALU.bitwise_and, reads=[ps_["th"].T()], writes=[THI.T()])
        E("dve", "tensor_tensor", TLO[:], ps_["th"][:], THI[:], ALU.subtract, reads=[ps_["th"].T(), THI.T()], writes=[TLO.T()])
        fw.barrier()
        A.release(m1)
        self.dbg("s5rho%d" % l, RHO[:], [], [128, 16])
        self.dbg("s5bbr%d" % l, BBR[:, :, :], [], [128, 16, 128])
        TAU = A.alloc("tau", [128, TC + 1], F32)
        fw.dma("sp", TAU[:], d["c_tau"], writes=[TAU.T()])
        COS = A.alloc("s5cos", [128, TC + 1], F32)
        SIN = A.alloc("s5sin", [128, TC + 1], F32)
        ANG = A.alloc("s5ang", [128, TC + 1], F32)
        KK = A.alloc("s5kk", [128, TC + 1], F32)
        RHOB = A.alloc("rhob", [128, TC], F32)
        W1 = [A.alloc("s5w%d" % i, [128, TC], F32) for i in range(6)]
        WRI = [[A.alloc("s5wr%d_%d" % (b_, i), [128, TC], F32) for i in range(2)] for b_ in range(2)]
        PBB = [[A.alloc("s5p%d_%d" % (b_, i), [128, TC], BF16) for i in range(4)] for b_ in range(2)]
        BU = [[A.alloc("s5bu%d_%d" % (b_, i), [128, TC], BF16) for i in range(2)] for b_ in range(2)]
        TB = [A.alloc("s5tb%d" % i, [128, TC], BF16) for i in range(4)]
        cnt = [0]
        cntb = [0]
        COSB = A.alloc("s5cosb", [128, TC], BF16)
        SINB = A.alloc("s5sinb", [128, TC], BF16)
        WRB = [[A.alloc("s5wrb%d_%d" % (b_, i), [128, TC], BF16) for i in range(2)] for b_ in range(2)]
        INI = A.alloc("s5ini", [128, 4], F32)
        INI2 = Buf(INI[:, 2:4], "ini2")
        CS2 = A.alloc("s5cs2", [128, 2], F32)
        NSC = A.alloc("s5nsc", [128, 2], F32)
        YF = A.alloc("s5y", [128, TC], F32)
        YT1 = A.alloc("s5yt1", [128, TC], F32)
        n = TC
        for jc in range(4):
            for j in range(4 * jc, 4 * jc + 4):
                E("dve", "tensor_scalar", ANG[:], TAU[:], THI[:, j:j + 1], None, ALU.mult, reads=[TAU.T(), THI.T(), ANG.T()], writes=[ANG.T()])
                E("dve", "tensor_scalar", KK[:], ANG[:], 1.0 / TWO_PI, MAGIC, ALU.mult, ALU.add, reads=[ANG.T(), KK.T()], writes=[KK.T()])
                E("dve", "tensor_scalar", KK[:], KK[:], -MAGIC, None, ALU.add, reads=[KK.T()], writes=[KK.T()])
                E("dve", "scalar_tensor_tensor", ANG[:], KK[:], -C1, ANG[:], ALU.mult, ALU.add, reads=[KK.T(), ANG.T()], writes=[ANG.T()])
                E("dve", "scalar_tensor_tensor", ANG[:], KK[:], -C2, ANG[:], ALU.mult, ALU.add, reads=[KK.T(), ANG.T()], writes=[ANG.T()])
                E("dve", "scalar_tensor_tensor", ANG[:], TAU[:], TLO[:, j:j + 1], ANG[:], ALU.mult, ALU.add, reads=[TAU.T(), TLO.T(), ANG.T()], writes=[ANG.T()])
                E("dve", "tensor_scalar", ANG[:], ANG[:], -math.pi, math.pi, ALU.max, ALU.min, reads=[ANG.T()], writes=[ANG.T()])
                E("act", "activation", SIN[:], ANG[:], AF.Sin, reads=[ANG.T()], writes=[SIN.T()])
                E("act", "activation", KK[:], ANG[:], AF.Abs, reads=[ANG.T()], writes=[KK.T()])
                E("act", "activation", COS[:], KK[:], AF.Sin, bias=self.HALFPI[:, 0:1], scale=-1.0, reads=[KK.T(), self.HALFPI.T()], writes=[COS.T()])
                E("dve", "tensor_scalar", RHOB[:], self.ONESF[:, 0:TC], RHO[:, j:j + 1], None, ALU.mult, reads=[RHO.T(), self.ONESF.T()], writes=[RHOB.T()])
                E("dve", "memset", INI[:], 0.0, writes=[INI.T()])
                E("dve", "tensor_copy", CS2[:, 0:1], COS[:, TC:TC + 1], reads=[COS.T(), CS2.T()], writes=[CS2.T()])
                E("dve", "tensor_copy", CS2[:, 1:2], SIN[:, TC:TC + 1], reads=[SIN.T(), CS2.T()], writes=[CS2.T()])
                E("dve", "tensor_scalar", NSC[:, 0:1], SIN[:, TC:TC + 1], -1.0, None, ALU.mult, reads=[SIN.T(), NSC.T()], writes=[NSC.T()])
                E("dve", "tensor_copy", NSC[:, 1:2], COS[:, TC:TC + 1], reads=[COS.T(), NSC.T()], writes=[NSC.T()])
                def emitB(cc, j=j, jc=jc):
                    cs_, ce_ = S5CH[cc]
                    pr, prt = self.P[6], self.PT[6]
                    pi_, pit = self.P[7], self.PT[7]
                    E("pe", "matmul", pr[:, 0:n], BBR[:, j, :], BRS[:, jc, cs_:ce_], start=True, stop=True, reads=[BBR.T(), BRS.T((jc, cc)), BRS.T("all")], writes=[prt])
                    E("pe", "matmul", pi_[:, 0:n], BBI[:, j, :], BRS[:, jc, cs_:ce_], start=True, stop=True, reads=[BBI.T(), BRS.T((jc, cc)), BRS.T("all")], writes=[pit])
                    bur_, bui_ = BU[(cntb[0]) % 2]
                    cntb[0] += 1
                    E("act", "copy", bur_[:], pr[:, 0:n], reads=[prt], writes=[bur_.T()])
                    E("act", "copy", bui_[:], pi_[:, 0:n], reads=[pit], writes=[bui_.T()])
                emitB(0)
                E("act", "copy", COSB[:], COS[:, 0:n], reads=[COS.T(), COSB.T()], writes=[COSB.T()])
                E("act", "copy", SINB[:], SIN[:, 0:n], reads=[SIN.T(), SINB.T()], writes=[SINB.T()])

                def front(c, j=j):
                    t1, t2, t3, t4, rr, ri = W1
                    k_ = cnt[0]
                    cnt[0] += 1
                    wr, wi = WRI[k_ % 2]
                    wrb, wib = WRB[k_ % 2]
                    bur, bui = BU[k_ % 2]
                    t1, t2, t3, t4 = TB
                    E("dve", "tensor_tensor", t1[:], bur[:], COSB[:], ALU.mult, reads=[bur.T(), COSB.T()], writes=[t1.T()])
                    E("dve", "tensor_tensor", t2[:], bui[:], SINB[:], ALU.mult, reads=[bui.T(), SINB.T()], writes=[t2.T()])
                    E("dve", "tensor_tensor", t3[:], bui[:], COSB[:], ALU.mult, reads=[bui.T(), COSB.T()], writes=[t3.T()])
                    E("dve", "tensor_tensor", t4[:], bur[:], SINB[:], ALU.mult, reads=[bur.T(), SINB.T()], writes=[t4.T()])
                    E("dve", "tensor_tensor", rr[:], t1[:], t2[:], ALU.add, reads=[t1.T(), t2.T()], writes=[rr.T()])
                    E("dve", "tensor_tensor", ri[:], t3[:], t4[:], ALU.subtract, reads=[t3.T(), t4.T()], writes=[ri.T()])
                    E("dve", "tensor_tensor_scan", wr[:], RHOB[:], rr[:], INI[:, 0:1], ALU.mult, ALU.add, reads=[RHOB.T(), rr.T(), INI.T()], writes=[wr.T()])
                    E("dve", "tensor_tensor_scan", wi[:], RHOB[:], ri[:], INI[:, 1:2], ALU.mult, ALU.add, reads=[RHOB.T(), ri.T(), INI.T()], writes=[wi.T()])
                    E("act", "activation", INI[:, 2:4], NSC[:, 0:2], AF.Copy, scale=wi[:, n - 1:n], reads=[wi.T(), NSC.T()], writes=[INI2.T()])
                    E("act", "activation", INI[:, 0:1], CS2[:, 0:1], AF.Identity, bias=INI[:, 2:3], scale=wr[:, n - 1:n], reads=[wr.T(), CS2.T(), INI2.T()], writes=[INI.T()])
                    E("act", "activation", INI[:, 1:2], CS2[:, 1:2], AF.Identity, bias=INI[:, 3:4], scale=wr[:, n - 1:n], reads=[wr.T(), CS2.T(), INI2.T()], writes=[INI.T()])
                    E("act", "copy", wrb[:], wr[:], reads=[wr.T(), wrb.T()], writes=[wrb.T()])
                    E("act", "copy", wib[:], wi[:], reads=[wi.T(), wib.T()], writes=[wib.T()])
                    return k_

                def back(c, k_, j=j, jc=jc):
                    wrb, wib = WRB[k_ % 2]
                    PB_ = PBB[k_ % 2]
                    E("dve", "tensor_tensor", PB_[0][:], COSB[:], wrb[:], ALU.mult, reads=[COSB.T(), wrb.T()], writes=[PB_[0].T()])
                    E("dve", "tensor_tensor", PB_[1][:], SINB[:], wib[:], ALU.mult, reads=[SINB.T(), wib.T()], writes=[PB_[1].T()])
                    E("dve", "tensor_tensor", PB_[2][:], SINB[:], wrb[:], ALU.mult, reads=[SINB.T(), wrb.T()], writes=[PB_[2].T()])
                    E("dve", "tensor_tensor", PB_[3][:], COSB[:], wib[:], ALU.mult, reads=[COSB.T(), wib.T()], writes=[PB_[3].T()])
                    py, pyt = self.P[c], self.PT[c]
                    first = (j == 4 * jc)
                    last = (j == 4 * jc + 3)
                    for q, (cm, pbuf) in enumerate(((CRE, PB_[0]), (CNR, PB_[1]), (CNI, PB_[2]), (CNI, PB_[3]))):
                        E("pe", "matmul", py[:, 0:n], cm[:, j, :], pbuf[:], start=(first and q == 0), stop=(last and q == 3),
                          reads=[cm.T(), pbuf.T()], writes=[pyt])
                prev = None
                for c, (cs, ce) in enumerate(S5CH):
                    if c + 1 < len(S5CH):
                        emitB(c + 1)
                    k_ = front(c)
                    if prev is not None:
                        back(*prev)
                    prev = (c, k_)
                back(*prev)
            for c, (cs, ce) in enumerate(S5CH):
                py, pyt = self.P[c], self.PT[c]
                E("dve", "scalar_tensor_tensor", YF[:], BRS[:, jc, cs:ce], DSK[:, jc:jc + 1], py[:, 0:n], ALU.mult, ALU.add,
                  reads=[pyt, BRS.T((jc, c)), BRS.T("all"), DSK.T()], writes=[YF.T()])
                E("act", "activation", YT1[:], YF[:], AF.Square, reads=[YF.T()], writes=[YT1.T()])
                E("dve", "tensor_scalar", YT1[:], YT1[:], 0.044715, 1.0, ALU.mult, ALU.add, reads=[YT1.T()], writes=[YT1.T()])
                E("dve", "tensor_tensor", YT1[:], YT1[:], YF[:], ALU.mult, reads=[YT1.T(), YF.T()], writes=[YT1.T()])
                E("act", "activation", YT1[:], YT1[:], AF.Sigmoid, scale=GELU_K, reads=[YT1.T()], writes=[YT1.T()])
                E("dve", "tensor_tensor", BRS[:, jc, cs:ce], YF[:], YT1[:], ALU.mult, reads=[YT1.T(), YF.T()], writes=[BRS.T((jc, c)), BRS.T("all")])
        fw.barrier()
        BRS.tiles = {}
        self.dbg("s5yg%d" % l, BRS[:, :, :], [], [128, 4, L])
        SGT = A.alloc("s5sg", [128, 4, 512], BF16)
        for ni, (n0, n1) in enumerate(NT):
            nn = n1 - n0
            for m in range(4):
                pb, pt = self.bank()
                for k in range(4):
                    E("pe", "matmul", pb[:, 0:nn], WG[:, k, m * 128:(m + 1) * 128], BRS[:, k, n0:n1], start=(k == 0), stop=(k == 3),
                      reads=[WG.T(), BRS.T("all")], writes=[pt])
                E("act", "activation", SGT[:, m, 0:nn], pb[:, 0:nn], AF.Sigmoid, reads=[pt], writes=[SGT.T()])
            for m in range(4):
                E("dve", "tensor_tensor", BRS[:, m, n0:n1], BRS[:, m, n0:n1], SGT[:, m, 0:nn], ALU.mult, reads=[SGT.T(), BRS.T("all")], writes=[BRS.T("all")])
        fw.barrier()
        BRS.tiles = {}
        A.release(m0)

    def hgrn(self, s, l, W, BRH):
        fw, A, d, E = self.fw, self.A, self.d, self.E
        m0 = A.mark()
        FK = A.alloc("hgFK", [128, 2 * L], F32)
        Fb = Buf(FK[:, 0:L], "hgF")
        Kb = Buf(FK[:, L:2 * L], "hgK")
        XS = Buf(FK[:, 0:4096].rearrange("p (i v) -> p i v", v=128), "hgXS")
        Fb.tiles = FK.tiles
        Kb.tiles = FK.tiles
        XS.tiles = FK.tiles
        CUM = A.alloc("hgC", [128, L], F32)
        SBA = Buf(CUM[:, 0:2048].bitcast(BF16).rearrange("p (i v) -> p i v", v=128), "hgSBA")
        SBA.tiles = CUM.tiles
        Eb = A.alloc("hgE", [128, L], F32)
        QT = A.alloc("hgq", [128, L], BF16)
        KT = A.alloc("hgk", [128, L], BF16)
        SGt = A.alloc("hgsg", [128, L], BF16)
        VT = A.alloc("hgv", [64, 33, 128], BF16)
        REF = A.alloc("hgref", [128, 33], F32)
        DD = A.alloc("hgdd", [128, 32], F32)
        ATM = [A.alloc("hgatm%d" % i, [64, 64], BF16) for i in range(3)]
        KTOK = [A.alloc("hgktok%d" % i, [64, 128], BF16) for i in range(3)]
        ON = [A.alloc("hgon%d" % i, [64, 128], BF16) for i in range(3)]
        JUNK = A.alloc("hgjunk", [64, 128], F32)
        SSL = [A.alloc("hgss%d" % i, [64, 2], F32) for i in range(4)]
        base = 1056
        for h in range(4):
            lb = self.LBT[:, l, 0, h:h + 1]
            oml = self.LBT[:, l, 1, h:h + 1]
            def cons_zf(mi, ni, pb, pt, n0, n1):
                E("act", "activation", Fb[:, n0:n1], pb[:, 0:n1 - n0], AF.Sigmoid, reads=[pt], writes=[Fb.T()])
            self.proj([(W[:, base + 512 + h * 128: base + 512 + (h + 1) * 128], 8, 128)], self.act_hi, cons_zf)
            def cons_g(mi, ni, pb, pt, n0, n1):
                E("act", "activation", SGt[:, n0:n1], pb[:, 0:n1 - n0], AF.Silu, reads=[pt], writes=[SGt.T()])
            self.proj([(W[:, base + 1536 + h * 128: base + 1536 + (h + 1) * 128], 8, 128)], self.act_hi, cons_g)
            wv, wvt = self.load_w(W[:, base + 1024 + h * 128: base + 1024 + (h + 1) * 128], 8, 128)
            for i, (c0, sz) in enumerate(HCH):
                pv, pvt = self.bank()
                for k in range(8):
                    E("pe", "matmul", pv[0:sz, 0:128], self.HI[:, k, c0:c0 + sz], wv[:, k, :], start=(k == 0), stop=(k == 7), reads=[wvt, self.HI.T("all")], writes=[pvt])
                E("act", "copy", VT[0:sz, i, :], pv[0:sz, 0:128], reads=[pvt], writes=[VT.T()])
            E("dve", "tensor_scalar", Fb[:], Fb[:], oml, lb, ALU.mult, ALU.add, reads=[Fb.T(), self.LBT.T()], writes=[Fb.T()])
            E("dve", "tensor_scalar", Kb[:], Fb[:], -1.0, 1.0, ALU.mult, ALU.add, reads=[Fb.T(), Kb.T()], writes=[Kb.T()])
            E("dve", "tensor_scalar_max", Fb[:], Fb[:], 1e-6, reads=[Fb.T()], writes=[Fb.T()])
            E("act", "activation", Fb[:], Fb[:], AF.Ln, reads=[Fb.T()], writes=[Fb.T()])
            E("dve", "tensor_tensor_scan", CUM[:], self.ONESF[:, 0:L], Fb[:], 0.0, ALU.mult, ALU.add, reads=[Fb.T(), self.ONESF.T(), CUM.T()], writes=[CUM.T()])
            E("dve", "tensor_copy", REF[:, 0:1], CUM[:, 8:9], reads=[CUM.T(), REF.T()], writes=[REF.T()])
            E("dve", "tensor_copy", REF[:, 1:33], CUM[:, 48:L:64], reads=[CUM.T(), REF.T()], writes=[REF.T()])
            E("dve", "tensor_tensor", DD[:], REF[:, 1:33], REF[:, 0:32], ALU.subtract, reads=[REF.T(), DD.T()], writes=[DD.T()])
            E("act", "activation", DD[:], DD[:], AF.Exp, reads=[DD.T()], writes=[DD.T()])
            for i, (c0, sz) in enumerate(HCH):
                E("dve", "tensor_scalar", CUM[:, c0:c0 + sz], CUM[:, c0:c0 + sz], REF[:, i:i + 1], None, ALU.subtract, reads=[CUM.T(), REF.T()], writes=[CUM.T()])
            E("act", "activation", Eb[:], CUM[:], AF.Exp, reads=[CUM.T(), Eb.T()], writes=[Eb.T()])
            def cons_q(mi, ni, pb, pt, n0, n1):
                E("dve", "tensor_tensor", QT[:, n0:n1], pb[:, 0:n1 - n0], Eb[:, n0:n1], ALU.mult, reads=[pt, Eb.T()], writes=[QT.T()])
            self.proj([(W[:, base + h * 128: base + (h + 1) * 128], 8, 128)], self.act_hi, cons_q)
            E("act", "activation", Eb[:], CUM[:], AF.Exp, scale=-1.0, reads=[CUM.T(), Eb.T(), QT.T()], writes=[Eb.T()])
            E("dve", "tensor_tensor", KT[:], Kb[:], Eb[:], ALU.mult, reads=[Kb.T(), Eb.T(), KT.T()], writes=[KT.T()])
            NCH = len(HCH)

            def p1_tr(i):
                c0, sz = HCH[i]
                pk, pkt = self.bank()
                pkb = pk[:].bitcast(BF16)
                E("pe", "transpose", pkb[0:sz, 0:128], KT[:, c0:c0 + sz], self.IDB[:], reads=[KT.T(), self.IDB.T()], writes=[pkt])
                ktok = KTOK[i % 3]
                E("act", "copy", ktok[0:sz, :], pkb[0:sz, 0:128], reads=[pkt], writes=[ktok.T()])

            def p1_mm(i):
                c0, sz = HCH[i]
                ktok = KTOK[i % 3]
                pt_, ptt = self.bank()
                E("pe", "matmul", pt_[:, 0:128], ktok[0:sz, :], VT[0:sz, i, :], start=True, stop=True, reads=[ktok.T(), VT.T()], writes=[ptt])
                E("dve", "tensor_scalar", XS[:, i, :], pt_[:, 0:128], DD[:, i:i + 1], None, ALU.mult, reads=[ptt, DD.T(), XS.T()], writes=[XS.T()])
            for t in range(NCH):
                if t < NCH - 1:
                    p1_tr(t)
                if 1 <= t:
                    p1_mm(t - 1)
            for i in range(1, NCH - 1):
                E("dve", "scalar_tensor_tensor", XS[:, i, :], XS[:, i - 1, :], DD[:, i:i + 1], XS[:, i, :], ALU.mult, ALU.add,
                  reads=[XS.T(), DD.T()], writes=[XS.T()])
            for q in range(4):
                E("act", "copy", SBA[:, 8 * q:8 * q + 8, :], XS[:, 8 * q:8 * q + 8, :], reads=[XS.T(), SBA.T()], writes=[SBA.T()])
            stt = {}

            def stA(i):
                c0, sz = HCH[i]
                pa, pat = self.bank()
                atm = ATM[i % 3]
                E("pe", "matmul", pa[0:sz, 0:sz], KT[:, c0:c0 + sz], QT[:, c0:c0 + sz], start=True, stop=True, reads=[KT.T(), QT.T()], writes=[pat])
                E("dve", "tensor_tensor", atm[0:sz, 0:sz], pa[0:sz, 0:sz], self.TRI[0:sz, 0:sz], ALU.mult, reads=[pat, self.TRI.T()], writes=[atm.T()])

            def stB(i):
                c0, sz = HCH[i]
                atm, SS = ATM[i % 3], SSL[i % 4]
                po, pot = self.bank()
                stt[i] = (po, pot)
                E("pe", "matmul", po[0:sz, 0:128], atm[0:sz, 0:sz], VT[0:sz, i, :], start=True, stop=(i == 0), reads=[atm.T(), VT.T()], writes=[pot])
                if i > 0:
                    E("pe", "matmul", po[0:sz, 0:128], QT[:, c0:c0 + sz], SBA[:, i - 1, :], start=False, stop=True, reads=[QT.T(), SBA.T()], writes=[pot])
                E("dve", "memset", SS[:, 0:1], 0.0, reads=[SS.T()], writes=[SS.T()])
                E("act", "activation", JUNK[0:sz, :], po[0:sz, 0:128], AF.Square, accum_out=SS[0:sz, 0:1], reads=[pot, SS.T(), JUNK.T()], writes=[SS.T(), JUNK.T()])

            def stB2(i):
                c0, sz = HCH[i]
                on, SS = ON[i % 3], SSL[i % 4]
                po, pot = stt.pop(i)
                E("act", "activation", SS[0:sz, 1:2], SS[0:sz, 0:1], AF.Ln, bias=self.EPS6[0:sz, 0:1], scale=1.0 / 128, reads=[SS.T(), self.EPS6.T()], writes=[SS.T()])
                E("act", "activation", SS[0:sz, 1:2], SS[0:sz, 1:2], AF.Exp, scale=-0.5, reads=[SS.T()], writes=[SS.T()])
                E("dve", "tensor_scalar", on[0:sz, :], po[0:sz, 0:128], SS[0:sz, 1:2], None, ALU.mult, reads=[pot, SS.T(), on.T()], writes=[on.T()])

            def stC(i):
                c0, sz = HCH[i]
                on = ON[i % 3]
                pe_, pet = self.bank()
                peb = pe_[:].bitcast(BF16)
                E("pe", "transpose", peb[:, 0:sz], on[0:sz, :], self.IDB[0:sz, 0:sz], reads=[on.T(), self.IDB.T()], writes=[pet])
                E("dve", "scalar_tensor_tensor", BRH[:, h, c0:c0 + sz], peb[:, 0:sz], self.HGN[:, l, h:h + 1], SGt[:, c0:c0 + sz], ALU.mult, ALU.mult,
                  reads=[pet, self.HGN.T(), SGt.T()], writes=[BRH.T("all")])
            for t in range(NCH + 3):
                if t < NCH:
                    stA(t)
                if 0 <= t - 1 < NCH:
                    stB(t - 1)
                if 0 <= t - 2 < NCH:
                    stB2(t - 2)
                if 0 <= t - 3 < NCH:
                    stC(t - 3)
        fw.barrier()
        A.release(m0)


def host_layout(inp):
    f = np.float32
    o = {}

    def fm(v):
        return np.ascontiguousarray(np.asarray(v, f).reshape(8, 128).T)
    o["meta_tokens"] = np.ascontiguousarray(inp["meta_tokens"], f)
    o["ln_in_g"] = fm(inp["ln_in_g"]); o["ln_in_b"] = fm(inp["ln_in_b"])
    w_in = np.ascontiguousarray(inp["w_in"], f)
    o["w_in"] = w_in
    kr = np.zeros((2, 1024, 96), f); krs = np.zeros((2, 1024, 96), f)
    kr[:, :, 64:96] = w_in[:, :, 512:544]
    krs[:, :, 64:80] = w_in[:, :, 528:544]
    krs[:, :, 80:96] = w_in[:, :, 512:528]
    o["w_kr"] = kr; o["w_kr_sw"] = krs
    o["q_norm"] = np.ascontiguousarray(np.asarray(inp["mla_q_norm"], f).reshape(2, 2, 128).transpose(0, 2, 1))
    o["kv_norm"] = np.ascontiguousarray(np.asarray(inp["mla_kv_norm"], f).reshape(2, 2, 128).transpose(0, 2, 1))
    uq = np.asarray(inp["mla_w_uq"], f)
    o["w_uq"] = np.ascontiguousarray(uq)
    uqs = np.zeros_like(uq).reshape(2, 256, 8, 96)
    uq4 = uq.reshape(2, 256, 8, 96)
    uqs[:, :, :, 64:80] = uq4[:, :, :, 80:96]
    uqs[:, :, :, 80:96] = uq4[:, :, :, 64:80]
    o["w_uq_sw"] = np.ascontiguousarray(uqs.reshape(2, 256, 768))
    o["w_ukv"] = np.ascontiguousarray(inp["mla_w_ukv"], f)
    def sm(v):
        return np.ascontiguousarray(np.asarray(v, f).reshape(2, 16, 2, 64).transpose(0, 2, 3, 1).reshape(2, 128, 16))
    lam_re = np.asarray(inp["s5_lam_re"], f); lam_im = np.asarray(inp["s5_lam_im"], f)
    logdt = np.broadcast_to(np.asarray(inp["s5_log_dt"], f)[:, :, None], (2, 32, 64))
    o["lamre_s"] = sm(lam_re); o["lamim_s"] = sm(lam_im); o["logdt_s"] = sm(logdt)
    def cm(v):
        t = np.asarray(v, f).reshape(2, 4, 8, 1, 64)
        t = np.broadcast_to(t, (2, 4, 8, 16, 64))
        return np.ascontiguousarray(t.transpose(0, 2, 3, 1, 4).reshape(2, 128, 256))
    o["lamre_c"] = cm(lam_re); o["lamim_c"] = cm(lam_im); o["logdt_c"] = cm(logdt)
    def bcm(v):
        t = np.asarray(v, f).reshape(2, 4, 8, 64, 16)
        return np.ascontiguousarray(t.transpose(0, 2, 4, 1, 3).reshape(2, 128, 256))
    o["bre_c"] = bcm(inp["s5_b_re"]); o["bim_c"] = bcm(inp["s5_b_im"])
    def cpad(v):
        v = np.asarray(v, f)
        out = np.zeros((2, 128, 16, 128), f)
        for j in range(16):
            for gl in range(2):
                g = 2 * j + gl
                g8 = g % 8
                out[:, gl * 64:(gl + 1) * 64, j, g8 * 16:(g8 + 1) * 16] = v[:, g].transpose(0, 2, 1)
        return np.ascontiguousarray(out.reshape(2, 128, 2048))
    o["cre_pad"] = cpad(inp["s5_c_re"]); o["cim_pad"] = cpad(inp["s5_c_im"])
    o["s5_d"] = np.ascontiguousarray(np.asarray(inp["s5_d"], f).reshape(2, 4, 128).transpose(0, 2, 1))
    o["w_glu"] = np.ascontiguousarray(inp["s5_w_glu"], f)
    o["lb_logits"] = np.ascontiguousarray(np.asarray(inp["hg_lb_logits"], f).reshape(2, 4, 128).transpose(0, 2, 1))
    o["hg_norm"] = np.ascontiguousarray(np.asarray(inp["hg_out_norm"], f).reshape(2, 4, 128).transpose(0, 2, 1))
    for n in ("w_br_mla", "w_br_s5", "w_br_hg", "w_out", "w_ffn_gate", "w_ffn_up", "w_ffn_down"):
        o[n] = np.ascontiguousarray(inp[n], f)
    for n in ("ln1_g", "ln1_b", "ln2_g", "ln2_b"):
        o[n] = np.ascontiguousarray(np.asarray(inp[n], f).reshape(2, 8, 128).transpose(0, 2, 1))
    inv = (10000.0 ** (-(np.arange(0, 32, 2, dtype=np.float32) / 32))).astype(f)
    cinv = np.zeros((128, 1), f); csgn = np.zeros((128, 1), f)
    cinv[64:80, 0] = inv; cinv[80:96, 0] = inv
    csgn[64:80, 0] = -1.0; csgn[80:96, 0] = 1.0
    o["c_inv"] = cinv; o["c_sgn"] = csgn
    o["c_tau"] = np.ascontiguousarray(np.broadcast_to(np.arange(TC + 1, dtype=f)[None], (128, TC + 1)))
    o["c_metapos"] = np.ascontiguousarray(np.broadcast_to(np.arange(16, dtype=f)[None], (128, 16)))
    bm = np.zeros((128, 32), f)
    for j in range(16):
        for gl in range(2):
            g8 = (2 * j + gl) % 8
            bm[g8 * 16:(g8 + 1) * 16, 2 * j + gl] = 1.0
    o["c_bmask"] = bm
    return o


_CACHE = {}


def kernel(**inputs):
    n_cores = 8
    nseq = 2
    shared = host_layout(inputs)
    x = np.ascontiguousarray(inputs["x"], np.float32)
    pos = np.ascontiguousarray(inputs["positions"], np.int32)
    if "nc" not in _CACHE:
        _CACHE["nc"] = Builder(nseq).build()
    nc = _CACHE["nc"]
    in_maps = []
    for c in range(n_cores):
        m = dict(shared)
        m["x"] = x[c * nseq:(c + 1) * nseq]
        m["positions"] = pos[c * nseq:(c + 1) * nseq]
        in_maps.append(m)
    res = run_bass_kernel_spmd(nc, in_maps, core_ids=list(range(n_cores)))
    return np.concatenate([r["out"] for r in res.results], axis=0).astype(np.float32)
```
